# Optimizing a Trainium2 kernel written in Bass

```python
import math
import jax, jax.numpy as jnp
from jax import lax
import numpy as np

D_MODEL = 4096
BATCH = 4
SEQ = 4096
DEPTH = 1

CHUNK = 64
N_META = 16
HEAD_DIM = 128
H_FOX = D_MODEL // (2 * HEAD_DIM)
H_DSA = D_MODEL // (2 * HEAD_DIM)
W_FOX = H_FOX * HEAD_DIM
W_DSA = H_DSA * HEAD_DIM
H_IDX = 32
D_IDX = 64
TOPK_MAX = 256
D_FF = 4 * D_MODEL
N_BUCKETS = 32
MAX_DISTANCE = 128
FOX_BLOCK = 128
DSA_BLOCK = 32
N_BRANCH = 2
RMS_EPS = 1e-6
IN_SPLITS = (W_FOX, W_FOX, W_FOX, H_FOX, W_DSA, W_DSA, W_DSA, H_IDX * D_IDX, D_IDX, H_IDX, N_BRANCH * D_MODEL)
D_IN = 3 * W_FOX + H_FOX + 3 * W_DSA + H_IDX * D_IDX + D_IDX + H_IDX + N_BRANCH * D_MODEL

kernel_name = "gated_fox_dsa_hybrid_block"


def rmsnorm(x, g):
    xf = x.astype(jnp.float32)
    y = xf * lax.rsqrt(jnp.mean(xf * xf, axis=-1, keepdims=True) + RMS_EPS)
    return (y * g.astype(jnp.float32)).astype(x.dtype)


def chunk_ids(n):
    p = jnp.arange(n)
    return jnp.where(p < N_META, 0, (p - N_META) // CHUNK + 1)


def t5_bucket(rel):
    half = N_BUCKETS // 2
    max_exact = half // 2
    ret = jnp.where(rel > 0, half, 0)
    n = jnp.abs(rel)
    nf = jnp.maximum(n, 1).astype(jnp.float32)
    large = max_exact + (jnp.log(nf / max_exact) / math.log(MAX_DISTANCE / max_exact)
                         * (half - max_exact)).astype(jnp.int32)
    large = jnp.minimum(large, half - 1)
    return ret + jnp.where(n < max_exact, n, large)


def forgetting_attention(q, k, v, log_f):
    B, T, H, hd = q.shape
    dcum = jnp.cumsum(log_f, axis=1).transpose(0, 2, 1)
    scale = hd ** -0.5
    outs = []
    for start in range(0, T, FOX_BLOCK):
        end = start + FOX_BLOCK
        s = jnp.einsum('bqhd,bkhd->bhqk', q[:, start:end], k[:, :end],
                       preferred_element_type=jnp.float32) * scale
        decay = dcum[:, :, start:end, None] - dcum[:, :, None, :end]
        qpos = jnp.arange(start, end)[:, None]
        kpos = jnp.arange(end)[None, :]
        s = jnp.where(kpos <= qpos, s + decay, -jnp.inf)
        p = jax.nn.softmax(s, axis=-1).astype(v.dtype)
        outs.append(jnp.einsum('bhqk,bkhd->bqhd', p, v[:, :end]))
    return jnp.concatenate(outs, axis=1)


def dsa_attention(q, k, v, q_idx, k_idx, w_idx, rel_bias, k_top):
    B, T, H, hd = q.shape
    nblk = T // DSA_BLOCK
    cid = chunk_ids(T)
    bidx = jnp.arange(B)[:, None, None]

    def to_blocks(a):
        return a.reshape(B, nblk, DSA_BLOCK, *a.shape[2:]).swapaxes(0, 1)

    def one_block(args):
        qb, qib, wib, start = args
        qpos = start + jnp.arange(DSA_BLOCK)
        dots = jnp.einsum('bqhd,bsd->bqhs', qib, k_idx,
                          preferred_element_type=jnp.float32) * (D_IDX ** -0.5)
        score = jnp.einsum('bqh,bqhs->bqs', wib.astype(jnp.float32) * (H_IDX ** -0.5),
                           jax.nn.relu(dots))
        admissible = cid[None, :] <= cid[qpos][:, None]
        score = jnp.where(admissible[None], score, -jnp.inf)
        top_val, top_idx = lax.top_k(score, k_top)
        valid = jnp.isfinite(top_val)
        ks = k[bidx, top_idx]
        vs = v[bidx, top_idx]
        s = jnp.einsum('bqhd,bqkhd->bhqk', qb, ks,
                       preferred_element_type=jnp.float32) * (hd ** -0.5)
        bucket = t5_bucket(top_idx - qpos[None, :, None])
        bias = rel_bias[bucket].astype(jnp.float32).transpose(0, 3, 1, 2)
        s = jnp.where(valid[:, None], s + bias, -jnp.inf)
        p = jax.nn.softmax(s, axis=-1).astype(v.dtype)
        return jnp.einsum('bhqk,bqkhd->bqhd', p, vs)

    starts = jnp.arange(nblk, dtype=jnp.int32) * DSA_BLOCK
    out = lax.map(one_block, (to_blocks(q), to_blocks(q_idx), to_blocks(w_idx), starts))
    return out.swapaxes(0, 1).reshape(B, T, H, hd)


def setup_inputs(seed: int = 0) -> dict:
    key = jax.random.key(seed)
    ks = jax.random.split(key, 14)
    f32 = jnp.float32
    nrm = lambda k, shape, s: jax.random.normal(k, shape, f32) * s
    return {
        "x": nrm(ks[0], (BATCH, SEQ, D_MODEL), 1.0),
        "meta_tokens": nrm(ks[1], (N_META, D_MODEL), 1.0),
        "attn_norm_g": 1.0 + nrm(ks[2], (DEPTH, D_MODEL), 0.02),
        "w_in": nrm(ks[3], (DEPTH, D_MODEL, D_IN), D_MODEL ** -0.5),
        "forget_bias": 2.0 + nrm(ks[4], (DEPTH, H_FOX), 0.1),
        "rel_bias": nrm(ks[5], (N_BUCKETS, H_DSA), 0.2),
        "w_branch_fox": nrm(ks[6], (DEPTH, W_FOX, D_MODEL), W_FOX ** -0.5),
        "w_branch_dsa": nrm(ks[7], (DEPTH, W_DSA, D_MODEL), W_DSA ** -0.5),
        "w_out": nrm(ks[8], (DEPTH, D_MODEL, D_MODEL), D_MODEL ** -0.5),
        "mlp_norm_g": 1.0 + nrm(ks[9], (DEPTH, D_MODEL), 0.02),
        "w_up": nrm(ks[10], (DEPTH, D_MODEL, D_FF), D_MODEL ** -0.5),
        "w_down": nrm(ks[11], (DEPTH, D_FF, D_MODEL), D_FF ** -0.5),
        "final_norm_g": 1.0 + nrm(ks[12], (D_MODEL,), 0.02),
    }


def reference(x, meta_tokens, attn_norm_g, w_in, forget_bias, rel_bias, w_branch_fox, w_branch_dsa,
              w_out, mlp_norm_g, w_up, w_down, final_norm_g):
    B, L, D = x.shape
    T = N_META + L
    T_pad = -(-T // FOX_BLOCK) * FOX_BLOCK
    meta = jnp.broadcast_to(meta_tokens[None].astype(x.dtype), (B, N_META, D))
    h = jnp.concatenate([meta, x, jnp.zeros((B, T_pad - T, D), x.dtype)], axis=1)
    k_top = min(TOPK_MAX, L // 4)
    split_at = np.cumsum(IN_SPLITS)[:-1].tolist()

    for layer in range(DEPTH):
        u = rmsnorm(h, attn_norm_g[layer])
        proj = u @ w_in[layer]
        qa, ka, va, fa, qb, kb, vb, qi, ki, wi, gl = jnp.split(proj, split_at, axis=-1)
        heads = lambda a, n: a.reshape(B, T_pad, n, -1)
        log_f = jax.nn.log_sigmoid(fa.astype(jnp.float32) + forget_bias[layer].astype(jnp.float32))
        o_fox = forgetting_attention(heads(qa, H_FOX), heads(ka, H_FOX), heads(va, H_FOX), log_f)
        o_dsa = dsa_attention(heads(qb, H_DSA), heads(kb, H_DSA), heads(vb, H_DSA),
                              heads(qi, H_IDX), ki, wi, rel_bias, k_top)
        y_fox = o_fox.reshape(B, T_pad, W_FOX) @ w_branch_fox[layer]
        y_dsa = o_dsa.reshape(B, T_pad, W_DSA) @ w_branch_dsa[layer]
        g = jax.nn.sigmoid(gl.astype(jnp.float32)).astype(h.dtype).reshape(B, T_pad, N_BRANCH, D)
        mixed = g[:, :, 0] * y_fox + g[:, :, 1] * y_dsa
        h = h + mixed @ w_out[layer]
        u = rmsnorm(h, mlp_norm_g[layer])
        h = h + jnp.square(jax.nn.relu(u @ w_up[layer])) @ w_down[layer]

    out = rmsnorm(h, final_norm_g)
    return out[:, N_META:N_META + L]
```

```python
import math
from contextlib import ExitStack

import numpy as np
import concourse.bass as bass
import concourse.mybir as mybir
from concourse.bass_utils import run_bass_kernel_spmd

F32 = mybir.dt.float32
BF16 = mybir.dt.bfloat16
AF = mybir.ActivationFunctionType
ALU = mybir.AluOpType

D = 4096
L = 4096
NMETA = 16
TK = L + NMETA
NOWN = 2048
H = 16
HD = 128
NIDX = 32
DIDX = 64
KTOP = 256
DFF = 16384
EPS = 1e-6
NEG = -1.0e30
N_CORES = 8

O_QA, O_KA, O_VA, O_FA, O_QB, O_KB, O_VB, O_QI, O_KI, O_WI, O_GL = (
    0, 2048, 4096, 6144, 6160, 8208, 10256, 12304, 14352, 14416, 14448)


class _S:
    __slots__ = ("h", "count")

    def __init__(self, h):
        self.h = h
        self.count = 0


class Buf:
    __slots__ = ("name", "writers", "readers", "dsem")

    def __init__(self, name=""):
        self.name = name
        self.writers = {}
        self.readers = {}
        self.dsem = None


class _Eng:
    def __init__(self, name):
        self.name = name
        self.ops = []
        self.sem = None
        self.waited = {}


class Sched:
    def __init__(self, nc, es):
        self.nc = nc
        self.es = es
        self.engs = {n: _Eng(n) for n in ("pe", "act", "dve", "pool", "sp")}
        for n, e in self.engs.items():
            e.sem = _S(es.enter_context(nc.semaphore("e_" + n)))
        self.dsems = []
        self.nbuf = 0

    def buf(self, name=""):
        self.nbuf += 1
        return Buf(name)

    def bufs(self, n, name=""):
        return [self.buf(name + str(i)) for i in range(n)]

    def _dsem(self, b):
        if b.dsem is None:
            b.dsem = _S(self.es.enter_context(self.nc.semaphore("d%d_%s" % (len(self.dsems), b.name[:10]))))
            self.dsems.append(b.dsem)
        return b.dsem

    def _emit(self, eng, toks, fn, sem, inc):
        waits = []
        for S, val in toks.items():
            if eng.name == "pe" and S is eng.sem:
                continue
            if eng.waited.get(S, 0) >= val:
                continue
            eng.waited[S] = val
            waits.append((S, val))
        sem.count += inc
        val = sem.count

        def thunk(e, waits=waits, fn=fn, sem=sem, inc=inc):
            for S, v in waits:
                e.wait_ge(S.h, v)
            ins = fn(e)
            ins.then_inc(sem.h, inc)

        eng.ops.append(thunk)
        return (sem, val)

    @staticmethod
    def _merge(d, tok):
        S, v = tok
        if d.get(S, 0) < v:
            d[S] = v

    def op(self, engname, fn, reads=(), writes=(), pwrites=(), dma_on=None):
        eng = self.engs[engname]
        toks = {}
        for b in reads:
            for t in b.writers.items():
                self._merge(toks, t)
        for b in writes:
            for t in b.writers.items():
                self._merge(toks, t)
            for t in b.readers.items():
                self._merge(toks, t)
        for b in pwrites:
            for t in b.readers.items():
                self._merge(toks, t)
        if dma_on is not None:
            sem, inc = self._dsem(dma_on), 16
        else:
            sem, inc = eng.sem, 1
        tok = self._emit(eng, toks, fn, sem, inc)
        for b in writes:
            b.writers = {tok[0]: tok[1]}
            b.readers = {}
        for b in pwrites:
            if b.readers:
                b.writers = {tok[0]: tok[1]}
                b.readers = {}
            else:
                self._merge(b.writers, tok)
        for b in reads:
            self._merge(b.readers, tok)
        return tok

    def barrier(self):
        allt = {}
        for e in self.engs.values():
            if e.sem.count:
                allt[e.sem] = e.sem.count
        for S in self.dsems:
            if S.count:
                allt[S] = S.count
        for e in self.engs.values():
            waits = []
            for S, val in allt.items():
                if e.waited.get(S, 0) >= val:
                    continue
                e.waited[S] = val
                waits.append((S, val))
            if waits:
                def thunk(en, waits=waits):
                    for S, v in waits:
                        en.wait_ge(S.h, v)
                e.ops.append(thunk)

    def run(self):
        nc = self.nc
        with nc.Block() as block:
            @block.tensor
            def _(e):
                for t in self.engs["pe"].ops:
                    t(e)

            @block.scalar
            def _(e):
                for t in self.engs["act"].ops:
                    t(e)

            @block.vector
            def _(e):
                for t in self.engs["dve"].ops:
                    t(e)

            @block.gpsimd
            def _(e):
                for t in self.engs["pool"].ops:
                    t(e)

            @block.sync
            def _(e):
                for t in self.engs["sp"].ops:
                    t(e)


def build_program(stop_after="all", debug_outs=()):
    nc = bass.Bass("TRN2", target_bir_lowering=False)
    es = ExitStack()
    sc = Sched(nc, es)

    def din(name, shape, dt=F32):
        return nc.dram_tensor(name, list(shape), dt, kind="ExternalInput").ap()

    def dscr(name, shape, dt):
        kind = "ExternalOutput" if name in debug_outs else "Internal"
        return nc.dram_tensor(name, list(shape), dt, kind=kind).ap()

    x_all = din("x_all", [L, D])
    x_own = din("x_own", [NOWN, D])
    meta = din("meta", [NMETA, D])
    g_attn = din("g_attn", [1, D])
    g_mlp = din("g_mlp", [128, 32])
    g_fin = din("g_fin", [128, 32])
    NK = 66
    NQ = 113
    wk = din("wk", [NK, 128, 32, 128])
    wq = din("wq", [NQ, 128, 32, 128])
    wbf = din("wbf", [32, 128, 16, 128])
    wbd = din("wbd", [32, 128, 16, 128])
    wo = din("wo", [32, 128, 32, 128])
    wu = din("wu", [128, 128, 32, 128])
    wd = din("wd", [32, 4, 128, 32, 128])
    negfb = din("negfb", [16, 1])
    sel = din("sel", [16, 2])
    identf_d = din("identf", [128, 128])
    ubias = din("ubias", [H, 128, 3, 128])
    umeta = din("umeta", [H, 16, 128])
    cfar = din("cfar", [128, H])
    ncfar = din("ncfar", [128, H])
    fmask = din("fmask", [8, 128, 512])
    adm = din("adm", [128, 256])
    out = nc.dram_tensor("out", [NOWN, D], F32, kind="ExternalOutput").ap()

    UTa = dscr("UTa", [32, 128, TK], BF16)
    UTo = dscr("UTo", [32, 128, NOWN], BF16)
    XTo = dscr("XTo", [32, 128, NOWN], F32)
    KTf = dscr("KTf", [H, 128, TK], BF16)
    KTd = dscr("KTd", [H, 128, TK], BF16)
    Vf = dscr("Vf", [H, TK, 128], BF16)
    Vd = dscr("Vd", [H, TK, 128], BF16)
    KI2 = dscr("KI2", [128, TK], BF16)
    QTf = dscr("QTf", [H, 128, NOWN], BF16)
    QTd = dscr("QTd", [H, 128, NOWN], BF16)
    QI = dscr("QI", [16, 128, NOWN], BF16)
    Gs = dscr("Gs", [64, 128, NOWN], BF16)
    DTo = dscr("DTo", [H, NOWN], F32)
    BMs = dscr("BMs", [H, 128, 9, 512], BF16)
    BMm = dscr("BMm", [H, 16, 512], BF16)
    OTf = dscr("OTf", [H, 128, NOWN], BF16)
    OTd = dscr("OTd", [H, 128, NOWN], BF16)
    H2s = dscr("H2s", [32, 128, NOWN], F32)
    dbg = {}

    def sb(name, shape, dt):
        return es.enter_context(nc.sbuf_tensor(name, list(shape), dt))

    identf = sb("identf_s", [128, 128], F32)
    identb = sb("identb_s", [128, 128], BF16)
    onesb = sb("onesb", [128, 128], BF16)
    b_ident = sc.buf("ident")
    b_LF = sc.buf("LF")
    b_WIT = sc.buf("WIT")
    b_negD = sc.buf("negD")
    psum = [es.enter_context(nc.psum_tensor("ps%d" % i, [128, 512], F32)) for i in range(8)]
    b_ps = sc.bufs(8, "ps")

    def psb(i):
        return psum[i][:].bitcast(BF16)

    sc.op("sp", lambda e: e.dma_start(out=identf[:], in_=identf_d), writes=[b_ident], dma_on=b_ident)
    sc.op("dve", lambda e: e.tensor_copy(out=identb[:], in_=identf[:]), reads=[b_ident], pwrites=[b_ident])
    sc.op("dve", lambda e: e.memset(onesb[:], 1.0), pwrites=[b_ident])
    epsc = sb("epsc", [128, 1], F32)
    sc.op("dve", lambda e: e.memset(epsc[:], EPS), pwrites=[b_ident])

    def phase_A():
        with ExitStack() as ps_:
            def sbl(name, shape, dt):
                return ps_.enter_context(nc.sbuf_tensor(name, list(shape), dt))
            gb = sbl("gb", [128, D], F32)
            xt = [sbl("xt%d" % i, [128, D], F32) for i in range(3)]
            ub = [sbl("ub%d" % i, [128, D], BF16) for i in range(2)]
            junk = sbl("junk", [128, D], BF16)
            ss = [sbl("ss%d" % i, [128, 1], F32) for i in range(2)]
            rs = [sbl("rs%d" % i, [128, 1], F32) for i in range(2)]
            ut = [sbl("ut%d" % i, [128, 32, 128], BF16) for i in range(2)]
            xtt = [sbl("xtt%d" % i, [128, 32, 128], F32) for i in range(2)]
            b_gb = sc.buf("gb")
            b_xt = sc.bufs(3, "xt")
            b_ub = sc.bufs(2, "ub")
            b_junk = sc.buf("junk")
            b_ss = sc.bufs(2, "ss")
            b_rs = sc.bufs(2, "rs")
            b_ut = sc.bufs(2, "ut")
            b_xtt = sc.bufs(2, "xtt")
            sc.op("sp", lambda e: e.dma_start(out=gb[:], in_=g_attn.partition_broadcast(128)),
                  writes=[b_gb], dma_on=b_gb)
            blocks = [("all", i) for i in range(32)] + [("meta", 0)] + [("own", i) for i in range(16)]
            pcnt = [0]
            for bi, (kind, i) in enumerate(blocks):
                np_ = 16 if kind == "meta" else 128
                if kind == "all":
                    src = x_all[i * 128:(i + 1) * 128, :]
                    dstu = UTa[:, :, i * 128:(i + 1) * 128]
                elif kind == "meta":
                    src = meta
                    dstu = UTa[:, :, L:L + 16]
                else:
                    src = x_own[i * 128:(i + 1) * 128, :]
                    dstu = UTo[:, :, i * 128:(i + 1) * 128]
                X, bX = xt[bi % 3], b_xt[bi % 3]
                U, bU = ub[bi % 2], b_ub[bi % 2]
                SS, bSS = ss[bi % 2], b_ss[bi % 2]
                RS, bRS = rs[bi % 2], b_rs[bi % 2]
                UT, bUT = ut[bi % 2], b_ut[bi % 2]
                sc.op("sp", lambda e, X=X, src=src, np_=np_: e.dma_start(out=X[:np_, :], in_=src),
                      writes=[bX], dma_on=bX)
                sc.op("act", lambda e, X=X, SS=SS, np_=np_: e.activation(
                    out=junk[:np_, :], in_=X[:np_, :], func=AF.Square, accum_out=SS[:np_, :]),
                    reads=[bX], writes=[b_junk, bSS])
                sc.op("act", lambda e, SS=SS, RS=RS, np_=np_: e.activation(
                    out=RS[:np_, :], in_=SS[:np_, :], func=AF.Sqrt, bias=epsc[:np_, 0:1], scale=1.0 / D),
                    reads=[bSS, b_ident], writes=[bRS])
                sc.op("dve", lambda e, RS=RS, np_=np_: e.reciprocal(out=RS[:np_, :], in_=RS[:np_, :]),
                    reads=[bRS], writes=[bRS])
                sc.op("dve", lambda e, X=X, U=U, RS=RS, np_=np_: e.scalar_tensor_tensor(
                    out=U[:np_, :], in0=X[:np_, :], scalar=RS[:np_, 0:1], in1=gb[:np_, :],
                    op0=ALU.mult, op1=ALU.mult), reads=[bX, bRS, b_gb], writes=[bU])
                for q in range(8):
                    bank = pcnt[0] % 4
                    pcnt[0] += 1

                    def tr(e, U=U, q=q, bank=bank, np_=np_):
                        ins = None
                        for j in range(4):
                            kc = q * 4 + j
                            ins = e.transpose(out=psb(bank)[:, j * 128:j * 128 + np_],
                                              in_=U[:np_, kc * 128:(kc + 1) * 128],
                                              identity=identb[:np_, :np_])
                        return ins
                    sc.op("pe", tr, reads=[bU, b_ident], writes=[b_ps[bank]])
                    ceng = "act" if q % 2 == 0 else "dve"

                    def cp(e, UT=UT, q=q, bank=bank, np_=np_, ceng=ceng):
                        src_ = psb(bank)[:, 0:512].rearrange("p (a b) -> p a b", b=128)[:, :, :np_]
                        if ceng == "act":
                            return e.copy(out=UT[:, q * 4:q * 4 + 4, :np_], in_=src_)
                        return e.tensor_copy(out=UT[:, q * 4:q * 4 + 4, :np_], in_=src_)
                    sc.op(ceng, cp, reads=[b_ps[bank]], pwrites=[bUT])
                sc.op("sp", lambda e, UT=UT, dstu=dstu, np_=np_: e.dma_start(
                    out=dstu.rearrange("kc kp c -> kp kc c"), in_=UT[:, :, :np_]),
                    reads=[bUT], dma_on=bUT)
                if kind == "own":
                    XT, bXT = xtt[bi % 2], b_xtt[bi % 2]
                    for q in range(8):
                        bank = 4 + pcnt[0] % 4
                        pcnt[0] += 1

                        def trf(e, X=X, q=q, bank=bank):
                            ins = None
                            for j in range(4):
                                m = q * 4 + j
                                ins = e.transpose(out=psum[bank][:, j * 128:(j + 1) * 128],
                                                  in_=X[:, m * 128:(m + 1) * 128], identity=identf[:])
                            return ins
                        sc.op("pe", trf, reads=[bX, b_ident], writes=[b_ps[bank]])
                        ceng = "dve" if q % 2 == 0 else "act"

                        def cpf(e, XT=XT, q=q, bank=bank, ceng=ceng):
                            src_ = psum[bank][:].rearrange("p (a b) -> p a b", b=128)
                            if ceng == "act":
                                return e.copy(out=XT[:, q * 4:q * 4 + 4, :], in_=src_)
                            return e.tensor_copy(out=XT[:, q * 4:q * 4 + 4, :], in_=src_)
                        sc.op(ceng, cpf, reads=[b_ps[bank]], pwrites=[bXT])
                    sc.op("sp", lambda e, XT=XT, i=i: e.dma_start(
                        out=XTo[:, :, i * 128:(i + 1) * 128].rearrange("kc kp c -> kp kc c"), in_=XT[:]),
                        reads=[bXT], dma_on=bXT)
            sc.barrier()

    def phase_B():
        with ExitStack() as ps_:
            def sbl(name, shape, dt):
                return ps_.enter_context(nc.sbuf_tensor(name, list(shape), dt))
            uT = [sbl("uT%d" % i, [128, 32, 1040], BF16) for i in range(2)]
            wsl = [sbl("wsl%d" % i, [128, 32, 128], BF16) for i in range(3)]
            stg = [sbl("stg%d" % i, [128, 512], BF16) for i in range(4)]
            vstg = [sbl("vstg%d" % i, [128, 4, 128], BF16) for i in range(2)]
            ftmp = [sbl("ftmp%d" % i, [16, 512], F32) for i in range(2)]
            nfb = sbl("nfb", [16, 1], F32)
            b_uT = sc.bufs(2, "uT")
            b_w = sc.bufs(3, "wsl")
            b_stg = sc.bufs(4, "stg")
            b_vstg = sc.bufs(2, "vstg")
            b_ftmp = sc.bufs(2, "ftmp")
            b_nfb = sc.buf("nfb")
            sc.op("sp", lambda e: e.dma_start(out=nfb[:], in_=negfb), writes=[b_nfb], dma_on=b_nfb)

            kchunks = ([("kt", KTf, h) for h in range(H)] + [("kt", KTd, h) for h in range(H)] +
                       [("vt", Vf, h) for h in range(H)] + [("vt", Vd, h) for h in range(H)] +
                       [("k2", KI2, 0), ("fa", None, 0)])
            qchunks = ([("kt", QTf, h) for h in range(H)] + [("kt", QTd, h) for h in range(H)] +
                       [("kt", QI, h) for h in range(16)] + [("wi", None, 0)] +
                       [("g", Gs, j) for j in range(64)])
            ktiles = [[(r * 1024, 512), (r * 1024 + 512, 512)] for r in range(3)] + \
                     [[(3072, 512), (3584, 512), (4096, 16)]]
            qtiles = [[(r * 1024, 512), (r * 1024 + 512, 512)] for r in range(2)]
            cnt = {"w": 0, "ps": 0, "stg": 0, "v": 0, "f": 0, "u": 0}

            def run_side(UTsrc, tiles, chunks, wdram):
                for segs in tiles:
                    c_lo = segs[0][0]
                    ncol = segs[-1][0] + segs[-1][1] - c_lo
                    ui = cnt["u"] % 2
                    cnt["u"] += 1
                    sc.op("sp", lambda e, ui=ui, c_lo=c_lo, ncol=ncol: e.dma_start(
                        out=uT[ui][:, :, 0:ncol],
                        in_=UTsrc[:, :, c_lo:c_lo + ncol].rearrange("kc kp c -> kp kc c")),
                        writes=[b_uT[ui]], dma_on=b_uT[ui])
                    for ci, (kind, dst, idx) in enumerate(chunks):
                        wi_ = cnt["w"] % 3
                        cnt["w"] += 1
                        sc.op("pool", lambda e, wi_=wi_, ci=ci: e.dma_start(out=wsl[wi_][:], in_=wdram[ci]),
                              writes=[b_w[wi_]], dma_on=b_w[wi_])
                        for (c0, n) in segs:
                            bank = cnt["ps"] % 6
                            cnt["ps"] += 1
                            o0 = c0 - c_lo

                            def mm(e, wi_=wi_, ui=ui, bank=bank, o0=o0, n=n):
                                ins = None
                                for kc in range(32):
                                    ins = e.matmul(psum[bank][:, 0:n], wsl[wi_][:, kc, :],
                                                   uT[ui][:, kc, o0:o0 + n], start=(kc == 0), stop=(kc == 31))
                                return ins
                            sc.op("pe", mm, reads=[b_w[wi_], b_uT[ui]], writes=[b_ps[bank]])
                            if kind in ("kt", "k2", "g", "vt"):
                                si = cnt["stg"] % 4
                                cnt["stg"] += 1
                                fn = AF.Sigmoid if kind == "g" else AF.Copy
                                sc.op("act", lambda e, si=si, bank=bank, n=n, fn=fn: e.activation(
                                    out=stg[si][:, 0:n], in_=psum[bank][:, 0:n], func=fn),
                                    reads=[b_ps[bank]], writes=[b_stg[si]])
                                if kind == "vt":
                                    vi = cnt["v"] % 2
                                    cnt["v"] += 1
                                    tb = 6 + vi
                                    nb = (n + 127) // 128
                                    w_ = min(n, 128)

                                    def trv(e, si=si, tb=tb, nb=nb, w_=w_):
                                        ins = None
                                        for j in range(nb):
                                            ins = e.transpose(out=psb(tb)[:w_, j * 128:(j + 1) * 128],
                                                              in_=stg[si][:, j * 128:j * 128 + w_],
                                                              identity=identb[:])
                                        return ins
                                    sc.op("pe", trv, reads=[b_stg[si], b_ident], writes=[b_ps[tb]])
                                    sc.op("dve", lambda e, vi=vi, tb=tb, nb=nb, w_=w_: e.tensor_copy(
                                        out=vstg[vi][:w_, 0:nb, :],
                                        in_=psb(tb)[:w_, 0:nb * 128].rearrange("p (a b) -> p a b", b=128)),
                                        reads=[b_ps[tb]], writes=[b_vstg[vi]])
                                    if n >= 128:
                                        dv = dst[idx, c0:c0 + n, :].rearrange("(b s) d -> s b d", s=128)
                                        sc.op("sp", lambda e, vi=vi, dv=dv, nb=nb: e.dma_start(
                                            out=dv, in_=vstg[vi][:, 0:nb, :]), reads=[b_vstg[vi]], dma_on=b_vstg[vi])
                                    else:
                                        dv = dst[idx, c0:c0 + n, :]
                                        sc.op("sp", lambda e, vi=vi, dv=dv, n=n: e.dma_start(
                                            out=dv, in_=vstg[vi][:n, 0, :]), reads=[b_vstg[vi]], dma_on=b_vstg[vi])
                                else:
                                    dd = dst[c0:c0 + n] if False else (
                                        dst[:, c0:c0 + n] if kind == "k2" else dst[idx, :, c0:c0 + n])
                                    sc.op("sp", lambda e, si=si, dd=dd, n=n: e.dma_start(
                                        out=dd, in_=stg[si][:, 0:n]), reads=[b_stg[si]], dma_on=b_stg[si])
                            elif kind == "fa":
                                fi = cnt["f"] % 2
                                cnt["f"] += 1
                                sc.op("act", lambda e, fi=fi, bank=bank, n=n: e.activation(
                                    out=ftmp[fi][:, 0:n], in_=psum[bank][0:16, 0:n], func=AF.Exp,
                                    bias=nfb[:, 0:1], scale=-1.0),
                                    reads=[b_ps[bank], b_nfb], writes=[b_ftmp[fi]])
                                sc.op("act", lambda e, fi=fi, n=n: e.activation(
                                    out=ftmp[fi][:, 0:n], in_=ftmp[fi][:, 0:n], func=AF.Ln, bias=1.0),
                                    reads=[b_ftmp[fi]], writes=[b_ftmp[fi]])
                                sc.op("dve", lambda e, fi=fi, c0=c0, n=n: e.tensor_scalar(
                                    out=LF[:, c0:c0 + n], in0=ftmp[fi][:, 0:n], scalar1=-1.0, scalar2=None,
                                    op0=ALU.mult), reads=[b_ftmp[fi]], pwrites=[b_LF])
                            elif kind == "wi":
                                sc.op("act", lambda e, bank=bank, c0=c0, n=n: e.copy(
                                    out=WIT[:, c0:c0 + n], in_=psum[bank][0:32, 0:n]),
                                    reads=[b_ps[bank]], pwrites=[b_WIT])
            run_side(UTa, ktiles, kchunks, wk)
            run_side(UTo, qtiles, qchunks, wq)
            sc.barrier()


    SCALE = float(HD) ** -0.5
    CW = float(DIDX) ** -0.5 * float(NIDX) ** -0.5

    def phase_B2():
        with ExitStack() as ps_:
            def sbl(name, shape, dt):
                return ps_.enter_context(nc.sbuf_tensor(name, list(shape), dt))
            ones16 = sbl("ones16", [16, L], F32)
            dto = sbl("dto", [16, NOWN], F32)
            tmpo = [sbl("tmpo%d" % i, [16, 128], F32) for i in range(2)]
            selt = sbl("selt", [16, 2], F32)
            b_ones = sc.buf("ones16")
            b_dto = sc.buf("dto")
            b_tmpo = sc.bufs(2, "tmpo")
            b_sel = sc.buf("sel")
            sc.op("pool", lambda e: e.memset(ones16[:], 1.0), writes=[b_ones])
            sc.op("sp", lambda e: e.dma_start(out=selt[:], in_=sel), writes=[b_sel], dma_on=b_sel)
            sc.op("dve", lambda e: e.tensor_tensor_scan(
                out=DT[:, L:TK], data0=ones16[:, 0:16], data1=LF[:, L:TK], initial=0.0,
                op0=ALU.mult, op1=ALU.add), reads=[b_LF, b_ones], writes=[b_DT])
            sc.op("dve", lambda e: e.tensor_tensor_scan(
                out=DT[:, 0:L], data0=ones16[:, 0:L], data1=LF[:, 0:L], initial=DT[:, TK - 1:TK],
                op0=ALU.mult, op1=ALU.add), reads=[b_LF, b_ones, b_DT], pwrites=[b_DT])
            for q in range(8):
                bank = q % 4

                def trd(e, q=q, bank=bank):
                    ins = None
                    for j in range(4):
                        blk = q * 4 + j
                        ins = e.transpose(out=psum[bank][:, j * 16:(j + 1) * 16],
                                          in_=DT[:, blk * 128:(blk + 1) * 128], identity=identf[:16, :16])
                    return ins
                sc.op("pe", trd, reads=[b_DT, b_ident], writes=[b_ps[bank]])
                sc.op("dve", lambda e, q=q, bank=bank: e.tensor_scalar(
                    out=negD[:, q * 4:q * 4 + 4, :],
                    in0=psum[bank][:, 0:64].rearrange("p (a b) -> p a b", b=16),
                    scalar1=-1.0, scalar2=None, op0=ALU.mult), reads=[b_ps[bank]], pwrites=[b_negD])
            sc.op("pe", lambda e: e.transpose(out=psum[4][:16, 0:16], in_=DT[:, L:TK], identity=identf[:16, :16]),
                  reads=[b_DT, b_ident], writes=[b_ps[4]])
            sc.op("dve", lambda e: e.tensor_scalar(out=negD[:16, 32, :], in0=psum[4][:16, 0:16],
                                                   scalar1=-1.0, scalar2=None, op0=ALU.mult),
                  reads=[b_ps[4]], pwrites=[b_negD])
            for i in range(16):
                t_, bt_ = tmpo[i % 2], b_tmpo[i % 2]
                sc.op("dve", lambda e, i=i, t_=t_: e.tensor_scalar(
                    out=t_[:], in0=DT[:, (2 * i) * 128:(2 * i + 1) * 128], scalar1=selt[:, 0:1], scalar2=None,
                    op0=ALU.mult), reads=[b_DT, b_sel], writes=[bt_])
                sc.op("dve", lambda e, i=i, t_=t_: e.scalar_tensor_tensor(
                    out=dto[:, i * 128:(i + 1) * 128], in0=DT[:, (2 * i + 1) * 128:(2 * i + 2) * 128],
                    scalar=selt[:, 1:2], in1=t_[:], op0=ALU.mult, op1=ALU.add),
                    reads=[b_DT, b_sel, bt_], pwrites=[b_dto])
            sc.op("sp", lambda e: e.dma_start(out=DTo, in_=dto[:]), reads=[b_dto], dma_on=b_dto)
            sc.barrier()

    def phase_C0():
        with ExitStack() as ps_:
            def sbl(name, shape, dt):
                return ps_.enter_context(nc.sbuf_tensor(name, list(shape), dt))
            ncf = sbl("ncf0", [128, H], F32)
            utl = [sbl("utl%d" % i, [128, 3, 128], F32) for i in range(2)]
            uml = [sbl("uml%d" % i, [16, 128], F32) for i in range(2)]
            bm = [sbl("bm0%d" % i, [128, 9, 512], BF16) for i in range(2)]
            bmm = [sbl("bmm0%d" % i, [16, 512], BF16) for i in range(2)]
            b_ncf = sc.buf("ncf")
            b_utl = sc.bufs(2, "utl")
            b_uml = sc.bufs(2, "uml")
            b_bm = sc.bufs(2, "bm")
            b_bmm = sc.bufs(2, "bmm")
            sc.op("sp", lambda e: e.dma_start(out=ncf[:], in_=ncfar), writes=[b_ncf], dma_on=b_ncf)
            for h in range(H):
                i2 = h % 2
                sc.op("sp", lambda e, h=h, i2=i2: e.dma_start(out=utl[i2][:], in_=ubias[h]),
                      writes=[b_utl[i2]], dma_on=b_utl[i2])
                sc.op("sp", lambda e, h=h, i2=i2: e.dma_start(out=uml[i2][:], in_=umeta[h]),
                      writes=[b_uml[i2]], dma_on=b_uml[i2])
                sc.op("dve", lambda e, i2=i2: e.memset(bm[i2][:], 1.0), writes=[b_bm[i2]])
                sc.op("dve", lambda e, i2=i2: e.memset(bmm[i2][:], 1.0), writes=[b_bmm[i2]])

                def fill(e, h=h, i2=i2):
                    ins = None
                    for k in range(4):
                        for jj in range(3):
                            slot = 2 * k + (jj - 1) + 1
                            if slot < 0 or slot > 8:
                                continue
                            ins = e.activation(out=bm[i2][:, slot, k * 128:(k + 1) * 128], in_=utl[i2][:, jj, :],
                                               func=AF.Exp, bias=ncf[:, h:h + 1], scale=1.0)
                    return ins
                sc.op("act", fill, reads=[b_utl[i2], b_ncf], pwrites=[b_bm[i2]])
                sc.op("act", lambda e, h=h, i2=i2: e.activation(
                    out=bmm[i2][:, 0:128], in_=uml[i2][:], func=AF.Exp, bias=ncf[:16, h:h + 1], scale=1.0),
                    reads=[b_uml[i2], b_ncf], pwrites=[b_bmm[i2]])
                sc.op("sp", lambda e, h=h, i2=i2: e.dma_start(out=BMs[h], in_=bm[i2][:]),
                      reads=[b_bm[i2]], dma_on=b_bm[i2])
                sc.op("sp", lambda e, h=h, i2=i2: e.dma_start(out=BMm[h], in_=bmm[i2][:]),
                      reads=[b_bmm[i2]], dma_on=b_bmm[i2])
            sc.barrier()

    def phase_C(do_fox=True, do_dsa=True):
        with ExitStack() as ps_:
            def sbl(name, shape, dt):
                return ps_.enter_context(nc.sbuf_tensor(name, list(shape), dt))
            kt = [sbl("kt%d" % i, [128, TK], BF16) for i in range(2)]
            vv = [sbl("vv%d" % i, [128, 33, 128], BF16) for i in range(2)]
            qt = [sbl("qt%d" % i, [128, 512], BF16) for i in range(2)]
            tmp = [sbl("tmp%d" % i, [128, 512], F32) for i in range(3)]
            pt = [sbl("pt%d" % i, [128, 512], BF16) for i in range(3)]
            rinv = [sbl("rinv%d" % i, [128, 512], F32) for i in range(2)]
            ost = [sbl("ost%d" % i, [128, 512], BF16) for i in range(2)]
            b_kt = sc.bufs(2, "kt")
            b_vv = sc.bufs(2, "vv")
            b_qt = sc.bufs(2, "qt")
            b_tmp = sc.bufs(3, "tmp")
            b_pt = sc.bufs(3, "pt")
            b_rinv = sc.bufs(2, "rinv")
            b_ost = sc.bufs(2, "ost")
            cnt = {"hd": 0, "s": 0, "t": 0, "p": 0, "e": 0, "e2": 0, "ip": 0, "rr": 0}

            def load_head(KT, V, QT, h, g, hb):
                nfb = 8 * g + 8
                sc.op("sp", lambda e: e.dma_start(out=kt[hb][:, 0:nfb * 128], in_=KT[h, :, 0:nfb * 128]),
                      writes=[b_kt[hb]], dma_on=b_kt[hb])
                sc.op("sp", lambda e: e.dma_start(out=kt[hb][:, L:TK], in_=KT[h, :, L:TK]),
                      pwrites=[b_kt[hb]], dma_on=b_kt[hb])
                sc.op("sp", lambda e: e.dma_start(
                    out=vv[hb][:, 0:nfb, :], in_=V[h, 0:nfb * 128, :].rearrange("(b s) d -> s b d", s=128)),
                    writes=[b_vv[hb]], dma_on=b_vv[hb])
                sc.op("sp", lambda e: e.dma_start(out=vv[hb][:16, 32, :], in_=V[h, L:TK, :]),
                      pwrites=[b_vv[hb]], dma_on=b_vv[hb])
                sc.op("sp", lambda e: e.dma_start(out=qt[hb][:], in_=QT[h, :, g * 512:(g + 1) * 512]),
                      writes=[b_qt[hb]], dma_on=b_qt[hb])

            def attn_head(g, hb, ew, OT, h, extra=None):
                nfb = 8 * g + 8
                blocks = [("m", 32)] + [("f", i) for i in range(nfb)]
                N = len(blocks)
                ob = 3 + 2 * (cnt["hd"] % 2)
                cnt["hd"] += 1
                sbanks = []

                def mm1(n):
                    kind, blk = blocks[n]
                    np_ = 16 if kind == "m" else 128
                    c0 = L if kind == "m" else blk * 128
                    sbank = cnt["s"] % 3
                    cnt["s"] += 1
                    sbanks.append(sbank)
                    ex = extra(kind, blk) if extra is not None else None
                    if ex is None:
                        sc.op("pe", lambda e: e.matmul(psum[sbank][:np_, :], kt[hb][:, c0:c0 + np_], qt[hb][:],
                                                       start=True, stop=True),
                              reads=[b_kt[hb], b_qt[hb]], writes=[b_ps[sbank]])
                    else:
                        def mm1x(e):
                            e.matmul(psum[sbank][:np_, :], kt[hb][:, c0:c0 + np_], qt[hb][:], start=True, stop=False)
                            return e.matmul(psum[sbank][:np_, :], identb[:], ex[0], start=False, stop=True)
                        sc.op("pe", mm1x, reads=[b_kt[hb], b_qt[hb], b_ident, ex[1]], writes=[b_ps[sbank]])
                mm1(0)
                for n in range(N):
                    if n + 1 < N:
                        mm1(n + 1)
                    kind, blk = blocks[n]
                    np_ = 16 if kind == "m" else 128
                    pi = ew(n, kind, blk, np_, sbanks[n])

                    def mm23(e, np_=np_, blk=blk, pi=pi, n=n):
                        e.matmul(psum[ob][:, :], vv[hb][:np_, blk, :], pt[pi][:np_, :],
                                 start=(n == 0), stop=(n == N - 1))
                        return e.matmul(psum[ob + 1][:, :], onesb[:np_, :], pt[pi][:np_, :],
                                        start=(n == 0), stop=(n == N - 1))
                    sc.op("pe", mm23, reads=[b_vv[hb], b_pt[pi], b_ident],
                          writes=([b_ps[ob], b_ps[ob + 1]] if n == 0 else []),
                          pwrites=([] if n == 0 else [b_ps[ob], b_ps[ob + 1]]))
                ri = cnt["hd"] % 2
                sc.op("dve", lambda e: e.reciprocal(out=rinv[ri][:], in_=psum[ob + 1][:]),
                      reads=[b_ps[ob + 1]], writes=[b_rinv[ri]])
                sc.op("dve", lambda e: e.tensor_tensor(out=ost[ri][:], in0=psum[ob][:], in1=rinv[ri][:], op=ALU.mult),
                      reads=[b_ps[ob], b_rinv[ri]], writes=[b_ost[ri]])
                sc.op("sp", lambda e: e.dma_start(out=OT[h, :, g * 512:(g + 1) * 512], in_=ost[ri][:]),
                      reads=[b_ost[ri]], dma_on=b_ost[ri])

            if do_fox:
                with ExitStack() as pf_:
                    def sbf(name, shape, dt):
                        return pf_.enter_context(nc.sbuf_tensor(name, list(shape), dt))
                    fmb = sbf("fmb", [128, 8, 512], BF16)
                    dtb = [sbf("dtb%d" % i, [128, 512], F32) for i in range(2)]
                    b_fmt = sc.buf("fmb")
                    b_dtb = sc.bufs(2, "dtb")
                    fmt = sbf("fmt", [128, 8, 512], F32)
                    b_fmt32 = sc.buf("fmt32")
                    sc.op("sp", lambda e: e.dma_start(out=fmt[:], in_=fmask.rearrange("r s t -> s r t")),
                          writes=[b_fmt32], dma_on=b_fmt32)
                    sc.op("dve", lambda e: e.tensor_copy(out=fmb[:], in_=fmt[:]), reads=[b_fmt32], writes=[b_fmt])

                    def load_fox(g, h, hb):
                        load_head(KTf, Vf, QTf, h, g, hb)
                        sc.op("sp", lambda e: e.dma_start(
                            out=dtb[hb][:], in_=DTo[h:h + 1, g * 512:(g + 1) * 512].partition_broadcast(128)),
                            writes=[b_dtb[hb]], dma_on=b_dtb[hb])
                    seq = [(g, h) for g in range(4) for h in range(H)]
                    load_fox(seq[0][0], seq[0][1], 0)
                    for qi_, (g, h) in enumerate(seq):
                        hb = qi_ % 2
                        if qi_ + 1 < len(seq):
                            load_fox(seq[qi_ + 1][0], seq[qi_ + 1][1], 1 - hb)

                        def ew(n, kind, blk, np_, sbank, g=g, h=h, hb=hb):
                            ti = cnt["t"] % 3
                            cnt["t"] += 1
                            pi = cnt["p"] % 3
                            cnt["p"] += 1
                            src, bsrc = dtb[hb][:np_, :], b_dtb[hb]
                            sc.op("dve", lambda e: e.scalar_tensor_tensor(
                                out=tmp[ti][:np_, :], in0=psum[sbank][:np_, :], scalar=SCALE, in1=src,
                                op0=ALU.mult, op1=ALU.add), reads=[b_ps[sbank], bsrc], writes=[b_tmp[ti]])
                            sc.op("act", lambda e: e.activation(
                                out=pt[pi][:np_, :], in_=tmp[ti][:np_, :], func=AF.Exp,
                                bias=negD[:np_, blk, h:h + 1], scale=1.0),
                                reads=[b_tmp[ti], b_negD], writes=[b_pt[pi]])
                            return pi
                        def extra(kind, blk, g=g):
                            if kind == "f" and blk >= 8 * g:
                                return (fmb[:, blk - 8 * g, :], b_fmt)
                            return None
                        attn_head(g, hb, ew, OTf, h, extra)
                    sc.barrier()

            if do_dsa:
                with ExitStack() as pd_:
                    def sbd(name, shape, dt):
                        return pd_.enter_context(nc.sbuf_tensor(name, list(shape), dt))
                    ki2 = sbd("ki2", [128, TK], BF16)
                    qi = [sbd("qi%d" % i, [128, 16, 128], BF16) for i in range(2)]
                    wa = [sbd("wa%d" % i, [128, 32], F32) for i in range(2)]
                    ws = [sbd("ws%d" % i, [128, 32], F32) for i in range(2)]
                    sct = sbd("sct", [128, TK], F32)
                    wkk = sbd("wkk", [128, TK], F32)
                    rr = [sbd("rr%d" % i, [128, 512], F32) for i in range(4)]
                    m8 = sbd("m8", [128, 8], F32)
                    thr = sbd("thr", [128, 1], F32)
                    m01 = sbd("m01", [128, TK], BF16)
                    maskT = sbd("maskT", [128, 33, 512], BF16)
                    admt = sbd("admt", [128, 256], F32)
                    cf = sbd("cf", [128, H], F32)
                    et = [sbd("et%d" % i, [128, 512], BF16) for i in range(3)]
                    et2 = [sbd("et2%d" % i, [128, 512], BF16) for i in range(2)]
                    bm = [sbd("bm%d" % i, [128, 9, 512], BF16) for i in range(2)]
                    bmm = [sbd("bmm%d" % i, [16, 512], BF16) for i in range(2)]
                    b_ki2 = sc.buf("ki2")
                    b_qi = sc.bufs(2, "qi")
                    b_wa = sc.bufs(2, "wa")
                    b_ws = sc.bufs(2, "ws")
                    b_scA = sc.bufs(10, "scA")
                    b_scB = sc.bufs(10, "scB")
                    b_wk = sc.buf("wkk")
                    b_rr = sc.bufs(4, "rr")
                    b_m8 = sc.buf("m8")
                    b_thr = sc.buf("thr")
                    b_m01 = sc.buf("m01")
                    b_maskT = sc.buf("maskT")
                    b_adm = sc.buf("adm")
                    b_cf = sc.buf("cf")
                    b_et = sc.bufs(3, "et")
                    b_et2 = sc.bufs(2, "et2")
                    b_bm = sc.bufs(2, "bm")
                    b_bmm = sc.bufs(2, "bmm")
                    sc.op("sp", lambda e: e.dma_start(out=ki2[:], in_=KI2), writes=[b_ki2], dma_on=b_ki2)
                    sc.op("sp", lambda e: e.dma_start(out=admt[:], in_=adm), writes=[b_adm], dma_on=b_adm)
                    sc.op("sp", lambda e: e.dma_start(out=cf[:], in_=cfar), writes=[b_cf], dma_on=b_cf)

                    def load_dsa(g, h, hb):
                        load_head(KTd, Vd, QTd, h, g, hb)
                        sc.op("sp", lambda e: e.dma_start(out=bm[hb][:], in_=BMs[h]), writes=[b_bm[hb]], dma_on=b_bm[hb])
                        if g == 0:
                            sc.op("sp", lambda e: e.dma_start(out=bmm[hb][:], in_=BMm[h]),
                                  writes=[b_bmm[hb]], dma_on=b_bmm[hb])

                    def indexer_block(g, k):
                        i = 4 * g + k
                        nkb = 2 * i + 2
                        ncols = nkb * 128
                        ntot = ncols + 16
                        q2 = i % 2
                        sc.op("sp", lambda e: e.dma_start(
                            out=qi[q2][:], in_=QI[:, :, i * 128:(i + 1) * 128].rearrange("c p t -> p c t")),
                            writes=[b_qi[q2]], dma_on=b_qi[q2])
                        wb = 6 + (i % 2)
                        sc.op("pe", lambda e: e.transpose(out=psum[wb][:, 0:32], in_=WIT[:, i * 128:(i + 1) * 128],
                                                          identity=identf[:32, :32]),
                              reads=[b_WIT, b_ident], writes=[b_ps[wb]])
                        sc.op("act", lambda e: e.activation(out=wa[q2][:], in_=psum[wb][:, 0:32], func=AF.Abs, scale=CW),
                              reads=[b_ps[wb]], writes=[b_wa[q2]])
                        sc.op("act", lambda e: e.sign(out=ws[q2][:], in_=psum[wb][:, 0:32]),
                              reads=[b_ps[wb]], writes=[b_ws[q2]])
                        tiles = []
                        c0 = 0
                        while c0 < ncols:
                            n = min(512, ncols - c0)
                            tiles.append((c0, n, c0))
                            c0 += n
                        tiles.append((L, 16, ncols))
                        for ti_, (c0, n, d0) in enumerate(tiles):
                            bA, bB = b_scA[ti_], b_scB[ti_]
                            for hi in range(NIDX):
                                ch, half = hi // 2, hi % 2
                                bank = cnt["ip"] % 6
                                cnt["ip"] += 1
                                ri = cnt["rr"] % 4
                                cnt["rr"] += 1
                                sc.op("pe", lambda e, ch=ch, half=half, bank=bank, c0=c0, n=n: e.matmul(
                                    psum[bank][:, 0:n], qi[q2][64 * half:64 * half + 64, ch, :],
                                    ki2[64 * half:64 * half + 64, c0:c0 + n], start=True, stop=True),
                                    reads=[b_qi[q2], b_ki2], writes=[b_ps[bank]])
                                sc.op("act", lambda e, hi=hi, bank=bank, ri=ri, n=n: e.activation(
                                    out=rr[ri][:, 0:n], in_=psum[bank][:, 0:n], func=AF.Relu,
                                    scale=wa[q2][:, hi:hi + 1]), reads=[b_ps[bank], b_wa[q2]], writes=[b_rr[ri]])
                                if hi == 0:
                                    sc.op("dve", lambda e, hi=hi, ri=ri, n=n, d0=d0: e.tensor_scalar(
                                        out=sct[:, d0:d0 + n], in0=rr[ri][:, 0:n], scalar1=ws[q2][:, hi:hi + 1],
                                        scalar2=None, op0=ALU.mult), reads=[b_rr[ri], b_ws[q2]], writes=[bA])
                                else:
                                    sc.op("dve", lambda e, hi=hi, ri=ri, n=n, d0=d0: e.scalar_tensor_tensor(
                                        out=sct[:, d0:d0 + n], in0=rr[ri][:, 0:n], scalar=ws[q2][:, hi:hi + 1],
                                        in1=sct[:, d0:d0 + n], op0=ALU.mult, op1=ALU.add),
                                        reads=[b_rr[ri], b_ws[q2]], writes=[bA])
                        lastt = [b_scA[t] for t in range(len(tiles))]
                        sc.op("dve", lambda e: e.tensor_tensor(
                            out=sct[:, ncols - 256:ncols], in0=sct[:, ncols - 256:ncols], in1=admt[:], op=ALU.add),
                            reads=[b_adm], writes=lastt)
                        allsc = [b_scA[t] for t in range(len(tiles))] + [b_scB[t] for t in range(len(tiles))]
                        for rnd in range(KTOP // 8):
                            srcv = sct if rnd == 0 else wkk
                            sc.op("dve", lambda e, srcv=srcv: e.max(out=m8[:], in_=srcv[:, 0:ntot]),
                                  reads=allsc + [b_wk], writes=[b_m8])
                            sc.op("dve", lambda e, srcv=srcv: e.match_replace(
                                out=wkk[:, 0:ntot], in_to_replace=m8[:], in_values=srcv[:, 0:ntot], imm_value=-3.0e38),
                                reads=[b_m8] + ([b_scA[t] for t in range(len(tiles))] if rnd == 0 else []),
                                writes=[b_wk] + ([b_scB[t] for t in range(len(tiles))] if rnd == 0 else []))
                        sc.op("dve", lambda e: e.tensor_scalar(out=thr[:], in0=m8[:, 7:8], scalar1=-1.0e29, scalar2=None,
                                                               op0=ALU.max), reads=[b_m8], writes=[b_thr])
                        sc.op("dve", lambda e: e.tensor_scalar(out=m01[:, 0:ntot], in0=sct[:, 0:ntot],
                                                               scalar1=thr[:, 0:1], scalar2=None, op0=ALU.is_ge),
                              reads=[b_thr] + lastt, writes=[b_m01])
                        j0 = 0
                        while j0 < nkb:
                            nb = min(4, nkb - j0)
                            bank = cnt["ip"] % 6
                            cnt["ip"] += 1

                            def trm(e, j0=j0, nb=nb, bank=bank):
                                ins = None
                                for j in range(nb):
                                    ins = e.transpose(out=psb(bank)[:, j * 128:(j + 1) * 128],
                                                      in_=m01[:, (j0 + j) * 128:(j0 + j + 1) * 128], identity=identb[:])
                                return ins
                            sc.op("pe", trm, reads=[b_m01, b_ident], writes=[b_ps[bank]])
                            ceng = "act" if (j0 // 4) % 2 == 0 else "pool"
                            if ceng == "act":
                                sc.op("act", lambda e, j0=j0, nb=nb, bank=bank: e.copy(
                                    out=maskT[:, j0:j0 + nb, k * 128:(k + 1) * 128],
                                    in_=psb(bank)[:, 0:nb * 128].rearrange("p (a b) -> p a b", b=128)),
                                    reads=[b_ps[bank]], pwrites=[b_maskT])
                            else:
                                sc.op("dve", lambda e, j0=j0, nb=nb, bank=bank: e.tensor_copy(
                                    out=maskT[:, j0:j0 + nb, k * 128:(k + 1) * 128],
                                    in_=psb(bank)[:, 0:nb * 128].rearrange("p (a b) -> p a b", b=128)),
                                    reads=[b_ps[bank]], pwrites=[b_maskT])
                            j0 += nb
                        bank = cnt["ip"] % 6
                        cnt["ip"] += 1
                        sc.op("pe", lambda e, bank=bank: e.transpose(out=psb(bank)[:16, 0:128], in_=m01[:, ncols:ncols + 16],
                                                                     identity=identb[:]),
                              reads=[b_m01, b_ident], writes=[b_ps[bank]])
                        sc.op("act", lambda e, bank=bank: e.copy(out=maskT[:16, 32, k * 128:(k + 1) * 128],
                                                                 in_=psb(bank)[:16, 0:128]),
                              reads=[b_ps[bank]], pwrites=[b_maskT])

                    for g in range(4):
                        sc.op("dve", lambda e: e.memset(maskT[:], 0.0), writes=[b_maskT])
                        for k in range(4):
                            indexer_block(g, k)
                        load_dsa(g, 0, 0)
                        for h in range(H):
                            hb = h % 2
                            if h + 1 < H:
                                load_dsa(g, h + 1, 1 - hb)

                            def ew(n, kind, blk, np_, sbank, g=g, h=h, hb=hb):
                                ei = cnt["e"] % 3
                                cnt["e"] += 1
                                pi = cnt["p"] % 3
                                cnt["p"] += 1
                                sc.op("act", lambda e: e.activation(
                                    out=et[ei][:np_, :], in_=psum[sbank][:np_, :], func=AF.Exp,
                                    bias=cf[:np_, h:h + 1], scale=SCALE),
                                    reads=[b_ps[sbank], b_cf], writes=[b_et[ei]])
                                srcE, bsrcE = et[ei], b_et[ei]
                                near = None
                                if kind == "f" and blk >= 8 * g - 1:
                                    near = (bm[hb][:np_, blk - (8 * g - 1), :], b_bm[hb])
                                elif kind == "m" and g == 0:
                                    near = (bmm[hb][:np_, :], b_bmm[hb])
                                if near is not None:
                                    e2 = cnt["e2"] % 2
                                    cnt["e2"] += 1
                                    sc.op("dve", lambda e: e.tensor_tensor(
                                        out=et2[e2][:np_, :], in0=et[ei][:np_, :], in1=near[0], op=ALU.mult),
                                        reads=[b_et[ei], near[1]], writes=[b_et2[e2]])
                                    srcE, bsrcE = et2[e2], b_et2[e2]
                                meng = "dve"
                                sc.op(meng, lambda e: e.tensor_tensor(
                                    out=pt[pi][:np_, :], in0=srcE[:np_, :], in1=maskT[:np_, blk, :], op=ALU.mult),
                                    reads=[bsrcE, b_maskT], writes=[b_pt[pi]])
                                return pi
                            attn_head(g, hb, ew, OTd, h)
                    sc.barrier()


    def phase_D():
        with ExitStack() as ps_:
            def sbl(name, shape, dt):
                return ps_.enter_context(nc.sbuf_tensor(name, list(shape), dt))
            arena = sbl("arena", [128, 32768], F32)
            hidv = arena[:].bitcast(BF16).rearrange("p (c t) -> p c t", t=512)
            OTf_v = hidv[:, 0:16, :]
            OTd_v = hidv[:, 16:32, :]
            mixedT = hidv[:, 32:64, :]
            H2T = arena[:, 16384:32768].rearrange("p (c t) -> p c t", t=512)
            orow = arena[:, 0:16384].rearrange("p (b m) -> p b m", m=D)
            U2T = sbl("U2T", [128, 32, 512], BF16)
            wsl = [sbl("wD%d" % i, [128, 32, 128], BF16) for i in range(3)]
            gA = [sbl("gA%d" % i, [128, 512], BF16) for i in range(2)]
            gB = [sbl("gB%d" % i, [128, 512], BF16) for i in range(2)]
            t1 = [sbl("t1%d" % i, [128, 512], F32) for i in range(2)]
            t2 = [sbl("t2%d" % i, [128, 512], F32) for i in range(2)]
            xst = [sbl("xst%d" % i, [128, 512], F32) for i in range(2)]
            h3s = t2
            sq = [sbl("sq%d" % i, [128, 512], BF16) for i in range(2)]
            rstd = sbl("rstd", [128, 512], F32)
            g2t = sbl("g2t", [128, 32], F32)
            gft = sbl("gft", [128, 32], F32)
            b_OT = sc.buf("OTv")
            b_mix = sc.buf("mixedT")
            b_H2T = sc.buf("H2T")
            b_hid = sc.buf("hid")
            b_orow = sc.buf("orow")
            b_U2T = sc.buf("U2T")
            b_w = sc.bufs(3, "wD")
            b_gA = sc.bufs(2, "gA")
            b_gB = sc.bufs(2, "gB")
            b_t1 = sc.bufs(2, "t1")
            b_t2 = sc.bufs(2, "t2")
            b_xst = sc.bufs(2, "xst")
            b_h3s = b_t2
            b_sq = sc.bufs(2, "sq")
            b_rstd = sc.buf("rstd")
            b_g = sc.buf("g2f")
            b_h2d = sc.bufs(32, "h2d")
            sc.op("sp", lambda e: e.dma_start(out=g2t[:], in_=g_mlp), writes=[b_g], dma_on=b_g)
            sc.op("sp", lambda e: e.dma_start(out=gft[:], in_=g_fin), pwrites=[b_g], dma_on=b_g)
            cnt = {"w": 0, "ps": 0, "x": 0, "q": 0, "h": 0}

            def wload(src, kcn):
                wi_ = cnt["w"] % 3
                cnt["w"] += 1
                sc.op("pool", lambda e: e.dma_start(out=wsl[wi_][:, 0:kcn, :], in_=src),
                      writes=[b_w[wi_]], dma_on=b_w[wi_])
                return wi_

            def nbank(nb=6):
                b = cnt["ps"] % nb
                cnt["ps"] += 1
                return b

            def sumsq_rstd(tag):
                sc.op("act", lambda e: e.activation(out=rstd[:], in_=psum[7][:], func=AF.Sqrt,
                                                    bias=epsc[:, 0:1], scale=1.0 / D),
                      reads=[b_ps[7], b_ident], writes=[b_rstd])
                sc.op("dve", lambda e: e.reciprocal(out=rstd[:], in_=rstd[:]), reads=[b_rstd], writes=[b_rstd])

            def row_tile(rt):
                t0 = rt * 512
                sc.op("sp", lambda e: e.dma_start(out=OTf_v, in_=OTf[:, :, t0:t0 + 512].rearrange("h d t -> d h t")),
                      writes=[b_OT], dma_on=b_OT)
                sc.op("sp", lambda e: e.dma_start(out=OTd_v, in_=OTd[:, :, t0:t0 + 512].rearrange("h d t -> d h t")),
                      pwrites=[b_OT], dma_on=b_OT)
                for j in range(32):
                    wa_ = wload(wbf[j], 16)
                    wb_ = wload(wbd[j], 16)
                    gi = j % 2
                    sc.op("sp", lambda e, j=j, gi=gi: e.dma_start(out=gA[gi][:], in_=Gs[j, :, t0:t0 + 512]),
                          writes=[b_gA[gi]], dma_on=b_gA[gi])
                    sc.op("sp", lambda e, j=j, gi=gi: e.dma_start(out=gB[gi][:], in_=Gs[32 + j, :, t0:t0 + 512]),
                          writes=[b_gB[gi]], dma_on=b_gB[gi])
                    ba, bb = nbank(), nbank()

                    def mmA(e, w=wa_, bank=ba, src=OTf_v):
                        ins = None
                        for kc in range(16):
                            ins = e.matmul(psum[bank][:, :], wsl[w][:, kc, :], src[:, kc, :], start=(kc == 0), stop=(kc == 15))
                        return ins

                    def mmB(e, w=wb_, bank=bb, src=OTd_v):
                        ins = None
                        for kc in range(16):
                            ins = e.matmul(psum[bank][:, :], wsl[w][:, kc, :], src[:, kc, :], start=(kc == 0), stop=(kc == 15))
                        return ins
                    sc.op("pe", mmA, reads=[b_w[wa_], b_OT], writes=[b_ps[ba]])
                    sc.op("pe", mmB, reads=[b_w[wb_], b_OT], writes=[b_ps[bb]])
                    sc.op("dve", lambda e, gi=gi, ba=ba: e.tensor_tensor(out=t1[gi][:], in0=psum[ba][:], in1=gA[gi][:], op=ALU.mult),
                          reads=[b_ps[ba], b_gA[gi]], writes=[b_t1[gi]])
                    sc.op("dve", lambda e, gi=gi, bb=bb: e.tensor_tensor(out=t2[gi][:], in0=psum[bb][:], in1=gB[gi][:], op=ALU.mult),
                          reads=[b_ps[bb], b_gB[gi]], writes=[b_t2[gi]])
                    sc.op("dve", lambda e, gi=gi, j=j: e.tensor_tensor(out=mixedT[:, j, :], in0=t1[gi][:], in1=t2[gi][:], op=ALU.add),
                          reads=[b_t1[gi], b_t2[gi]], pwrites=[b_mix])
                for j in range(32):
                    w_ = wload(wo[j], 32)
                    bank = nbank()
                    xi = cnt["x"] % 2
                    cnt["x"] += 1
                    qi_ = cnt["q"] % 2
                    cnt["q"] += 1
                    sc.op("sp", lambda e, j=j, xi=xi: e.dma_start(out=xst[xi][:], in_=XTo[j, :, t0:t0 + 512]),
                          writes=[b_xst[xi]], dma_on=b_xst[xi])

                    def mmO(e, w=w_, bank=bank):
                        ins = None
                        for kc in range(32):
                            ins = e.matmul(psum[bank][:, :], wsl[w][:, kc, :], mixedT[:, kc, :], start=(kc == 0), stop=(kc == 31))
                        return ins
                    sc.op("pe", mmO, reads=[b_w[w_], b_mix], writes=[b_ps[bank]])
                    sc.op("dve", lambda e, j=j, xi=xi, bank=bank: e.tensor_tensor(
                        out=H2T[:, j, :], in0=psum[bank][:], in1=xst[xi][:], op=ALU.add),
                        reads=[b_ps[bank], b_xst[xi]], pwrites=[b_H2T])
                    sc.op("act", lambda e, j=j, qi_=qi_: e.activation(out=sq[qi_][:], in_=H2T[:, j, :], func=AF.Square),
                          reads=[b_H2T], writes=[b_sq[qi_]])
                    sc.op("pe", lambda e, j=j, qi_=qi_: e.matmul(psum[7][:, :], onesb[:], sq[qi_][:], start=(j == 0), stop=(j == 31)),
                          reads=[b_sq[qi_], b_ident], writes=([b_ps[7]] if j == 0 else []), pwrites=([] if j == 0 else [b_ps[7]]))
                    sc.op("sp", lambda e, j=j: e.dma_start(out=H2s[j, :, t0:t0 + 512], in_=H2T[:, j, :]),
                          reads=[b_H2T], writes=[b_h2d[j]], dma_on=b_H2T)
                sumsq_rstd("a")
                for j in range(32):
                    sc.op("dve", lambda e, j=j: e.scalar_tensor_tensor(
                        out=U2T[:, j, :], in0=H2T[:, j, :], scalar=g2t[:, j:j + 1], in1=rstd[:], op0=ALU.mult, op1=ALU.mult),
                        reads=[b_H2T, b_g, b_rstd], pwrites=[b_U2T])
                sc.barrier()
                for f in range(128):
                    w_ = wload(wu[f], 32)
                    bank = nbank()
                    gi = f % 2

                    def mmU(e, w=w_, bank=bank):
                        ins = None
                        for kc in range(32):
                            ins = e.matmul(psum[bank][:, :], wsl[w][:, kc, :], U2T[:, kc, :], start=(kc == 0), stop=(kc == 31))
                        return ins
                    sc.op("pe", mmU, reads=[b_w[w_], b_U2T], writes=[b_ps[bank]])
                    sc.op("act", lambda e, gi=gi, bank=bank: e.activation(out=t1[gi][:], in_=psum[bank][:], func=AF.Relu),
                          reads=[b_ps[bank]], writes=[b_t1[gi]])
                    sc.op("dve", lambda e, gi=gi, f=f: e.tensor_tensor(out=hidv[:, f, :], in0=t1[gi][:], in1=t1[gi][:], op=ALU.mult),
                          reads=[b_t1[gi]], pwrites=[b_hid])
                for j in range(32):
                    bank = nbank(4)
                    xi = cnt["x"] % 2
                    cnt["x"] += 1
                    qi_ = cnt["q"] % 2
                    cnt["q"] += 1
                    hi_ = cnt["h"] % 2
                    cnt["h"] += 1
                    sc.op("sp", lambda e, j=j, xi=xi: e.dma_start(out=xst[xi][:], in_=H2s[j, :, t0:t0 + 512]),
                          reads=[b_h2d[j]], writes=[b_xst[xi]], dma_on=b_xst[xi])
                    for q in range(4):
                        w_ = wload(wd[j, q], 32)

                        def mmD(e, w=w_, bank=bank, q=q):
                            ins = None
                            for kc in range(32):
                                ins = e.matmul(psum[bank][:, :], wsl[w][:, kc, :], hidv[:, q * 32 + kc, :],
                                               start=(q == 0 and kc == 0), stop=(q == 3 and kc == 31))
                            return ins
                        sc.op("pe", mmD, reads=[b_w[w_], b_hid], writes=([b_ps[bank]] if q == 0 else []),
                              pwrites=([] if q == 0 else [b_ps[bank]]))
                    sc.op("dve", lambda e, xi=xi, bank=bank, hi_=hi_: e.tensor_tensor(
                        out=h3s[hi_][:], in0=psum[bank][:], in1=xst[xi][:], op=ALU.add),
                        reads=[b_ps[bank], b_xst[xi]], writes=[b_h3s[hi_]])
                    sc.op("act", lambda e, hi_=hi_, qi_=qi_: e.activation(out=sq[qi_][:], in_=h3s[hi_][:], func=AF.Square),
                          reads=[b_h3s[hi_]], writes=[b_sq[qi_]])
                    sc.op("pe", lambda e, j=j, qi_=qi_: e.matmul(psum[7][:, :], onesb[:], sq[qi_][:], start=(j == 0), stop=(j == 31)),
                          reads=[b_sq[qi_], b_ident], writes=([b_ps[7]] if j == 0 else []), pwrites=([] if j == 0 else [b_ps[7]]))
                    sc.op("sp", lambda e, j=j, hi_=hi_: e.dma_start(out=H2s[j, :, t0:t0 + 512], in_=h3s[hi_][:]),
                          reads=[b_h3s[hi_]], writes=[b_h2d[j]], dma_on=b_h3s[hi_])
                sumsq_rstd("b")
                sc.barrier()
                for j in range(32):
                    bank = nbank(4)
                    xi = cnt["x"] % 2
                    cnt["x"] += 1
                    gi = j % 2
                    sc.op("sp", lambda e, j=j, xi=xi: e.dma_start(out=xst[xi][:], in_=H2s[j, :, t0:t0 + 512]),
                          reads=[b_h2d[j]], writes=[b_xst[xi]], dma_on=b_xst[xi])
                    sc.op("dve", lambda e, j=j, xi=xi, gi=gi: e.scalar_tensor_tensor(
                        out=t1[gi][:], in0=xst[xi][:], scalar=gft[:, j:j + 1], in1=rstd[:], op0=ALU.mult, op1=ALU.mult),
                        reads=[b_xst[xi], b_g, b_rstd], writes=[b_t1[gi]])

                    def trO(e, gi=gi, bank=bank):
                        ins = None
                        for tb in range(4):
                            ins = e.transpose(out=psum[bank][:, tb * 128:(tb + 1) * 128],
                                              in_=t1[gi][:, tb * 128:(tb + 1) * 128], identity=identf[:])
                        return ins
                    sc.op("pe", trO, reads=[b_t1[gi], b_ident], writes=[b_ps[bank]])
                    ceng = "act" if j % 2 == 0 else "dve"
                    if ceng == "act":
                        sc.op("act", lambda e, j=j, bank=bank: e.copy(
                            out=orow[:, :, j * 128:(j + 1) * 128],
                            in_=psum[bank][:].rearrange("p (a b) -> p a b", b=128)),
                            reads=[b_ps[bank]], pwrites=[b_orow])
                    else:
                        sc.op("dve", lambda e, j=j, bank=bank: e.tensor_copy(
                            out=orow[:, :, j * 128:(j + 1) * 128],
                            in_=psum[bank][:].rearrange("p (a b) -> p a b", b=128)),
                            reads=[b_ps[bank]], pwrites=[b_orow])
                for tb in range(4):
                    sc.op("sp", lambda e, tb=tb: e.dma_start(out=out[t0 + tb * 128:t0 + (tb + 1) * 128, :], in_=orow[:, tb, :]),
                          reads=[b_orow], dma_on=b_orow)
                sc.barrier()
            for rt in range(4):
                row_tile(rt)

    es_mid = ExitStack()

    def sbm(name, shape, dt):
        return es_mid.enter_context(nc.sbuf_tensor(name, list(shape), dt))
    DT = sbm("DT", [16, TK], F32)
    WIT = sbm("WIT", [32, NOWN], F32)
    negD = sbm("negD", [128, 33, 16], F32)
    b_DT = sc.buf("DT")
    es_lf = ExitStack()
    LF = es_lf.enter_context(nc.sbuf_tensor("LF", [16, TK], F32))
    phase_A()
    if stop_after != "A":
        phase_B()
    if stop_after not in ("A", "B"):
        phase_B2()
        es_lf.close()
        phase_C0()
        phase_C(do_fox=True, do_dsa=(stop_after != "C1"))
    else:
        es_lf.close()
    es_mid.close()
    if stop_after == "all":
        phase_D()

    sc.barrier()
    sc.run()
    es.close()
    return nc


def _tile_w(W):
    K, N = W.shape
    return np.ascontiguousarray(W.reshape(K // 128, 128, N // 128, 128).transpose(2, 1, 0, 3))


def _t5_bucket_table():
    rel = np.arange(-4300, 4301, dtype=np.int64)
    half, max_exact = 16, 8
    ret = np.where(rel > 0, half, 0)
    n = np.abs(rel)
    nf = np.maximum(n, 1).astype(np.float32)
    large = max_exact + (np.log(nf / np.float32(max_exact)) / np.float32(math.log(128 / max_exact))
                         * np.float32(half - max_exact)).astype(np.int32)
    large = np.minimum(large, half - 1)
    return (ret + np.where(n < max_exact, n, large)).astype(np.int64)


def _host_prepare(inp):
    f32 = np.float32
    x = np.asarray(inp["x"], f32)
    w_in = np.asarray(inp["w_in"], f32)[0]
    maps = []
    z = np.zeros
    ki = w_in[:, O_KI:O_KI + 64]
    fa_chunk = np.concatenate([w_in[:, O_FA:O_FA + 16], z((D, 112), f32)], axis=1)
    wi_chunk = np.concatenate([w_in[:, O_WI:O_WI + 32], z((D, 96), f32)], axis=1)
    wk_cols = np.concatenate([w_in[:, O_KA:O_KA + 2048], w_in[:, O_KB:O_KB + 2048],
                              w_in[:, O_VA:O_VA + 2048], w_in[:, O_VB:O_VB + 2048],
                              ki, ki, fa_chunk], axis=1)
    wq_cols = np.concatenate([w_in[:, O_QA:O_QA + 2048], w_in[:, O_QB:O_QB + 2048],
                              w_in[:, O_QI:O_QI + 2048], wi_chunk,
                              w_in[:, O_GL:O_GL + 8192]], axis=1)
    wk = _tile_w(wk_cols)
    wq = _tile_w(wq_cols)
    wbf = _tile_w(np.asarray(inp["w_branch_fox"], f32)[0])
    wbd = _tile_w(np.asarray(inp["w_branch_dsa"], f32)[0])
    wo = _tile_w(np.asarray(inp["w_out"], f32)[0])
    wu = _tile_w(np.asarray(inp["w_up"], f32)[0])
    wdn = _tile_w(np.asarray(inp["w_down"], f32)[0])
    wd = np.ascontiguousarray(wdn.reshape(32, 128, 4, 32, 128).transpose(0, 2, 1, 3, 4))
    g_attn = np.asarray(inp["attn_norm_g"], f32).reshape(1, D)
    g_mlp = np.ascontiguousarray(np.asarray(inp["mlp_norm_g"], f32).reshape(32, 128).T)
    g_fin = np.ascontiguousarray(np.asarray(inp["final_norm_g"], f32).reshape(32, 128).T)
    negfb = np.ascontiguousarray(-np.asarray(inp["forget_bias"], f32).reshape(16, 1))
    rel_bias = np.asarray(inp["rel_bias"], f32)
    meta = np.asarray(inp["meta_tokens"], f32)
    bt = _t5_bucket_table()

    def bkt(rel):
        return bt[rel + 4300]
    cfar = np.ascontiguousarray(np.broadcast_to(rel_bias[15][None, :], (128, H))).astype(f32)
    identf = np.eye(128, dtype=f32)
    si = np.arange(128)
    per_par = {}
    for p in (0, 1):
        u = np.zeros((H, 128, 3, 128), f32)
        for jj, j in enumerate((-1, 0, 1)):
            dblk = j - p
            rel = dblk * 128 + si[:, None] - si[None, :]
            u[:, :, jj, :] = rel_bias[bkt(rel)].transpose(2, 0, 1)
        relm = si[:16, None] - (16 + 128 * p + si[None, :])
        um = np.ascontiguousarray(rel_bias[bkt(relm)].transpose(2, 0, 1)).astype(f32)
        fm = np.zeros((8, 128, 512), f32)
        for r in range(8):
            for k in range(4):
                d = r - (2 * k + p)
                blk = fm[r, :, k * 128:(k + 1) * 128]
                if d > 0:
                    blk[:] = NEG
                elif d == 0:
                    blk[:] = np.where(si[:, None] <= si[None, :], 0.0, NEG)
        ad = np.zeros((128, 256), f32)
        cm = np.where((si[None, :] // 64) <= (si[:, None] // 64), 0.0, NEG)
        if p == 0:
            ad[:, 0:128] = cm
            ad[:, 128:256] = NEG
        else:
            ad[:, 128:256] = cm
        selv = np.zeros((16, 2), f32)
        selv[:, p] = 1.0
        per_par[p] = dict(ubias=u, umeta=um, fmask=fm, adm=ad, sel=selv)
    for c in range(N_CORES):
        b, p = c // 2, c % 2
        xb = x[b]
        x_own = np.ascontiguousarray(xb.reshape(32, 128, D)[p::2].reshape(NOWN, D))
        m = dict(x_all=xb, x_own=x_own, meta=meta, g_attn=g_attn, g_mlp=g_mlp, g_fin=g_fin,
                 wk=wk, wq=wq, wbf=wbf, wbd=wbd, wo=wo, wu=wu, wd=wd, negfb=negfb,
                 identf=identf, cfar=cfar, ncfar=-cfar)
        m.update(per_par[p])
        maps.append(m)
    return maps


def _assemble(results):
    outp = np.empty((4, L, D), np.float32)
    for c in range(N_CORES):
        b, p = c // 2, c % 2
        o = np.asarray(results[c]["out"], np.float32).reshape(16, 128, D)
        outp[b].reshape(32, 128, D)[p::2] = o
    return outp


def kernel(**inputs):
    maps = _host_prepare(inputs)
    nc = build_program()
    res = run_bass_kernel_spmd(nc, maps, core_ids=list(range(N_CORES)))
    return _assemble(res.results)
```

```python
import math
from contextlib import ExitStack

import numpy as np
import concourse.bass as bass
import concourse.mybir as mybir
from concourse.bass_utils import run_bass_kernel_spmd

F32 = mybir.dt.float32
BF16 = mybir.dt.bfloat16
AF = mybir.ActivationFunctionType
ALU = mybir.AluOpType

D = 4096
L = 4096
NMETA = 16
TK = L + NMETA
NOWN = 2048
H = 16
HD = 128
NIDX = 32
DIDX = 64
KTOP = 256
DFF = 16384
EPS = 1e-6
NEG = -1.0e30
N_CORES = 8

O_QA, O_KA, O_VA, O_FA, O_QB, O_KB, O_VB, O_QI, O_KI, O_WI, O_GL = (
    0, 2048, 4096, 6144, 6160, 8208, 10256, 12304, 14352, 14416, 14448)


class _S:
    __slots__ = ("h", "count")

    def __init__(self, h):
        self.h = h
        self.count = 0


class Buf:
    __slots__ = ("name", "writers", "readers", "dsem")

    def __init__(self, name=""):
        self.name = name
        self.writers = {}
        self.readers = {}
        self.dsem = None


class _Eng:
    def __init__(self, name):
        self.name = name
        self.ops = []
        self.sem = None
        self.waited = {}


class Sched:
    def __init__(self, nc, es):
        self.nc = nc
        self.es = es
        self.engs = {n: _Eng(n) for n in ("pe", "act", "dve", "pool", "sp")}
        for n, e in self.engs.items():
            e.sem = _S(es.enter_context(nc.semaphore("e_" + n)))
        self.dsems = []
        self.nbuf = 0

    def buf(self, name=""):
        self.nbuf += 1
        return Buf(name)

    def bufs(self, n, name=""):
        return [self.buf(name + str(i)) for i in range(n)]

    def _dsem(self, b):
        if b.dsem is None:
            b.dsem = _S(self.es.enter_context(self.nc.semaphore("d%d_%s" % (len(self.dsems), b.name[:10]))))
            self.dsems.append(b.dsem)
        return b.dsem

    def _emit(self, eng, toks, fn, sem, inc):
        waits = []
        for S, val in toks.items():
            if eng.name == "pe" and S is eng.sem:
                continue
            if eng.waited.get(S, 0) >= val:
                continue
            eng.waited[S] = val
            waits.append((S, val))
        sem.count += inc
        val = sem.count

        def thunk(e, waits=waits, fn=fn, sem=sem, inc=inc):
            for S, v in waits:
                e.wait_ge(S.h, v)
            ins = fn(e)
            ins.then_inc(sem.h, inc)

        eng.ops.append(thunk)
        return (sem, val)

    @staticmethod
    def _merge(d, tok):
        S, v = tok
        if d.get(S, 0) < v:
            d[S] = v

    def op(self, engname, fn, reads=(), writes=(), pwrites=(), dma_on=None):
        eng = self.engs[engname]
        toks = {}
        for b in reads:
            for t in b.writers.items():
                self._merge(toks, t)
        for b in writes:
            for t in b.writers.items():
                self._merge(toks, t)
            for t in b.readers.items():
                self._merge(toks, t)
        for b in pwrites:
            for t in b.readers.items():
                self._merge(toks, t)
        if dma_on is not None:
            sem, inc = self._dsem(dma_on), 16
        else:
            sem, inc = eng.sem, 1
        tok = self._emit(eng, toks, fn, sem, inc)
        for b in writes:
            b.writers = {tok[0]: tok[1]}
            b.readers = {}
        for b in pwrites:
            if b.readers:
                b.writers = {tok[0]: tok[1]}
                b.readers = {}
            else:
                self._merge(b.writers, tok)
        for b in reads:
            self._merge(b.readers, tok)
        return tok

    def barrier(self):
        allt = {}
        for e in self.engs.values():
            if e.sem.count:
                allt[e.sem] = e.sem.count
        for S in self.dsems:
            if S.count:
                allt[S] = S.count
        for e in self.engs.values():
            waits = []
            for S, val in allt.items():
                if e.waited.get(S, 0) >= val:
                    continue
                e.waited[S] = val
                waits.append((S, val))
            if waits:
                def thunk(en, waits=waits):
                    for S, v in waits:
                        en.wait_ge(S.h, v)
                e.ops.append(thunk)

    def run(self):
        nc = self.nc
        with nc.Block() as block:
            @block.tensor
            def _(e):
                for t in self.engs["pe"].ops:
                    t(e)

            @block.scalar
            def _(e):
                for t in self.engs["act"].ops:
                    t(e)

            @block.vector
            def _(e):
                for t in self.engs["dve"].ops:
                    t(e)

            @block.gpsimd
            def _(e):
                for t in self.engs["pool"].ops:
                    t(e)

            @block.sync
            def _(e):
                for t in self.engs["sp"].ops:
                    t(e)


def build_program(stop_after="all", debug_outs=()):
    nc = bass.Bass("TRN2", target_bir_lowering=False)
    es = ExitStack()
    sc = Sched(nc, es)

    def din(name, shape, dt=F32):
        return nc.dram_tensor(name, list(shape), dt, kind="ExternalInput").ap()

    def dscr(name, shape, dt):
        kind = "ExternalOutput" if name in debug_outs else "Internal"
        return nc.dram_tensor(name, list(shape), dt, kind=kind).ap()

    x_all = din("x_all", [L, D])
    x_own = din("x_own", [NOWN, D])
    meta = din("meta", [NMETA, D])
    g_attn = din("g_attn", [1, D])
    g_mlp = din("g_mlp", [128, 32])
    g_fin = din("g_fin", [128, 32])
    NK = 66
    NQ = 113
    wk = din("wk", [NK, 128, 32, 128])
    wq = din("wq", [NQ, 128, 32, 128])
    wbf = din("wbf", [32, 128, 16, 128])
    wbd = din("wbd", [32, 128, 16, 128])
    wo = din("wo", [32, 128, 32, 128])
    wu = din("wu", [128, 128, 32, 128])
    wd = din("wd", [32, 4, 128, 32, 128])
    negfb = din("negfb", [16, 1])
    sel = din("sel", [16, 2])
    identf_d = din("identf", [128, 128])
    ubias = din("ubias", [H, 128, 3, 128])
    umeta = din("umeta", [H, 16, 128])
    cfar = din("cfar", [128, H])
    ncfar = din("ncfar", [128, H])
    fmask = din("fmask", [8, 128, 512])
    adm = din("adm", [128, 256])
    out = nc.dram_tensor("out", [NOWN, D], F32, kind="ExternalOutput").ap()

    UTa = dscr("UTa", [32, 128, TK], BF16)
    UTo = dscr("UTo", [32, 128, NOWN], BF16)
    XTo = dscr("XTo", [32, 128, NOWN], F32)
    KTf = dscr("KTf", [H, 128, TK], BF16)
    KTd = dscr("KTd", [H, 128, TK], BF16)
    Vf = dscr("Vf", [H, TK, 128], BF16)
    Vd = dscr("Vd", [H, TK, 128], BF16)
    KI2 = dscr("KI2", [128, TK], BF16)
    QTf = dscr("QTf", [H, 128, NOWN], BF16)
    QTd = dscr("QTd", [H, 128, NOWN], BF16)
    QI = dscr("QI", [16, 128, NOWN], BF16)
    Gs = dscr("Gs", [64, 128, NOWN], BF16)
    DTo = dscr("DTo", [H, NOWN], F32)
    BMs = dscr("BMs", [H, 128, 9, 512], BF16)
    BMm = dscr("BMm", [H, 16, 512], BF16)
    OTf = dscr("OTf", [H, 128, NOWN], BF16)
    OTd = dscr("OTd", [H, 128, NOWN], BF16)
    H2s = dscr("H2s", [32, 128, NOWN], F32)
    dbg = {}

    def sb(name, shape, dt):
        return es.enter_context(nc.sbuf_tensor(name, list(shape), dt))

    identf = sb("identf_s", [128, 128], F32)
    identb = sb("identb_s", [128, 128], BF16)
    onesb = sb("onesb", [128, 128], BF16)
    b_ident = sc.buf("ident")
    b_LF = sc.buf("LF")
    b_WIT = sc.buf("WIT")
    b_negD = sc.buf("negD")
    psum = [es.enter_context(nc.psum_tensor("ps%d" % i, [128, 512], F32)) for i in range(8)]
    b_ps = sc.bufs(8, "ps")

    def psb(i):
        return psum[i][:].bitcast(BF16)

    sc.op("sp", lambda e: e.dma_start(out=identf[:], in_=identf_d), writes=[b_ident], dma_on=b_ident)
    sc.op("dve", lambda e: e.tensor_copy(out=identb[:], in_=identf[:]), reads=[b_ident], pwrites=[b_ident])
    sc.op("dve", lambda e: e.memset(onesb[:], 1.0), pwrites=[b_ident])
    epsc = sb("epsc", [128, 1], F32)
    sc.op("dve", lambda e: e.memset(epsc[:], EPS), pwrites=[b_ident])

    def phase_A():
        with ExitStack() as ps_:
            def sbl(name, shape, dt):
                return ps_.enter_context(nc.sbuf_tensor(name, list(shape), dt))
            gb = sbl("gb", [128, D], F32)
            xt = [sbl("xt%d" % i, [128, D], F32) for i in range(3)]
            ub = [sbl("ub%d" % i, [128, D], BF16) for i in range(2)]
            junk = sbl("junk", [128, D], BF16)
            ss = [sbl("ss%d" % i, [128, 1], F32) for i in range(2)]
            rs = [sbl("rs%d" % i, [128, 1], F32) for i in range(2)]
            ut = [sbl("ut%d" % i, [128, 32, 128], BF16) for i in range(2)]
            xtt = [sbl("xtt%d" % i, [128, 32, 128], F32) for i in range(2)]
            b_gb = sc.buf("gb")
            b_xt = sc.bufs(3, "xt")
            b_ub = sc.bufs(2, "ub")
            b_junk = sc.buf("junk")
            b_ss = sc.bufs(2, "ss")
            b_rs = sc.bufs(2, "rs")
            b_ut = sc.bufs(2, "ut")
            b_xtt = sc.bufs(2, "xtt")
            sc.op("sp", lambda e: e.dma_start(out=gb[:], in_=g_attn.partition_broadcast(128)),
                  writes=[b_gb], dma_on=b_gb)
            blocks = [("all", i) for i in range(32)] + [("meta", 0)] + [("own", i) for i in range(16)]
            pcnt = [0]
            for bi, (kind, i) in enumerate(blocks):
                np_ = 16 if kind == "meta" else 128
                if kind == "all":
                    src = x_all[i * 128:(i + 1) * 128, :]
                    dstu = UTa[:, :, i * 128:(i + 1) * 128]
                elif kind == "meta":
                    src = meta
                    dstu = UTa[:, :, L:L + 16]
                else:
                    src = x_own[i * 128:(i + 1) * 128, :]
                    dstu = UTo[:, :, i * 128:(i + 1) * 128]
                X, bX = xt[bi % 3], b_xt[bi % 3]
                U, bU = ub[bi % 2], b_ub[bi % 2]
                SS, bSS = ss[bi % 2], b_ss[bi % 2]
                RS, bRS = rs[bi % 2], b_rs[bi % 2]
                UT, bUT = ut[bi % 2], b_ut[bi % 2]
                sc.op("sp", lambda e, X=X, src=src, np_=np_: e.dma_start(out=X[:np_, :], in_=src),
                      writes=[bX], dma_on=bX)
                sc.op("act", lambda e, X=X, SS=SS, np_=np_: e.activation(
                    out=junk[:np_, :], in_=X[:np_, :], func=AF.Square, accum_out=SS[:np_, :]),
                    reads=[bX], writes=[b_junk, bSS])
                sc.op("act", lambda e, SS=SS, RS=RS, np_=np_: e.activation(
                    out=RS[:np_, :], in_=SS[:np_, :], func=AF.Sqrt, bias=epsc[:np_, 0:1], scale=1.0 / D),
                    reads=[bSS, b_ident], writes=[bRS])
                sc.op("dve", lambda e, RS=RS, np_=np_: e.reciprocal(out=RS[:np_, :], in_=RS[:np_, :]),
                    reads=[bRS], writes=[bRS])
                sc.op("dve", lambda e, X=X, U=U, RS=RS, np_=np_: e.scalar_tensor_tensor(
                    out=U[:np_, :], in0=X[:np_, :], scalar=RS[:np_, 0:1], in1=gb[:np_, :],
                    op0=ALU.mult, op1=ALU.mult), reads=[bX, bRS, b_gb], writes=[bU])
                for q in range(8):
                    bank = pcnt[0] % 4
                    pcnt[0] += 1

                    def tr(e, U=U, q=q, bank=bank, np_=np_):
                        ins = None
                        for j in range(4):
                            kc = q * 4 + j
                            ins = e.transpose(out=psb(bank)[:, j * 128:j * 128 + np_],
                                              in_=U[:np_, kc * 128:(kc + 1) * 128],
                                              identity=identb[:np_, :np_])
                        return ins
                    sc.op("pe", tr, reads=[bU, b_ident], writes=[b_ps[bank]])
                    ceng = "act" if q % 2 == 0 else "dve"

                    def cp(e, UT=UT, q=q, bank=bank, np_=np_, ceng=ceng):
                        src_ = psb(bank)[:, 0:512].rearrange("p (a b) -> p a b", b=128)[:, :, :np_]
                        if ceng == "act":
                            return e.copy(out=UT[:, q * 4:q * 4 + 4, :np_], in_=src_)
                        return e.tensor_copy(out=UT[:, q * 4:q * 4 + 4, :np_], in_=src_)
                    sc.op(ceng, cp, reads=[b_ps[bank]], pwrites=[bUT])
                sc.op("sp", lambda e, UT=UT, dstu=dstu, np_=np_: e.dma_start(
                    out=dstu.rearrange("kc kp c -> kp kc c"), in_=UT[:, :, :np_]),
                    reads=[bUT], dma_on=bUT)
                if kind == "own":
                    XT, bXT = xtt[bi % 2], b_xtt[bi % 2]
                    for q in range(8):
                        bank = 4 + pcnt[0] % 4
                        pcnt[0] += 1

                        def trf(e, X=X, q=q, bank=bank):
                            ins = None
                            for j in range(4):
                                m = q * 4 + j
                                ins = e.transpose(out=psum[bank][:, j * 128:(j + 1) * 128],
                                                  in_=X[:, m * 128:(m + 1) * 128], identity=identf[:])
                            return ins
                        sc.op("pe", trf, reads=[bX, b_ident], writes=[b_ps[bank]])
                        ceng = "dve" if q % 2 == 0 else "act"

                        def cpf(e, XT=XT, q=q, bank=bank, ceng=ceng):
                            src_ = psum[bank][:].rearrange("p (a b) -> p a b", b=128)
                            if ceng == "act":
                                return e.copy(out=XT[:, q * 4:q * 4 + 4, :], in_=src_)
                            return e.tensor_copy(out=XT[:, q * 4:q * 4 + 4, :], in_=src_)
                        sc.op(ceng, cpf, reads=[b_ps[bank]], pwrites=[bXT])
                    sc.op("sp", lambda e, XT=XT, i=i: e.dma_start(
                        out=XTo[:, :, i * 128:(i + 1) * 128].rearrange("kc kp c -> kp kc c"), in_=XT[:]),
                        reads=[bXT], dma_on=bXT)
            sc.barrier()

    def phase_B():
        with ExitStack() as ps_:
            def sbl(name, shape, dt):
                return ps_.enter_context(nc.sbuf_tensor(name, list(shape), dt))
            uT = [sbl("uT%d" % i, [128, 32, 1040], BF16) for i in range(2)]
            wsl = [sbl("wsl%d" % i, [128, 32, 128], BF16) for i in range(3)]
            stg = [sbl("stg%d" % i, [128, 512], BF16) for i in range(4)]
            vstg = [sbl("vstg%d" % i, [128, 4, 128], BF16) for i in range(2)]
            ftmp = [sbl("ftmp%d" % i, [16, 512], F32) for i in range(2)]
            nfb = sbl("nfb", [16, 1], F32)
            b_uT = sc.bufs(2, "uT")
            b_w = sc.bufs(3, "wsl")
            b_stg = sc.bufs(4, "stg")
            b_vstg = sc.bufs(2, "vstg")
            b_ftmp = sc.bufs(2, "ftmp")
            b_nfb = sc.buf("nfb")
            sc.op("sp", lambda e: e.dma_start(out=nfb[:], in_=negfb), writes=[b_nfb], dma_on=b_nfb)

            kchunks = ([("kt", KTf, h) for h in range(H)] + [("kt", KTd, h) for h in range(H)] +
                       [("vt", Vf, h) for h in range(H)] + [("vt", Vd, h) for h in range(H)] +
                       [("k2", KI2, 0), ("fa", None, 0)])
            qchunks = ([("kt", QTf, h) for h in range(H)] + [("kt", QTd, h) for h in range(H)] +
                       [("kt", QI, h) for h in range(16)] + [("wi", None, 0)] +
                       [("g", Gs, j) for j in range(64)])
            ktiles = [[(r * 1024, 512), (r * 1024 + 512, 512)] for r in range(3)] + \
                     [[(3072, 512), (3584, 512), (4096, 16)]]
            qtiles = [[(r * 1024, 512), (r * 1024 + 512, 512)] for r in range(2)]
            cnt = {"w": 0, "ps": 0, "stg": 0, "v": 0, "f": 0, "u": 0}

            def run_side(UTsrc, tiles, chunks, wdram):
                for segs in tiles:
                    c_lo = segs[0][0]
                    ncol = segs[-1][0] + segs[-1][1] - c_lo
                    ui = cnt["u"] % 2
                    cnt["u"] += 1
                    sc.op("sp", lambda e, ui=ui, c_lo=c_lo, ncol=ncol: e.dma_start(
                        out=uT[ui][:, :, 0:ncol],
                        in_=UTsrc[:, :, c_lo:c_lo + ncol].rearrange("kc kp c -> kp kc c")),
                        writes=[b_uT[ui]], dma_on=b_uT[ui])
                    for ci, (kind, dst, idx) in enumerate(chunks):
                        wi_ = cnt["w"] % 3
                        cnt["w"] += 1
                        sc.op("pool", lambda e, wi_=wi_, ci=ci: e.dma_start(out=wsl[wi_][:], in_=wdram[ci]),
                              writes=[b_w[wi_]], dma_on=b_w[wi_])
                        for (c0, n) in segs:
                            bank = cnt["ps"] % 6
                            cnt["ps"] += 1
                            o0 = c0 - c_lo

                            def mm(e, wi_=wi_, ui=ui, bank=bank, o0=o0, n=n):
                                ins = None
                                for kc in range(32):
                                    ins = e.matmul(psum[bank][:, 0:n], wsl[wi_][:, kc, :],
                                                   uT[ui][:, kc, o0:o0 + n], start=(kc == 0), stop=(kc == 31))
                                return ins
                            sc.op("pe", mm, reads=[b_w[wi_], b_uT[ui]], writes=[b_ps[bank]])
                            if kind in ("kt", "k2", "g", "vt"):
                                si = cnt["stg"] % 4
                                cnt["stg"] += 1
                                fn = AF.Sigmoid if kind == "g" else AF.Copy
                                sc.op("act", lambda e, si=si, bank=bank, n=n, fn=fn: e.activation(
                                    out=stg[si][:, 0:n], in_=psum[bank][:, 0:n], func=fn),
                                    reads=[b_ps[bank]], writes=[b_stg[si]])
                                if kind == "vt":
                                    vi = cnt["v"] % 2
                                    cnt["v"] += 1
                                    tb = 6 + vi
                                    nb = (n + 127) // 128
                                    w_ = min(n, 128)

                                    def trv(e, si=si, tb=tb, nb=nb, w_=w_):
                                        ins = None
                                        for j in range(nb):
                                            ins = e.transpose(out=psb(tb)[:w_, j * 128:(j + 1) * 128],
                                                              in_=stg[si][:, j * 128:j * 128 + w_],
                                                              identity=identb[:])
                                        return ins
                                    sc.op("pe", trv, reads=[b_stg[si], b_ident], writes=[b_ps[tb]])
                                    sc.op("dve", lambda e, vi=vi, tb=tb, nb=nb, w_=w_: e.tensor_copy(
                                        out=vstg[vi][:w_, 0:nb, :],
                                        in_=psb(tb)[:w_, 0:nb * 128].rearrange("p (a b) -> p a b", b=128)),
                                        reads=[b_ps[tb]], writes=[b_vstg[vi]])
                                    if n >= 128:
                                        dv = dst[idx, c0:c0 + n, :].rearrange("(b s) d -> s b d", s=128)
                                        sc.op("sp", lambda e, vi=vi, dv=dv, nb=nb: e.dma_start(
                                            out=dv, in_=vstg[vi][:, 0:nb, :]), reads=[b_vstg[vi]], dma_on=b_vstg[vi])
                                    else:
                                        dv = dst[idx, c0:c0 + n, :]
                                        sc.op("sp", lambda e, vi=vi, dv=dv, n=n: e.dma_start(
                                            out=dv, in_=vstg[vi][:n, 0, :]), reads=[b_vstg[vi]], dma_on=b_vstg[vi])
                                else:
                                    dd = dst[c0:c0 + n] if False else (
                                        dst[:, c0:c0 + n] if kind == "k2" else dst[idx, :, c0:c0 + n])
                                    sc.op("sp", lambda e, si=si, dd=dd, n=n: e.dma_start(
                                        out=dd, in_=stg[si][:, 0:n]), reads=[b_stg[si]], dma_on=b_stg[si])
                            elif kind == "fa":
                                fi = cnt["f"] % 2
                                cnt["f"] += 1
                                sc.op("act", lambda e, fi=fi, bank=bank, n=n: e.activation(
                                    out=ftmp[fi][:, 0:n], in_=psum[bank][0:16, 0:n], func=AF.Exp,
                                    bias=nfb[:, 0:1], scale=-1.0),
                                    reads=[b_ps[bank], b_nfb], writes=[b_ftmp[fi]])
                                sc.op("act", lambda e, fi=fi, n=n: e.activation(
                                    out=ftmp[fi][:, 0:n], in_=ftmp[fi][:, 0:n], func=AF.Ln, bias=1.0),
                                    reads=[b_ftmp[fi]], writes=[b_ftmp[fi]])
                                sc.op("dve", lambda e, fi=fi, c0=c0, n=n: e.tensor_scalar(
                                    out=LF[:, c0:c0 + n], in0=ftmp[fi][:, 0:n], scalar1=-1.0, scalar2=None,
                                    op0=ALU.mult), reads=[b_ftmp[fi]], pwrites=[b_LF])
                            elif kind == "wi":
                                sc.op("act", lambda e, bank=bank, c0=c0, n=n: e.copy(
                                    out=WIT[:, c0:c0 + n], in_=psum[bank][0:32, 0:n]),
                                    reads=[b_ps[bank]], pwrites=[b_WIT])
            run_side(UTa, ktiles, kchunks, wk)
            run_side(UTo, qtiles, qchunks, wq)
            sc.barrier()


    SCALE = float(HD) ** -0.5
    CW = float(DIDX) ** -0.5 * float(NIDX) ** -0.5

    def phase_B2():
        with ExitStack() as ps_:
            def sbl(name, shape, dt):
                return ps_.enter_context(nc.sbuf_tensor(name, list(shape), dt))
            ones16 = sbl("ones16", [16, L], F32)
            dto = sbl("dto", [16, NOWN], F32)
            tmpo = [sbl("tmpo%d" % i, [16, 128], F32) for i in range(2)]
            selt = sbl("selt", [16, 2], F32)
            b_ones = sc.buf("ones16")
            b_dto = sc.buf("dto")
            b_tmpo = sc.bufs(2, "tmpo")
            b_sel = sc.buf("sel")
            sc.op("pool", lambda e: e.memset(ones16[:], 1.0), writes=[b_ones])
            sc.op("sp", lambda e: e.dma_start(out=selt[:], in_=sel), writes=[b_sel], dma_on=b_sel)
            sc.op("dve", lambda e: e.tensor_tensor_scan(
                out=DT[:, L:TK], data0=ones16[:, 0:16], data1=LF[:, L:TK], initial=0.0,
                op0=ALU.mult, op1=ALU.add), reads=[b_LF, b_ones], writes=[b_DT])
            sc.op("dve", lambda e: e.tensor_tensor_scan(
                out=DT[:, 0:L], data0=ones16[:, 0:L], data1=LF[:, 0:L], initial=DT[:, TK - 1:TK],
                op0=ALU.mult, op1=ALU.add), reads=[b_LF, b_ones, b_DT], pwrites=[b_DT])
            for q in range(8):
                bank = q % 4

                def trd(e, q=q, bank=bank):
                    ins = None
                    for j in range(4):
                        blk = q * 4 + j
                        ins = e.transpose(out=psum[bank][:, j * 16:(j + 1) * 16],
                                          in_=DT[:, blk * 128:(blk + 1) * 128], identity=identf[:16, :16])
                    return ins
                sc.op("pe", trd, reads=[b_DT, b_ident], writes=[b_ps[bank]])
                sc.op("dve", lambda e, q=q, bank=bank: e.tensor_scalar(
                    out=negD[:, q * 4:q * 4 + 4, :],
                    in0=psum[bank][:, 0:64].rearrange("p (a b) -> p a b", b=16),
                    scalar1=-1.0, scalar2=None, op0=ALU.mult), reads=[b_ps[bank]], pwrites=[b_negD])
            sc.op("pe", lambda e: e.transpose(out=psum[4][:16, 0:16], in_=DT[:, L:TK], identity=identf[:16, :16]),
                  reads=[b_DT, b_ident], writes=[b_ps[4]])
            sc.op("dve", lambda e: e.tensor_scalar(out=negD[:16, 32, :], in0=psum[4][:16, 0:16],
                                                   scalar1=-1.0, scalar2=None, op0=ALU.mult),
                  reads=[b_ps[4]], pwrites=[b_negD])
            for i in range(16):
                t_, bt_ = tmpo[i % 2], b_tmpo[i % 2]
                sc.op("dve", lambda e, i=i, t_=t_: e.tensor_scalar(
                    out=t_[:], in0=DT[:, (2 * i) * 128:(2 * i + 1) * 128], scalar1=selt[:, 0:1], scalar2=None,
                    op0=ALU.mult), reads=[b_DT, b_sel], writes=[bt_])
                sc.op("dve", lambda e, i=i, t_=t_: e.scalar_tensor_tensor(
                    out=dto[:, i * 128:(i + 1) * 128], in0=DT[:, (2 * i + 1) * 128:(2 * i + 2) * 128],
                    scalar=selt[:, 1:2], in1=t_[:], op0=ALU.mult, op1=ALU.add),
                    reads=[b_DT, b_sel, bt_], pwrites=[b_dto])
            sc.op("sp", lambda e: e.dma_start(out=DTo, in_=dto[:]), reads=[b_dto], dma_on=b_dto)
            sc.barrier()

    def phase_C0():
        with ExitStack() as ps_:
            def sbl(name, shape, dt):
                return ps_.enter_context(nc.sbuf_tensor(name, list(shape), dt))
            ncf = sbl("ncf0", [128, H], F32)
            utl = [sbl("utl%d" % i, [128, 3, 128], F32) for i in range(2)]
            uml = [sbl("uml%d" % i, [16, 128], F32) for i in range(2)]
            bm = [sbl("bm0%d" % i, [128, 9, 512], BF16) for i in range(2)]
            bmm = [sbl("bmm0%d" % i, [16, 512], BF16) for i in range(2)]
            b_ncf = sc.buf("ncf")
            b_utl = sc.bufs(2, "utl")
            b_uml = sc.bufs(2, "uml")
            b_bm = sc.bufs(2, "bm")
            b_bmm = sc.bufs(2, "bmm")
            sc.op("sp", lambda e: e.dma_start(out=ncf[:], in_=ncfar), writes=[b_ncf], dma_on=b_ncf)
            for h in range(H):
                i2 = h % 2
                sc.op("sp", lambda e, h=h, i2=i2: e.dma_start(out=utl[i2][:], in_=ubias[h]),
                      writes=[b_utl[i2]], dma_on=b_utl[i2])
                sc.op("sp", lambda e, h=h, i2=i2: e.dma_start(out=uml[i2][:], in_=umeta[h]),
                      writes=[b_uml[i2]], dma_on=b_uml[i2])
                sc.op("dve", lambda e, i2=i2: e.memset(bm[i2][:], 1.0), writes=[b_bm[i2]])
                sc.op("dve", lambda e, i2=i2: e.memset(bmm[i2][:], 1.0), writes=[b_bmm[i2]])

                def fill(e, h=h, i2=i2):
                    ins = None
                    for k in range(4):
                        for jj in range(3):
                            slot = 2 * k + (jj - 1) + 1
                            if slot < 0 or slot > 8:
                                continue
                            ins = e.activation(out=bm[i2][:, slot, k * 128:(k + 1) * 128], in_=utl[i2][:, jj, :],
                                               func=AF.Exp, bias=ncf[:, h:h + 1], scale=1.0)
                    return ins
                sc.op("act", fill, reads=[b_utl[i2], b_ncf], pwrites=[b_bm[i2]])
                sc.op("act", lambda e, h=h, i2=i2: e.activation(
                    out=bmm[i2][:, 0:128], in_=uml[i2][:], func=AF.Exp, bias=ncf[:16, h:h + 1], scale=1.0),
                    reads=[b_uml[i2], b_ncf], pwrites=[b_bmm[i2]])
                sc.op("sp", lambda e, h=h, i2=i2: e.dma_start(out=BMs[h], in_=bm[i2][:]),
                      reads=[b_bm[i2]], dma_on=b_bm[i2])
                sc.op("sp", lambda e, h=h, i2=i2: e.dma_start(out=BMm[h], in_=bmm[i2][:]),
                      reads=[b_bmm[i2]], dma_on=b_bmm[i2])
            sc.barrier()

    def phase_C(do_fox=True, do_dsa=True):
        with ExitStack() as ps_:
            def sbl(name, shape, dt):
                return ps_.enter_context(nc.sbuf_tensor(name, list(shape), dt))
            kt = [sbl("kt%d" % i, [128, TK], BF16) for i in range(2)]
            vv = [sbl("vv%d" % i, [128, 33, 128], BF16) for i in range(2)]
            qt = [sbl("qt%d" % i, [128, 512], BF16) for i in range(2)]
            tmp = [sbl("tmp%d" % i, [128, 512], F32) for i in range(4)]
            pt = [sbl("pt%d" % i, [128, 512], BF16) for i in range(4)]
            rinv = [sbl("rinv%d" % i, [128, 512], F32) for i in range(2)]
            ost = [sbl("ost%d" % i, [128, 512], BF16) for i in range(2)]
            b_kt = sc.bufs(2, "kt")
            b_vv = sc.bufs(2, "vv")
            b_qt = sc.bufs(2, "qt")
            b_tmp = sc.bufs(4, "tmp")
            b_pt = sc.bufs(4, "pt")
            b_rinv = sc.bufs(2, "rinv")
            b_ost = sc.bufs(2, "ost")
            cnt = {"hd": 0, "s": 0, "t": 0, "p": 0, "e": 0, "e2": 0, "ip": 0, "rr": 0}

            def load_head(KT, V, QT, h, g, hb):
                nfb = 8 * g + 8
                sc.op("sp", lambda e: e.dma_start(out=kt[hb][:, 0:nfb * 128], in_=KT[h, :, 0:nfb * 128]),
                      writes=[b_kt[hb]], dma_on=b_kt[hb])
                sc.op("sp", lambda e: e.dma_start(out=kt[hb][:, L:TK], in_=KT[h, :, L:TK]),
                      pwrites=[b_kt[hb]], dma_on=b_kt[hb])
                sc.op("sp", lambda e: e.dma_start(
                    out=vv[hb][:, 0:nfb, :], in_=V[h, 0:nfb * 128, :].rearrange("(b s) d -> s b d", s=128)),
                    writes=[b_vv[hb]], dma_on=b_vv[hb])
                sc.op("sp", lambda e: e.dma_start(out=vv[hb][:16, 32, :], in_=V[h, L:TK, :]),
                      pwrites=[b_vv[hb]], dma_on=b_vv[hb])
                sc.op("sp", lambda e: e.dma_start(out=qt[hb][:], in_=QT[h, :, g * 512:(g + 1) * 512]),
                      writes=[b_qt[hb]], dma_on=b_qt[hb])

            def attn_head(g, hb, ew, OT, h, extra=None):
                nfb = 8 * g + 8
                blocks = [("m", 32)] + [("f", i) for i in range(nfb)]
                N = len(blocks)
                ob = 3 + 2 * (cnt["hd"] % 2)
                cnt["hd"] += 1
                sbanks = []

                def mm1(n):
                    kind, blk = blocks[n]
                    np_ = 16 if kind == "m" else 128
                    c0 = L if kind == "m" else blk * 128
                    sbank = (0, 1, 2, 7)[cnt["s"] % 4]
                    cnt["s"] += 1
                    sbanks.append(sbank)
                    ex = extra(kind, blk) if extra is not None else None
                    if ex is None:
                        sc.op("pe", lambda e: e.matmul(psum[sbank][:np_, :], kt[hb][:, c0:c0 + np_], qt[hb][:],
                                                       start=True, stop=True),
                              reads=[b_kt[hb], b_qt[hb]], writes=[b_ps[sbank]])
                    else:
                        def mm1x(e):
                            e.matmul(psum[sbank][:np_, :], kt[hb][:, c0:c0 + np_], qt[hb][:], start=True, stop=False)
                            return e.matmul(psum[sbank][:np_, :], identb[:], ex[0], start=False, stop=True)
                        sc.op("pe", mm1x, reads=[b_kt[hb], b_qt[hb], b_ident, ex[1]], writes=[b_ps[sbank]])
                LOOK = 3
                for n in range(min(LOOK, N)):
                    mm1(n)
                for n in range(N):
                    if n + LOOK < N:
                        mm1(n + LOOK)
                    kind, blk = blocks[n]
                    np_ = 16 if kind == "m" else 128
                    pi = ew(n, kind, blk, np_, sbanks[n])

                    def mm23(e, np_=np_, blk=blk, pi=pi, n=n):
                        e.matmul(psum[ob][:, :], vv[hb][:np_, blk, :], pt[pi][:np_, :],
                                 start=(n == 0), stop=(n == N - 1))
                        return e.matmul(psum[ob + 1][:, :], onesb[:np_, :], pt[pi][:np_, :],
                                        start=(n == 0), stop=(n == N - 1))
                    sc.op("pe", mm23, reads=[b_vv[hb], b_pt[pi], b_ident],
                          writes=([b_ps[ob], b_ps[ob + 1]] if n == 0 else []),
                          pwrites=([] if n == 0 else [b_ps[ob], b_ps[ob + 1]]))
                ri = cnt["hd"] % 2
                sc.op("dve", lambda e: e.reciprocal(out=rinv[ri][:], in_=psum[ob + 1][:]),
                      reads=[b_ps[ob + 1]], writes=[b_rinv[ri]])
                sc.op("dve", lambda e: e.tensor_tensor(out=ost[ri][:], in0=psum[ob][:], in1=rinv[ri][:], op=ALU.mult),
                      reads=[b_ps[ob], b_rinv[ri]], writes=[b_ost[ri]])
                sc.op("sp", lambda e: e.dma_start(out=OT[h, :, g * 512:(g + 1) * 512], in_=ost[ri][:]),
                      reads=[b_ost[ri]], dma_on=b_ost[ri])

            if do_fox:
                with ExitStack() as pf_:
                    def sbf(name, shape, dt):
                        return pf_.enter_context(nc.sbuf_tensor(name, list(shape), dt))
                    fmb = sbf("fmb", [128, 8, 512], BF16)
                    dtb = [sbf("dtb%d" % i, [128, 512], F32) for i in range(2)]
                    b_fmt = sc.buf("fmb")
                    b_dtb = sc.bufs(2, "dtb")
                    fmt = sbf("fmt", [128, 8, 512], F32)
                    b_fmt32 = sc.buf("fmt32")
                    sc.op("sp", lambda e: e.dma_start(out=fmt[:], in_=fmask.rearrange("r s t -> s r t")),
                          writes=[b_fmt32], dma_on=b_fmt32)
                    sc.op("dve", lambda e: e.tensor_copy(out=fmb[:], in_=fmt[:]), reads=[b_fmt32], writes=[b_fmt])

                    def load_fox(g, h, hb):
                        load_head(KTf, Vf, QTf, h, g, hb)
                        sc.op("sp", lambda e: e.dma_start(
                            out=dtb[hb][:], in_=DTo[h:h + 1, g * 512:(g + 1) * 512].partition_broadcast(128)),
                            writes=[b_dtb[hb]], dma_on=b_dtb[hb])
                    seq = [(g, h) for g in range(4) for h in range(H)]
                    load_fox(seq[0][0], seq[0][1], 0)
                    for qi_, (g, h) in enumerate(seq):
                        hb = qi_ % 2
                        if qi_ + 1 < len(seq):
                            load_fox(seq[qi_ + 1][0], seq[qi_ + 1][1], 1 - hb)

                        def ew(n, kind, blk, np_, sbank, g=g, h=h, hb=hb):
                            ti = cnt["t"] % 4
                            cnt["t"] += 1
                            pi = cnt["p"] % 4
                            cnt["p"] += 1
                            src, bsrc = dtb[hb][:np_, :], b_dtb[hb]
                            sc.op("dve", lambda e: e.scalar_tensor_tensor(
                                out=tmp[ti][:np_, :], in0=psum[sbank][:np_, :], scalar=SCALE, in1=src,
                                op0=ALU.mult, op1=ALU.add), reads=[b_ps[sbank], bsrc], writes=[b_tmp[ti]])
                            sc.op("act", lambda e: e.activation(
                                out=pt[pi][:np_, :], in_=tmp[ti][:np_, :], func=AF.Exp,
                                bias=negD[:np_, blk, h:h + 1], scale=1.0),
                                reads=[b_tmp[ti], b_negD], writes=[b_pt[pi]])
                            return pi
                        def extra(kind, blk, g=g):
                            if kind == "f" and blk >= 8 * g:
                                return (fmb[:, blk - 8 * g, :], b_fmt)
                            return None
                        attn_head(g, hb, ew, OTf, h, extra)
                    sc.barrier()

            if do_dsa:
                with ExitStack() as pd_:
                    def sbd(name, shape, dt):
                        return pd_.enter_context(nc.sbuf_tensor(name, list(shape), dt))
                    ki2 = sbd("ki2", [128, TK], BF16)
                    qi = [sbd("qi%d" % i, [128, 16, 128], BF16) for i in range(2)]
                    wa = [sbd("wa%d" % i, [128, 32], F32) for i in range(2)]
                    ws = [sbd("ws%d" % i, [128, 32], F32) for i in range(2)]
                    sct = sbd("sct", [128, TK], F32)
                    wkk = sbd("wkk", [128, TK], F32)
                    rr = [sbd("rr%d" % i, [128, 512], F32) for i in range(4)]
                    m8 = sbd("m8", [128, 8], F32)
                    thr = sbd("thr", [128, 1], F32)
                    m01 = sbd("m01", [128, TK], BF16)
                    maskT = sbd("maskT", [128, 33, 512], BF16)
                    admt = sbd("admt", [128, 256], F32)
                    cf = sbd("cf", [128, H], F32)
                    et = [sbd("et%d" % i, [128, 512], BF16) for i in range(4)]
                    et2 = [sbd("et2%d" % i, [128, 512], BF16) for i in range(2)]
                    bm = [sbd("bm%d" % i, [128, 9, 512], BF16) for i in range(2)]
                    bmm = [sbd("bmm%d" % i, [16, 512], BF16) for i in range(2)]
                    b_ki2 = sc.buf("ki2")
                    b_qi = sc.bufs(2, "qi")
                    b_wa = sc.bufs(2, "wa")
                    b_ws = sc.bufs(2, "ws")
                    b_scA = sc.bufs(10, "scA")
                    b_scB = sc.bufs(10, "scB")
                    b_wk = sc.buf("wkk")
                    b_rr = sc.bufs(4, "rr")
                    b_m8 = sc.buf("m8")
                    b_thr = sc.buf("thr")
                    b_m01 = sc.buf("m01")
                    b_maskT = sc.buf("maskT")
                    b_adm = sc.buf("adm")
                    b_cf = sc.buf("cf")
                    b_et = sc.bufs(4, "et")
                    b_et2 = sc.bufs(2, "et2")
                    b_bm = sc.bufs(2, "bm")
                    b_bmm = sc.bufs(2, "bmm")
                    sc.op("sp", lambda e: e.dma_start(out=ki2[:], in_=KI2), writes=[b_ki2], dma_on=b_ki2)
                    sc.op("sp", lambda e: e.dma_start(out=admt[:], in_=adm), writes=[b_adm], dma_on=b_adm)
                    sc.op("sp", lambda e: e.dma_start(out=cf[:], in_=cfar), writes=[b_cf], dma_on=b_cf)

                    def load_dsa(g, h, hb):
                        load_head(KTd, Vd, QTd, h, g, hb)
                        sc.op("sp", lambda e: e.dma_start(out=bm[hb][:], in_=BMs[h]), writes=[b_bm[hb]], dma_on=b_bm[hb])
                        if g == 0:
                            sc.op("sp", lambda e: e.dma_start(out=bmm[hb][:], in_=BMm[h]),
                                  writes=[b_bmm[hb]], dma_on=b_bmm[hb])

                    def indexer_block(g, k):
                        i = 4 * g + k
                        nkb = 2 * i + 2
                        ncols = nkb * 128
                        ntot = ncols + 16
                        q2 = i % 2
                        sc.op("sp", lambda e: e.dma_start(
                            out=qi[q2][:], in_=QI[:, :, i * 128:(i + 1) * 128].rearrange("c p t -> p c t")),
                            writes=[b_qi[q2]], dma_on=b_qi[q2])
                        wb = 6 + (i % 2)
                        sc.op("pe", lambda e: e.transpose(out=psum[wb][:, 0:32], in_=WIT[:, i * 128:(i + 1) * 128],
                                                          identity=identf[:32, :32]),
                              reads=[b_WIT, b_ident], writes=[b_ps[wb]])
                        sc.op("act", lambda e: e.activation(out=wa[q2][:], in_=psum[wb][:, 0:32], func=AF.Abs, scale=CW),
                              reads=[b_ps[wb]], writes=[b_wa[q2]])
                        sc.op("act", lambda e: e.sign(out=ws[q2][:], in_=psum[wb][:, 0:32]),
                              reads=[b_ps[wb]], writes=[b_ws[q2]])
                        tiles = []
                        c0 = 0
                        while c0 < ncols:
                            n = min(512, ncols - c0)
                            tiles.append((c0, n, c0))
                            c0 += n
                        tiles.append((L, 16, ncols))
                        for ti_, (c0, n, d0) in enumerate(tiles):
                            bA, bB = b_scA[ti_], b_scB[ti_]
                            for hi in range(NIDX):
                                ch, half = hi // 2, hi % 2
                                bank = cnt["ip"] % 6
                                cnt["ip"] += 1
                                ri = cnt["rr"] % 4
                                cnt["rr"] += 1
                                sc.op("pe", lambda e, ch=ch, half=half, bank=bank, c0=c0, n=n: e.matmul(
                                    psum[bank][:, 0:n], qi[q2][64 * half:64 * half + 64, ch, :],
                                    ki2[64 * half:64 * half + 64, c0:c0 + n], start=True, stop=True),
                                    reads=[b_qi[q2], b_ki2], writes=[b_ps[bank]])
                                sc.op("act", lambda e, hi=hi, bank=bank, ri=ri, n=n: e.activation(
                                    out=rr[ri][:, 0:n], in_=psum[bank][:, 0:n], func=AF.Relu,
                                    scale=wa[q2][:, hi:hi + 1]), reads=[b_ps[bank], b_wa[q2]], writes=[b_rr[ri]])
                                if hi == 0:
                                    sc.op("dve", lambda e, hi=hi, ri=ri, n=n, d0=d0: e.tensor_scalar(
                                        out=sct[:, d0:d0 + n], in0=rr[ri][:, 0:n], scalar1=ws[q2][:, hi:hi + 1],
                                        scalar2=None, op0=ALU.mult), reads=[b_rr[ri], b_ws[q2]], writes=[bA])
                                else:
                                    sc.op("dve", lambda e, hi=hi, ri=ri, n=n, d0=d0: e.scalar_tensor_tensor(
                                        out=sct[:, d0:d0 + n], in0=rr[ri][:, 0:n], scalar=ws[q2][:, hi:hi + 1],
                                        in1=sct[:, d0:d0 + n], op0=ALU.mult, op1=ALU.add),
                                        reads=[b_rr[ri], b_ws[q2]], writes=[bA])
                        lastt = [b_scA[t] for t in range(len(tiles))]
                        sc.op("dve", lambda e: e.tensor_tensor(
                            out=sct[:, ncols - 256:ncols], in0=sct[:, ncols - 256:ncols], in1=admt[:], op=ALU.add),
                            reads=[b_adm], writes=lastt)
                        allsc = [b_scA[t] for t in range(len(tiles))] + [b_scB[t] for t in range(len(tiles))]
                        for rnd in range(KTOP // 8):
                            srcv = sct if rnd == 0 else wkk
                            sc.op("dve", lambda e, srcv=srcv: e.max(out=m8[:], in_=srcv[:, 0:ntot]),
                                  reads=allsc + [b_wk], writes=[b_m8])
                            sc.op("dve", lambda e, srcv=srcv: e.match_replace(
                                out=wkk[:, 0:ntot], in_to_replace=m8[:], in_values=srcv[:, 0:ntot], imm_value=-3.0e38),
                                reads=[b_m8] + ([b_scA[t] for t in range(len(tiles))] if rnd == 0 else []),
                                writes=[b_wk] + ([b_scB[t] for t in range(len(tiles))] if rnd == 0 else []))
                        sc.op("dve", lambda e: e.tensor_scalar(out=thr[:], in0=m8[:, 7:8], scalar1=-1.0e29, scalar2=None,
                                                               op0=ALU.max), reads=[b_m8], writes=[b_thr])
                        sc.op("dve", lambda e: e.tensor_scalar(out=m01[:, 0:ntot], in0=sct[:, 0:ntot],
                                                               scalar1=thr[:, 0:1], scalar2=None, op0=ALU.is_ge),
                              reads=[b_thr] + lastt, writes=[b_m01])
                        j0 = 0
                        while j0 < nkb:
                            nb = min(4, nkb - j0)
                            bank = cnt["ip"] % 6
                            cnt["ip"] += 1

                            def trm(e, j0=j0, nb=nb, bank=bank):
                                ins = None
                                for j in range(nb):
                                    ins = e.transpose(out=psb(bank)[:, j * 128:(j + 1) * 128],
                                                      in_=m01[:, (j0 + j) * 128:(j0 + j + 1) * 128], identity=identb[:])
                                return ins
                            sc.op("pe", trm, reads=[b_m01, b_ident], writes=[b_ps[bank]])
                            ceng = "act" if (j0 // 4) % 2 == 0 else "pool"
                            if ceng == "act":
                                sc.op("act", lambda e, j0=j0, nb=nb, bank=bank: e.copy(
                                    out=maskT[:, j0:j0 + nb, k * 128:(k + 1) * 128],
                                    in_=psb(bank)[:, 0:nb * 128].rearrange("p (a b) -> p a b", b=128)),
                                    reads=[b_ps[bank]], pwrites=[b_maskT])
                            else:
                                sc.op("dve", lambda e, j0=j0, nb=nb, bank=bank: e.tensor_copy(
                                    out=maskT[:, j0:j0 + nb, k * 128:(k + 1) * 128],
                                    in_=psb(bank)[:, 0:nb * 128].rearrange("p (a b) -> p a b", b=128)),
                                    reads=[b_ps[bank]], pwrites=[b_maskT])
                            j0 += nb
                        bank = cnt["ip"] % 6
                        cnt["ip"] += 1
                        sc.op("pe", lambda e, bank=bank: e.transpose(out=psb(bank)[:16, 0:128], in_=m01[:, ncols:ncols + 16],
                                                                     identity=identb[:]),
                              reads=[b_m01, b_ident], writes=[b_ps[bank]])
                        sc.op("act", lambda e, bank=bank: e.copy(out=maskT[:16, 32, k * 128:(k + 1) * 128],
                                                                 in_=psb(bank)[:16, 0:128]),
                              reads=[b_ps[bank]], pwrites=[b_maskT])

                    for g in range(4):
                        sc.op("dve", lambda e: e.memset(maskT[:], 0.0), writes=[b_maskT])
                        for k in range(4):
                            indexer_block(g, k)
                        load_dsa(g, 0, 0)
                        for h in range(H):
                            hb = h % 2
                            if h + 1 < H:
                                load_dsa(g, h + 1, 1 - hb)

                            def ew(n, kind, blk, np_, sbank, g=g, h=h, hb=hb):
                                ei = cnt["e"] % 4
                                cnt["e"] += 1
                                pi = cnt["p"] % 4
                                cnt["p"] += 1
                                sc.op("act", lambda e: e.activation(
                                    out=et[ei][:np_, :], in_=psum[sbank][:np_, :], func=AF.Exp,
                                    bias=cf[:np_, h:h + 1], scale=SCALE),
                                    reads=[b_ps[sbank], b_cf], writes=[b_et[ei]])
                                srcE, bsrcE = et[ei], b_et[ei]
                                near = None
                                if kind == "f" and blk >= 8 * g - 1:
                                    near = (bm[hb][:np_, blk - (8 * g - 1), :], b_bm[hb])
                                elif kind == "m" and g == 0:
                                    near = (bmm[hb][:np_, :], b_bmm[hb])
                                if near is not None:
                                    e2 = cnt["e2"] % 2
                                    cnt["e2"] += 1
                                    sc.op("dve", lambda e: e.tensor_tensor(
                                        out=et2[e2][:np_, :], in0=et[ei][:np_, :], in1=near[0], op=ALU.mult),
                                        reads=[b_et[ei], near[1]], writes=[b_et2[e2]])
                                    srcE, bsrcE = et2[e2], b_et2[e2]
                                meng = "dve"
                                sc.op(meng, lambda e: e.tensor_tensor(
                                    out=pt[pi][:np_, :], in0=srcE[:np_, :], in1=maskT[:np_, blk, :], op=ALU.mult),
                                    reads=[bsrcE, b_maskT], writes=[b_pt[pi]])
                                return pi
                            attn_head(g, hb, ew, OTd, h)
                    sc.barrier()


    def phase_D():
        with ExitStack() as ps_:
            def sbl(name, shape, dt):
                return ps_.enter_context(nc.sbuf_tensor(name, list(shape), dt))
            arena = sbl("arena", [128, 32768], F32)
            hidv = arena[:].bitcast(BF16).rearrange("p (c t) -> p c t", t=512)
            OTf_v = hidv[:, 0:16, :]
            OTd_v = hidv[:, 16:32, :]
            mixedT = hidv[:, 32:64, :]
            H2T = arena[:, 16384:32768].rearrange("p (c t) -> p c t", t=512)
            orow = arena[:, 0:16384].rearrange("p (b m) -> p b m", m=D)
            U2T = sbl("U2T", [128, 32, 512], BF16)
            wsl = [sbl("wD%d" % i, [128, 32, 128], BF16) for i in range(3)]
            gA = [sbl("gA%d" % i, [128, 512], BF16) for i in range(2)]
            gB = [sbl("gB%d" % i, [128, 512], BF16) for i in range(2)]
            t1 = [sbl("t1%d" % i, [128, 512], F32) for i in range(2)]
            t2 = [sbl("t2%d" % i, [128, 512], F32) for i in range(2)]
            xst = [sbl("xst%d" % i, [128, 512], F32) for i in range(2)]
            h3s = t2
            sq = [sbl("sq%d" % i, [128, 512], BF16) for i in range(2)]
            rstd = sbl("rstd", [128, 512], F32)
            g2t = sbl("g2t", [128, 32], F32)
            gft = sbl("gft", [128, 32], F32)
            b_OT = sc.buf("OTv")
            b_mix = sc.buf("mixedT")
            b_H2T = sc.buf("H2T")
            b_hid = sc.buf("hid")
            b_orow = sc.buf("orow")
            b_U2T = sc.buf("U2T")
            b_w = sc.bufs(3, "wD")
            b_gA = sc.bufs(2, "gA")
            b_gB = sc.bufs(2, "gB")
            b_t1 = sc.bufs(2, "t1")
            b_t2 = sc.bufs(2, "t2")
            b_xst = sc.bufs(2, "xst")
            b_h3s = b_t2
            b_sq = sc.bufs(2, "sq")
            b_rstd = sc.buf("rstd")
            b_g = sc.buf("g2f")
            b_h2d = sc.bufs(32, "h2d")
            sc.op("sp", lambda e: e.dma_start(out=g2t[:], in_=g_mlp), writes=[b_g], dma_on=b_g)
            sc.op("sp", lambda e: e.dma_start(out=gft[:], in_=g_fin), pwrites=[b_g], dma_on=b_g)
            cnt = {"w": 0, "ps": 0, "x": 0, "q": 0, "h": 0}

            def wload(src, kcn):
                wi_ = cnt["w"] % 3
                cnt["w"] += 1
                sc.op("pool", lambda e: e.dma_start(out=wsl[wi_][:, 0:kcn, :], in_=src),
                      writes=[b_w[wi_]], dma_on=b_w[wi_])
                return wi_

            def nbank(nb=6):
                b = cnt["ps"] % nb
                cnt["ps"] += 1
                return b

            def sumsq_rstd(tag):
                sc.op("act", lambda e: e.activation(out=rstd[:], in_=psum[7][:], func=AF.Sqrt,
                                                    bias=epsc[:, 0:1], scale=1.0 / D),
                      reads=[b_ps[7], b_ident], writes=[b_rstd])
                sc.op("dve", lambda e: e.reciprocal(out=rstd[:], in_=rstd[:]), reads=[b_rstd], writes=[b_rstd])

            def row_tile(rt):
                t0 = rt * 512
                sc.op("sp", lambda e: e.dma_start(out=OTf_v, in_=OTf[:, :, t0:t0 + 512].rearrange("h d t -> d h t")),
                      writes=[b_OT], dma_on=b_OT)
                sc.op("sp", lambda e: e.dma_start(out=OTd_v, in_=OTd[:, :, t0:t0 + 512].rearrange("h d t -> d h t")),
                      pwrites=[b_OT], dma_on=b_OT)
                for j in range(32):
                    wa_ = wload(wbf[j], 16)
                    wb_ = wload(wbd[j], 16)
                    gi = j % 2
                    sc.op("sp", lambda e, j=j, gi=gi: e.dma_start(out=gA[gi][:], in_=Gs[j, :, t0:t0 + 512]),
                          writes=[b_gA[gi]], dma_on=b_gA[gi])
                    sc.op("sp", lambda e, j=j, gi=gi: e.dma_start(out=gB[gi][:], in_=Gs[32 + j, :, t0:t0 + 512]),
                          writes=[b_gB[gi]], dma_on=b_gB[gi])
                    ba, bb = nbank(), nbank()

                    def mmA(e, w=wa_, bank=ba, src=OTf_v):
                        ins = None
                        for kc in range(16):
                            ins = e.matmul(psum[bank][:, :], wsl[w][:, kc, :], src[:, kc, :], start=(kc == 0), stop=(kc == 15))
                        return ins

                    def mmB(e, w=wb_, bank=bb, src=OTd_v):
                        ins = None
                        for kc in range(16):
                            ins = e.matmul(psum[bank][:, :], wsl[w][:, kc, :], src[:, kc, :], start=(kc == 0), stop=(kc == 15))
                        return ins
                    sc.op("pe", mmA, reads=[b_w[wa_], b_OT], writes=[b_ps[ba]])
                    sc.op("pe", mmB, reads=[b_w[wb_], b_OT], writes=[b_ps[bb]])
                    sc.op("dve", lambda e, gi=gi, ba=ba: e.tensor_tensor(out=t1[gi][:], in0=psum[ba][:], in1=gA[gi][:], op=ALU.mult),
                          reads=[b_ps[ba], b_gA[gi]], writes=[b_t1[gi]])
                    sc.op("dve", lambda e, gi=gi, bb=bb: e.tensor_tensor(out=t2[gi][:], in0=psum[bb][:], in1=gB[gi][:], op=ALU.mult),
                          reads=[b_ps[bb], b_gB[gi]], writes=[b_t2[gi]])
                    sc.op("dve", lambda e, gi=gi, j=j: e.tensor_tensor(out=mixedT[:, j, :], in0=t1[gi][:], in1=t2[gi][:], op=ALU.add),
                          reads=[b_t1[gi], b_t2[gi]], pwrites=[b_mix])
                for j in range(32):
                    w_ = wload(wo[j], 32)
                    bank = nbank()
                    xi = cnt["x"] % 2
                    cnt["x"] += 1
                    qi_ = cnt["q"] % 2
                    cnt["q"] += 1
                    sc.op("sp", lambda e, j=j, xi=xi: e.dma_start(out=xst[xi][:], in_=XTo[j, :, t0:t0 + 512]),
                          writes=[b_xst[xi]], dma_on=b_xst[xi])

                    def mmO(e, w=w_, bank=bank):
                        ins = None
                        for kc in range(32):
                            ins = e.matmul(psum[bank][:, :], wsl[w][:, kc, :], mixedT[:, kc, :], start=(kc == 0), stop=(kc == 31))
                        return ins
                    sc.op("pe", mmO, reads=[b_w[w_], b_mix], writes=[b_ps[bank]])
                    sc.op("dve", lambda e, j=j, xi=xi, bank=bank: e.tensor_tensor(
                        out=H2T[:, j, :], in0=psum[bank][:], in1=xst[xi][:], op=ALU.add),
                        reads=[b_ps[bank], b_xst[xi]], pwrites=[b_H2T])
                    sc.op("act", lambda e, j=j, qi_=qi_: e.activation(out=sq[qi_][:], in_=H2T[:, j, :], func=AF.Square),
                          reads=[b_H2T], writes=[b_sq[qi_]])
                    sc.op("pe", lambda e, j=j, qi_=qi_: e.matmul(psum[7][:, :], onesb[:], sq[qi_][:], start=(j == 0), stop=(j == 31)),
                          reads=[b_sq[qi_], b_ident], writes=([b_ps[7]] if j == 0 else []), pwrites=([] if j == 0 else [b_ps[7]]))
                    sc.op("sp", lambda e, j=j: e.dma_start(out=H2s[j, :, t0:t0 + 512], in_=H2T[:, j, :]),
                          reads=[b_H2T], writes=[b_h2d[j]], dma_on=b_H2T)
                sumsq_rstd("a")
                for j in range(32):
                    sc.op("dve", lambda e, j=j: e.scalar_tensor_tensor(
                        out=U2T[:, j, :], in0=H2T[:, j, :], scalar=g2t[:, j:j + 1], in1=rstd[:], op0=ALU.mult, op1=ALU.mult),
                        reads=[b_H2T, b_g, b_rstd], pwrites=[b_U2T])
                sc.barrier()
                for f in range(128):
                    w_ = wload(wu[f], 32)
                    bank = nbank()
                    gi = f % 2

                    def mmU(e, w=w_, bank=bank):
                        ins = None
                        for kc in range(32):
                            ins = e.matmul(psum[bank][:, :], wsl[w][:, kc, :], U2T[:, kc, :], start=(kc == 0), stop=(kc == 31))
                        return ins
                    sc.op("pe", mmU, reads=[b_w[w_], b_U2T], writes=[b_ps[bank]])
                    sc.op("act", lambda e, gi=gi, bank=bank: e.activation(out=t1[gi][:], in_=psum[bank][:], func=AF.Relu),
                          reads=[b_ps[bank]], writes=[b_t1[gi]])
                    sc.op("dve", lambda e, gi=gi, f=f: e.tensor_tensor(out=hidv[:, f, :], in0=t1[gi][:], in1=t1[gi][:], op=ALU.mult),
                          reads=[b_t1[gi]], pwrites=[b_hid])
                for j in range(32):
                    bank = nbank(4)
                    xi = cnt["x"] % 2
                    cnt["x"] += 1
                    qi_ = cnt["q"] % 2
                    cnt["q"] += 1
                    hi_ = cnt["h"] % 2
                    cnt["h"] += 1
                    sc.op("sp", lambda e, j=j, xi=xi: e.dma_start(out=xst[xi][:], in_=H2s[j, :, t0:t0 + 512]),
                          reads=[b_h2d[j]], writes=[b_xst[xi]], dma_on=b_xst[xi])
                    for q in range(4):
                        w_ = wload(wd[j, q], 32)

                        def mmD(e, w=w_, bank=bank, q=q):
                            ins = None
                            for kc in range(32):
                                ins = e.matmul(psum[bank][:, :], wsl[w][:, kc, :], hidv[:, q * 32 + kc, :],
                                               start=(q == 0 and kc == 0), stop=(q == 3 and kc == 31))
                            return ins
                        sc.op("pe", mmD, reads=[b_w[w_], b_hid], writes=([b_ps[bank]] if q == 0 else []),
                              pwrites=([] if q == 0 else [b_ps[bank]]))
                    sc.op("dve", lambda e, xi=xi, bank=bank, hi_=hi_: e.tensor_tensor(
                        out=h3s[hi_][:], in0=psum[bank][:], in1=xst[xi][:], op=ALU.add),
                        reads=[b_ps[bank], b_xst[xi]], writes=[b_h3s[hi_]])
                    sc.op("act", lambda e, hi_=hi_, qi_=qi_: e.activation(out=sq[qi_][:], in_=h3s[hi_][:], func=AF.Square),
                          reads=[b_h3s[hi_]], writes=[b_sq[qi_]])
                    sc.op("pe", lambda e, j=j, qi_=qi_: e.matmul(psum[7][:, :], onesb[:], sq[qi_][:], start=(j == 0), stop=(j == 31)),
                          reads=[b_sq[qi_], b_ident], writes=([b_ps[7]] if j == 0 else []), pwrites=([] if j == 0 else [b_ps[7]]))
                    sc.op("sp", lambda e, j=j, hi_=hi_: e.dma_start(out=H2s[j, :, t0:t0 + 512], in_=h3s[hi_][:]),
                          reads=[b_h3s[hi_]], writes=[b_h2d[j]], dma_on=b_h3s[hi_])
                sumsq_rstd("b")
                sc.barrier()
                for j in range(32):
                    bank = nbank(4)
                    xi = cnt["x"] % 2
                    cnt["x"] += 1
                    gi = j % 2
                    sc.op("sp", lambda e, j=j, xi=xi: e.dma_start(out=xst[xi][:], in_=H2s[j, :, t0:t0 + 512]),
                          reads=[b_h2d[j]], writes=[b_xst[xi]], dma_on=b_xst[xi])
                    sc.op("dve", lambda e, j=j, xi=xi, gi=gi: e.scalar_tensor_tensor(
                        out=t1[gi][:], in0=xst[xi][:], scalar=gft[:, j:j + 1], in1=rstd[:], op0=ALU.mult, op1=ALU.mult),
                        reads=[b_xst[xi], b_g, b_rstd], writes=[b_t1[gi]])

                    def trO(e, gi=gi, bank=bank):
                        ins = None
                        for tb in range(4):
                            ins = e.transpose(out=psum[bank][:, tb * 128:(tb + 1) * 128],
                                              in_=t1[gi][:, tb * 128:(tb + 1) * 128], identity=identf[:])
                        return ins
                    sc.op("pe", trO, reads=[b_t1[gi], b_ident], writes=[b_ps[bank]])
                    ceng = "act" if j % 2 == 0 else "dve"
                    if ceng == "act":
                        sc.op("act", lambda e, j=j, bank=bank: e.copy(
                            out=orow[:, :, j * 128:(j + 1) * 128],
                            in_=psum[bank][:].rearrange("p (a b) -> p a b", b=128)),
                            reads=[b_ps[bank]], pwrites=[b_orow])
                    else:
                        sc.op("dve", lambda e, j=j, bank=bank: e.tensor_copy(
                            out=orow[:, :, j * 128:(j + 1) * 128],
                            in_=psum[bank][:].rearrange("p (a b) -> p a b", b=128)),
                            reads=[b_ps[bank]], pwrites=[b_orow])
                for tb in range(4):
                    sc.op("sp", lambda e, tb=tb: e.dma_start(out=out[t0 + tb * 128:t0 + (tb + 1) * 128, :], in_=orow[:, tb, :]),
                          reads=[b_orow], dma_on=b_orow)
                sc.barrier()
            for rt in range(4):
                row_tile(rt)

    es_mid = ExitStack()

    def sbm(name, shape, dt):
        return es_mid.enter_context(nc.sbuf_tensor(name, list(shape), dt))
    DT = sbm("DT", [16, TK], F32)
    WIT = sbm("WIT", [32, NOWN], F32)
    negD = sbm("negD", [128, 33, 16], F32)
    b_DT = sc.buf("DT")
    es_lf = ExitStack()
    LF = es_lf.enter_context(nc.sbuf_tensor("LF", [16, TK], F32))
    phase_A()
    if stop_after != "A":
        phase_B()
    if stop_after not in ("A", "B"):
        phase_B2()
        es_lf.close()
        phase_C0()
        phase_C(do_fox=True, do_dsa=(stop_after != "C1"))
    else:
        es_lf.close()
    es_mid.close()
    if stop_after == "all":
        phase_D()

    sc.barrier()
    sc.run()
    es.close()
    return nc


def _tile_w(W):
    K, N = W.shape
    return np.ascontiguousarray(W.reshape(K // 128, 128, N // 128, 128).transpose(2, 1, 0, 3))


def _t5_bucket_table():
    rel = np.arange(-4300, 4301, dtype=np.int64)
    half, max_exact = 16, 8
    ret = np.where(rel > 0, half, 0)
    n = np.abs(rel)
    nf = np.maximum(n, 1).astype(np.float32)
    large = max_exact + (np.log(nf / np.float32(max_exact)) / np.float32(math.log(128 / max_exact))
                         * np.float32(half - max_exact)).astype(np.int32)
    large = np.minimum(large, half - 1)
    return (ret + np.where(n < max_exact, n, large)).astype(np.int64)


def _host_prepare(inp):
    f32 = np.float32
    x = np.asarray(inp["x"], f32)
    w_in = np.asarray(inp["w_in"], f32)[0]
    maps = []
    z = np.zeros
    ki = w_in[:, O_KI:O_KI + 64]
    fa_chunk = np.concatenate([w_in[:, O_FA:O_FA + 16], z((D, 112), f32)], axis=1)
    wi_chunk = np.concatenate([w_in[:, O_WI:O_WI + 32], z((D, 96), f32)], axis=1)
    wk_cols = np.concatenate([w_in[:, O_KA:O_KA + 2048], w_in[:, O_KB:O_KB + 2048],
                              w_in[:, O_VA:O_VA + 2048], w_in[:, O_VB:O_VB + 2048],
                              ki, ki, fa_chunk], axis=1)
    wq_cols = np.concatenate([w_in[:, O_QA:O_QA + 2048], w_in[:, O_QB:O_QB + 2048],
                              w_in[:, O_QI:O_QI + 2048], wi_chunk,
                              w_in[:, O_GL:O_GL + 8192]], axis=1)
    wk = _tile_w(wk_cols)
    wq = _tile_w(wq_cols)
    wbf = _tile_w(np.asarray(inp["w_branch_fox"], f32)[0])
    wbd = _tile_w(np.asarray(inp["w_branch_dsa"], f32)[0])
    wo = _tile_w(np.asarray(inp["w_out"], f32)[0])
    wu = _tile_w(np.asarray(inp["w_up"], f32)[0])
    wdn = _tile_w(np.asarray(inp["w_down"], f32)[0])
    wd = np.ascontiguousarray(wdn.reshape(32, 128, 4, 32, 128).transpose(0, 2, 1, 3, 4))
    g_attn = np.asarray(inp["attn_norm_g"], f32).reshape(1, D)
    g_mlp = np.ascontiguousarray(np.asarray(inp["mlp_norm_g"], f32).reshape(32, 128).T)
    g_fin = np.ascontiguousarray(np.asarray(inp["final_norm_g"], f32).reshape(32, 128).T)
    negfb = np.ascontiguousarray(-np.asarray(inp["forget_bias"], f32).reshape(16, 1))
    rel_bias = np.asarray(inp["rel_bias"], f32)
    meta = np.asarray(inp["meta_tokens"], f32)
    bt = _t5_bucket_table()

    def bkt(rel):
        return bt[rel + 4300]
    cfar = np.ascontiguousarray(np.broadcast_to(rel_bias[15][None, :], (128, H))).astype(f32)
    identf = np.eye(128, dtype=f32)
    si = np.arange(128)
    per_par = {}
    for p in (0, 1):
        u = np.zeros((H, 128, 3, 128), f32)
        for jj, j in enumerate((-1, 0, 1)):
            dblk = j - p
            rel = dblk * 128 + si[:, None] - si[None, :]
            u[:, :, jj, :] = rel_bias[bkt(rel)].transpose(2, 0, 1)
        relm = si[:16, None] - (16 + 128 * p + si[None, :])
        um = np.ascontiguousarray(rel_bias[bkt(relm)].transpose(2, 0, 1)).astype(f32)
        fm = np.zeros((8, 128, 512), f32)
        for r in range(8):
            for k in range(4):
                d = r - (2 * k + p)
                blk = fm[r, :, k * 128:(k + 1) * 128]
                if d > 0:
                    blk[:] = NEG
                elif d == 0:
                    blk[:] = np.where(si[:, None] <= si[None, :], 0.0, NEG)
        ad = np.zeros((128, 256), f32)
        cm = np.where((si[None, :] // 64) <= (si[:, None] // 64), 0.0, NEG)
        if p == 0:
            ad[:, 0:128] = cm
            ad[:, 128:256] = NEG
        else:
            ad[:, 128:256] = cm
        selv = np.zeros((16, 2), f32)
        selv[:, p] = 1.0
        per_par[p] = dict(ubias=u, umeta=um, fmask=fm, adm=ad, sel=selv)
    for c in range(N_CORES):
        b, p = c // 2, c % 2
        xb = x[b]
        x_own = np.ascontiguousarray(xb.reshape(32, 128, D)[p::2].reshape(NOWN, D))
        m = dict(x_all=xb, x_own=x_own, meta=meta, g_attn=g_attn, g_mlp=g_mlp, g_fin=g_fin,
                 wk=wk, wq=wq, wbf=wbf, wbd=wbd, wo=wo, wu=wu, wd=wd, negfb=negfb,
                 identf=identf, cfar=cfar, ncfar=-cfar)
        m.update(per_par[p])
        maps.append(m)
    return maps


def _assemble(results):
    outp = np.empty((4, L, D), np.float32)
    for c in range(N_CORES):
        b, p = c // 2, c % 2
        o = np.asarray(results[c]["out"], np.float32).reshape(16, 128, D)
        outp[b].reshape(32, 128, D)[p::2] = o
    return outp


def kernel(**inputs):
    maps = _host_prepare(inputs)
    nc = build_program()
    res = run_bass_kernel_spmd(nc, maps, core_ids=list(range(N_CORES)))
    return _assemble(res.results)
```

```python
import math
from contextlib import ExitStack

import numpy as np
import concourse.bass as bass
import concourse.mybir as mybir
from concourse.bass_utils import run_bass_kernel_spmd

F32 = mybir.dt.float32
BF16 = mybir.dt.bfloat16
AF = mybir.ActivationFunctionType
ALU = mybir.AluOpType

D = 4096
L = 4096
NMETA = 16
TK = L + NMETA
NOWN = 2048
H = 16
HD = 128
NIDX = 32
DIDX = 64
KTOP = 256
DFF = 16384
EPS = 1e-6
NEG = -1.0e30
N_CORES = 8

O_QA, O_KA, O_VA, O_FA, O_QB, O_KB, O_VB, O_QI, O_KI, O_WI, O_GL = (
    0, 2048, 4096, 6144, 6160, 8208, 10256, 12304, 14352, 14416, 14448)


class _S:
    __slots__ = ("h", "count")

    def __init__(self, h):
        self.h = h
        self.count = 0


class Buf:
    __slots__ = ("name", "writers", "readers", "dsem")

    def __init__(self, name=""):
        self.name = name
        self.writers = {}
        self.readers = {}
        self.dsem = None


class _Eng:
    def __init__(self, name):
        self.name = name
        self.ops = []
        self.sem = None
        self.waited = {}


class Sched:
    def __init__(self, nc, es):
        self.nc = nc
        self.es = es
        self.engs = {n: _Eng(n) for n in ("pe", "act", "dve", "pool", "sp")}
        for n, e in self.engs.items():
            e.sem = _S(es.enter_context(nc.semaphore("e_" + n)))
        self.dsems = []
        self.nbuf = 0

    def buf(self, name=""):
        self.nbuf += 1
        return Buf(name)

    def bufs(self, n, name=""):
        return [self.buf(name + str(i)) for i in range(n)]

    def _dsem(self, b):
        if b.dsem is None:
            b.dsem = _S(self.es.enter_context(self.nc.semaphore("d%d_%s" % (len(self.dsems), b.name[:10]))))
            self.dsems.append(b.dsem)
        return b.dsem

    def _emit(self, eng, toks, fn, sem, inc):
        waits = []
        for S, val in toks.items():
            if eng.name == "pe" and S is eng.sem:
                continue
            if eng.waited.get(S, 0) >= val:
                continue
            eng.waited[S] = val
            waits.append((S, val))
        sem.count += inc
        val = sem.count

        def thunk(e, waits=waits, fn=fn, sem=sem, inc=inc):
            for S, v in waits:
                e.wait_ge(S.h, v)
            ins = fn(e)
            ins.then_inc(sem.h, inc)

        eng.ops.append(thunk)
        return (sem, val)

    @staticmethod
    def _merge(d, tok):
        S, v = tok
        if d.get(S, 0) < v:
            d[S] = v

    def op(self, engname, fn, reads=(), writes=(), pwrites=(), dma_on=None):
        eng = self.engs[engname]
        toks = {}
        for b in reads:
            for t in b.writers.items():
                self._merge(toks, t)
        for b in writes:
            for t in b.writers.items():
                self._merge(toks, t)
            for t in b.readers.items():
                self._merge(toks, t)
        for b in pwrites:
            for t in b.readers.items():
                self._merge(toks, t)
        if dma_on is not None:
            sem, inc = self._dsem(dma_on), 16
        else:
            sem, inc = eng.sem, 1
        tok = self._emit(eng, toks, fn, sem, inc)
        for b in writes:
            b.writers = {tok[0]: tok[1]}
            b.readers = {}
        for b in pwrites:
            if b.readers:
                b.writers = {tok[0]: tok[1]}
                b.readers = {}
            else:
                self._merge(b.writers, tok)
        for b in reads:
            self._merge(b.readers, tok)
        return tok

    def barrier(self):
        allt = {}
        for e in self.engs.values():
            if e.sem.count:
                allt[e.sem] = e.sem.count
        for S in self.dsems:
            if S.count:
                allt[S] = S.count
        for e in self.engs.values():
            waits = []
            for S, val in allt.items():
                if e.waited.get(S, 0) >= val:
                    continue
                e.waited[S] = val
                waits.append((S, val))
            if waits:
                def thunk(en, waits=waits):
                    for S, v in waits:
                        en.wait_ge(S.h, v)
                e.ops.append(thunk)

    def run(self):
        nc = self.nc
        with nc.Block() as block:
            @block.tensor
            def _(e):
                for t in self.engs["pe"].ops:
                    t(e)

            @block.scalar
            def _(e):
                for t in self.engs["act"].ops:
                    t(e)

            @block.vector
            def _(e):
                for t in self.engs["dve"].ops:
                    t(e)

            @block.gpsimd
            def _(e):
                for t in self.engs["pool"].ops:
                    t(e)

            @block.sync
            def _(e):
                for t in self.engs["sp"].ops:
                    t(e)


def build_program(stop_after="all", debug_outs=()):
    nc = bass.Bass("TRN2", target_bir_lowering=False)
    es = ExitStack()
    sc = Sched(nc, es)

    def din(name, shape, dt=F32):
        return nc.dram_tensor(name, list(shape), dt, kind="ExternalInput").ap()

    def dscr(name, shape, dt):
        kind = "ExternalOutput" if name in debug_outs else "Internal"
        return nc.dram_tensor(name, list(shape), dt, kind=kind).ap()

    x_all = din("x_all", [L, D])
    x_own = din("x_own", [NOWN, D])
    meta = din("meta", [NMETA, D])
    g_attn = din("g_attn", [1, D])
    g_mlp = din("g_mlp", [128, 32])
    g_fin = din("g_fin", [128, 32])
    NK = 66
    NQ = 113
    wk = din("wk", [NK, 128, 32, 128])
    wq = din("wq", [NQ, 128, 32, 128])
    wbf = din("wbf", [32, 128, 16, 128])
    wbd = din("wbd", [32, 128, 16, 128])
    wo = din("wo", [32, 128, 32, 128])
    wu = din("wu", [128, 128, 32, 128])
    wd = din("wd", [32, 4, 128, 32, 128])
    negfb = din("negfb", [16, 1])
    sel = din("sel", [16, 2])
    identf_d = din("identf", [128, 128])
    ubias = din("ubias", [H, 128, 3, 128])
    umeta = din("umeta", [H, 16, 128])
    cfar = din("cfar", [128, H])
    ncfar = din("ncfar", [128, H])
    fmask = din("fmask", [8, 128, 512])
    adm = din("adm", [128, 256])
    out = nc.dram_tensor("out", [NOWN, D], F32, kind="ExternalOutput").ap()

    UTa = dscr("UTa", [32, 128, TK], BF16)
    UTo = dscr("UTo", [32, 128, NOWN], BF16)
    XTo = dscr("XTo", [32, 128, NOWN], F32)
    KTf = dscr("KTf", [H, 128, TK], BF16)
    KTd = dscr("KTd", [H, 128, TK], BF16)
    Vf = dscr("Vf", [H, TK, 128], BF16)
    Vd = dscr("Vd", [H, TK, 128], BF16)
    KI2 = dscr("KI2", [128, TK], BF16)
    QTf = dscr("QTf", [H, 128, NOWN], BF16)
    QTd = dscr("QTd", [H, 128, NOWN], BF16)
    QI = dscr("QI", [16, 128, NOWN], BF16)
    Gs = dscr("Gs", [64, 128, NOWN], BF16)
    DTo = dscr("DTo", [H, NOWN], F32)
    BMs = dscr("BMs", [H, 128, 9, 512], BF16)
    BMm = dscr("BMm", [H, 16, 512], BF16)
    OTf = dscr("OTf", [H, 128, NOWN], BF16)
    OTd = dscr("OTd", [H, 128, NOWN], BF16)
    H2s = dscr("H2s", [32, 128, NOWN], F32)
    dbg = {}

    def sb(name, shape, dt):
        return es.enter_context(nc.sbuf_tensor(name, list(shape), dt))

    identf = sb("identf_s", [128, 128], F32)
    identb = sb("identb_s", [128, 128], BF16)
    onesb = sb("onesb", [128, 128], BF16)
    b_ident = sc.buf("ident")
    b_LF = sc.buf("LF")
    b_WIT = sc.buf("WIT")
    b_negD = sc.buf("negD")
    psum = [es.enter_context(nc.psum_tensor("ps%d" % i, [128, 512], F32)) for i in range(8)]
    b_ps = sc.bufs(8, "ps")

    def psb(i):
        return psum[i][:].bitcast(BF16)

    sc.op("sp", lambda e: e.dma_start(out=identf[:], in_=identf_d), writes=[b_ident], dma_on=b_ident)
    sc.op("dve", lambda e: e.tensor_copy(out=identb[:], in_=identf[:]), reads=[b_ident], pwrites=[b_ident])
    sc.op("dve", lambda e: e.memset(onesb[:], 1.0), pwrites=[b_ident])
    epsc = sb("epsc", [128, 1], F32)
    sc.op("dve", lambda e: e.memset(epsc[:], EPS), pwrites=[b_ident])

    def phase_A():
        with ExitStack() as ps_:
            def sbl(name, shape, dt):
                return ps_.enter_context(nc.sbuf_tensor(name, list(shape), dt))
            gb = sbl("gb", [128, D], F32)
            xt = [sbl("xt%d" % i, [128, D], F32) for i in range(3)]
            ub = [sbl("ub%d" % i, [128, D], BF16) for i in range(2)]
            junk = sbl("junk", [128, D], BF16)
            ss = [sbl("ss%d" % i, [128, 1], F32) for i in range(2)]
            rs = [sbl("rs%d" % i, [128, 1], F32) for i in range(2)]
            ut = [sbl("ut%d" % i, [128, 32, 128], BF16) for i in range(2)]
            xtt = [sbl("xtt%d" % i, [128, 32, 128], F32) for i in range(2)]
            b_gb = sc.buf("gb")
            b_xt = sc.bufs(3, "xt")
            b_ub = sc.bufs(2, "ub")
            b_junk = sc.buf("junk")
            b_ss = sc.bufs(2, "ss")
            b_rs = sc.bufs(2, "rs")
            b_ut = sc.bufs(2, "ut")
            b_xtt = sc.bufs(2, "xtt")
            sc.op("sp", lambda e: e.dma_start(out=gb[:], in_=g_attn.partition_broadcast(128)),
                  writes=[b_gb], dma_on=b_gb)
            blocks = [("all", i) for i in range(32)] + [("meta", 0)] + [("own", i) for i in range(16)]
            pcnt = [0]
            for bi, (kind, i) in enumerate(blocks):
                np_ = 16 if kind == "meta" else 128
                if kind == "all":
                    src = x_all[i * 128:(i + 1) * 128, :]
                    dstu = UTa[:, :, i * 128:(i + 1) * 128]
                elif kind == "meta":
                    src = meta
                    dstu = UTa[:, :, L:L + 16]
                else:
                    src = x_own[i * 128:(i + 1) * 128, :]
                    dstu = UTo[:, :, i * 128:(i + 1) * 128]
                X, bX = xt[bi % 3], b_xt[bi % 3]
                U, bU = ub[bi % 2], b_ub[bi % 2]
                SS, bSS = ss[bi % 2], b_ss[bi % 2]
                RS, bRS = rs[bi % 2], b_rs[bi % 2]
                UT, bUT = ut[bi % 2], b_ut[bi % 2]
                sc.op("sp", lambda e, X=X, src=src, np_=np_: e.dma_start(out=X[:np_, :], in_=src),
                      writes=[bX], dma_on=bX)
                sc.op("act", lambda e, X=X, SS=SS, np_=np_: e.activation(
                    out=junk[:np_, :], in_=X[:np_, :], func=AF.Square, accum_out=SS[:np_, :]),
                    reads=[bX], writes=[b_junk, bSS])
                sc.op("act", lambda e, SS=SS, RS=RS, np_=np_: e.activation(
                    out=RS[:np_, :], in_=SS[:np_, :], func=AF.Sqrt, bias=epsc[:np_, 0:1], scale=1.0 / D),
                    reads=[bSS, b_ident], writes=[bRS])
                sc.op("dve", lambda e, RS=RS, np_=np_: e.reciprocal(out=RS[:np_, :], in_=RS[:np_, :]),
                    reads=[bRS], writes=[bRS])
                sc.op("dve", lambda e, X=X, U=U, RS=RS, np_=np_: e.scalar_tensor_tensor(
                    out=U[:np_, :], in0=X[:np_, :], scalar=RS[:np_, 0:1], in1=gb[:np_, :],
                    op0=ALU.mult, op1=ALU.mult), reads=[bX, bRS, b_gb], writes=[bU])
                for q in range(8):
                    bank = pcnt[0] % 4
                    pcnt[0] += 1

                    def tr(e, U=U, q=q, bank=bank, np_=np_):
                        ins = None
                        for j in range(4):
                            kc = q * 4 + j
                            ins = e.transpose(out=psb(bank)[:, j * 128:j * 128 + np_],
                                              in_=U[:np_, kc * 128:(kc + 1) * 128],
                                              identity=identb[:np_, :np_])
                        return ins
                    sc.op("pe", tr, reads=[bU, b_ident], writes=[b_ps[bank]])
                    ceng = "act" if q % 2 == 0 else "dve"

                    def cp(e, UT=UT, q=q, bank=bank, np_=np_, ceng=ceng):
                        src_ = psb(bank)[:, 0:512].rearrange("p (a b) -> p a b", b=128)[:, :, :np_]
                        if ceng == "act":
                            return e.copy(out=UT[:, q * 4:q * 4 + 4, :np_], in_=src_)
                        return e.tensor_copy(out=UT[:, q * 4:q * 4 + 4, :np_], in_=src_)
                    sc.op(ceng, cp, reads=[b_ps[bank]], pwrites=[bUT])
                sc.op("sp", lambda e, UT=UT, dstu=dstu, np_=np_: e.dma_start(
                    out=dstu.rearrange("kc kp c -> kp kc c"), in_=UT[:, :, :np_]),
                    reads=[bUT], dma_on=bUT)
                if kind == "own":
                    XT, bXT = xtt[bi % 2], b_xtt[bi % 2]
                    for q in range(8):
                        bank = 4 + pcnt[0] % 4
                        pcnt[0] += 1

                        def trf(e, X=X, q=q, bank=bank):
                            ins = None
                            for j in range(4):
                                m = q * 4 + j
                                ins = e.transpose(out=psum[bank][:, j * 128:(j + 1) * 128],
                                                  in_=X[:, m * 128:(m + 1) * 128], identity=identf[:])
                            return ins
                        sc.op("pe", trf, reads=[bX, b_ident], writes=[b_ps[bank]])
                        ceng = "dve" if q % 2 == 0 else "act"

                        def cpf(e, XT=XT, q=q, bank=bank, ceng=ceng):
                            src_ = psum[bank][:].rearrange("p (a b) -> p a b", b=128)
                            if ceng == "act":
                                return e.copy(out=XT[:, q * 4:q * 4 + 4, :], in_=src_)
                            return e.tensor_copy(out=XT[:, q * 4:q * 4 + 4, :], in_=src_)
                        sc.op(ceng, cpf, reads=[b_ps[bank]], pwrites=[bXT])
                    sc.op("sp", lambda e, XT=XT, i=i: e.dma_start(
                        out=XTo[:, :, i * 128:(i + 1) * 128].rearrange("kc kp c -> kp kc c"), in_=XT[:]),
                        reads=[bXT], dma_on=bXT)
            sc.barrier()

    def phase_B():
        with ExitStack() as ps_:
            def sbl(name, shape, dt):
                return ps_.enter_context(nc.sbuf_tensor(name, list(shape), dt))
            uT = [sbl("uT%d" % i, [128, 32, 1040], BF16) for i in range(2)]
            wsl = [sbl("wsl%d" % i, [128, 32, 128], BF16) for i in range(3)]
            stg = [sbl("stg%d" % i, [128, 512], BF16) for i in range(4)]
            vstg = [sbl("vstg%d" % i, [128, 4, 128], BF16) for i in range(2)]
            ftmp = [sbl("ftmp%d" % i, [16, 512], F32) for i in range(2)]
            nfb = sbl("nfb", [16, 1], F32)
            b_uT = sc.bufs(2, "uT")
            b_w = sc.bufs(3, "wsl")
            b_stg = sc.bufs(4, "stg")
            b_vstg = sc.bufs(2, "vstg")
            b_ftmp = sc.bufs(2, "ftmp")
            b_nfb = sc.buf("nfb")
            sc.op("sp", lambda e: e.dma_start(out=nfb[:], in_=negfb), writes=[b_nfb], dma_on=b_nfb)

            kchunks = ([("kt", KTf, h) for h in range(H)] + [("kt", KTd, h) for h in range(H)] +
                       [("vt", Vf, h) for h in range(H)] + [("vt", Vd, h) for h in range(H)] +
                       [("k2", KI2, 0), ("fa", None, 0)])
            qchunks = ([("kt", QTf, h) for h in range(H)] + [("kt", QTd, h) for h in range(H)] +
                       [("kt", QI, h) for h in range(16)] + [("wi", None, 0)] +
                       [("g", Gs, j) for j in range(64)])
            ktiles = [[(r * 1024, 512), (r * 1024 + 512, 512)] for r in range(3)] + \
                     [[(3072, 512), (3584, 512), (4096, 16)]]
            qtiles = [[(r * 1024, 512), (r * 1024 + 512, 512)] for r in range(2)]
            cnt = {"w": 0, "ps": 0, "stg": 0, "v": 0, "f": 0, "u": 0}

            def run_side(UTsrc, tiles, chunks, wdram):
                for segs in tiles:
                    c_lo = segs[0][0]
                    ncol = segs[-1][0] + segs[-1][1] - c_lo
                    ui = cnt["u"] % 2
                    cnt["u"] += 1
                    sc.op("sp", lambda e, ui=ui, c_lo=c_lo, ncol=ncol: e.dma_start(
                        out=uT[ui][:, :, 0:ncol],
                        in_=UTsrc[:, :, c_lo:c_lo + ncol].rearrange("kc kp c -> kp kc c")),
                        writes=[b_uT[ui]], dma_on=b_uT[ui])
                    for ci, (kind, dst, idx) in enumerate(chunks):
                        wi_ = cnt["w"] % 3
                        cnt["w"] += 1
                        sc.op("pool", lambda e, wi_=wi_, ci=ci: e.dma_start(out=wsl[wi_][:], in_=wdram[ci]),
                              writes=[b_w[wi_]], dma_on=b_w[wi_])
                        for (c0, n) in segs:
                            bank = cnt["ps"] % 6
                            cnt["ps"] += 1
                            o0 = c0 - c_lo

                            def mm(e, wi_=wi_, ui=ui, bank=bank, o0=o0, n=n):
                                ins = None
                                for kc in range(32):
                                    ins = e.matmul(psum[bank][:, 0:n], wsl[wi_][:, kc, :],
                                                   uT[ui][:, kc, o0:o0 + n], start=(kc == 0), stop=(kc == 31))
                                return ins
                            sc.op("pe", mm, reads=[b_w[wi_], b_uT[ui]], writes=[b_ps[bank]])
                            if kind in ("kt", "k2", "g", "vt"):
                                si = cnt["stg"] % 4
                                cnt["stg"] += 1
                                fn = AF.Sigmoid if kind == "g" else AF.Copy
                                sc.op("act", lambda e, si=si, bank=bank, n=n, fn=fn: e.activation(
                                    out=stg[si][:, 0:n], in_=psum[bank][:, 0:n], func=fn),
                                    reads=[b_ps[bank]], writes=[b_stg[si]])
                                if kind == "vt":
                                    vi = cnt["v"] % 2
                                    cnt["v"] += 1
                                    tb = 6 + vi
                                    nb = (n + 127) // 128
                                    w_ = min(n, 128)

                                    def trv(e, si=si, tb=tb, nb=nb, w_=w_):
                                        ins = None
                                        for j in range(nb):
                                            ins = e.transpose(out=psb(tb)[:w_, j * 128:(j + 1) * 128],
                                                              in_=stg[si][:, j * 128:j * 128 + w_],
                                                              identity=identb[:])
                                        return ins
                                    sc.op("pe", trv, reads=[b_stg[si], b_ident], writes=[b_ps[tb]])
                                    sc.op("dve", lambda e, vi=vi, tb=tb, nb=nb, w_=w_: e.tensor_copy(
                                        out=vstg[vi][:w_, 0:nb, :],
                                        in_=psb(tb)[:w_, 0:nb * 128].rearrange("p (a b) -> p a b", b=128)),
                                        reads=[b_ps[tb]], writes=[b_vstg[vi]])
                                    if n >= 128:
                                        dv = dst[idx, c0:c0 + n, :].rearrange("(b s) d -> s b d", s=128)
                                        sc.op("sp", lambda e, vi=vi, dv=dv, nb=nb: e.dma_start(
                                            out=dv, in_=vstg[vi][:, 0:nb, :]), reads=[b_vstg[vi]], dma_on=b_vstg[vi])
                                    else:
                                        dv = dst[idx, c0:c0 + n, :]
                                        sc.op("sp", lambda e, vi=vi, dv=dv, n=n: e.dma_start(
                                            out=dv, in_=vstg[vi][:n, 0, :]), reads=[b_vstg[vi]], dma_on=b_vstg[vi])
                                else:
                                    dd = dst[c0:c0 + n] if False else (
                                        dst[:, c0:c0 + n] if kind == "k2" else dst[idx, :, c0:c0 + n])
                                    sc.op("sp", lambda e, si=si, dd=dd, n=n: e.dma_start(
                                        out=dd, in_=stg[si][:, 0:n]), reads=[b_stg[si]], dma_on=b_stg[si])
                            elif kind == "fa":
                                fi = cnt["f"] % 2
                                cnt["f"] += 1
                                sc.op("act", lambda e, fi=fi, bank=bank, n=n: e.activation(
                                    out=ftmp[fi][:, 0:n], in_=psum[bank][0:16, 0:n], func=AF.Exp,
                                    bias=nfb[:, 0:1], scale=-1.0),
                                    reads=[b_ps[bank], b_nfb], writes=[b_ftmp[fi]])
                                sc.op("act", lambda e, fi=fi, n=n: e.activation(
                                    out=ftmp[fi][:, 0:n], in_=ftmp[fi][:, 0:n], func=AF.Ln, bias=1.0),
                                    reads=[b_ftmp[fi]], writes=[b_ftmp[fi]])
                                sc.op("dve", lambda e, fi=fi, c0=c0, n=n: e.tensor_scalar(
                                    out=LF[:, c0:c0 + n], in0=ftmp[fi][:, 0:n], scalar1=-1.0, scalar2=None,
                                    op0=ALU.mult), reads=[b_ftmp[fi]], pwrites=[b_LF])
                            elif kind == "wi":
                                sc.op("act", lambda e, bank=bank, c0=c0, n=n: e.copy(
                                    out=WIT[:, c0:c0 + n], in_=psum[bank][0:32, 0:n]),
                                    reads=[b_ps[bank]], pwrites=[b_WIT])
            run_side(UTa, ktiles, kchunks, wk)
            run_side(UTo, qtiles, qchunks, wq)
            sc.barrier()


    SCALE = float(HD) ** -0.5
    CW = float(DIDX) ** -0.5 * float(NIDX) ** -0.5

    def phase_B2():
        with ExitStack() as ps_:
            def sbl(name, shape, dt):
                return ps_.enter_context(nc.sbuf_tensor(name, list(shape), dt))
            ones16 = sbl("ones16", [16, L], F32)
            dto = sbl("dto", [16, NOWN], F32)
            tmpo = [sbl("tmpo%d" % i, [16, 128], F32) for i in range(2)]
            selt = sbl("selt", [16, 2], F32)
            b_ones = sc.buf("ones16")
            b_dto = sc.buf("dto")
            b_tmpo = sc.bufs(2, "tmpo")
            b_sel = sc.buf("sel")
            sc.op("pool", lambda e: e.memset(ones16[:], 1.0), writes=[b_ones])
            sc.op("sp", lambda e: e.dma_start(out=selt[:], in_=sel), writes=[b_sel], dma_on=b_sel)
            sc.op("dve", lambda e: e.tensor_tensor_scan(
                out=DT[:, L:TK], data0=ones16[:, 0:16], data1=LF[:, L:TK], initial=0.0,
                op0=ALU.mult, op1=ALU.add), reads=[b_LF, b_ones], writes=[b_DT])
            sc.op("dve", lambda e: e.tensor_tensor_scan(
                out=DT[:, 0:L], data0=ones16[:, 0:L], data1=LF[:, 0:L], initial=DT[:, TK - 1:TK],
                op0=ALU.mult, op1=ALU.add), reads=[b_LF, b_ones, b_DT], pwrites=[b_DT])
            for q in range(8):
                bank = q % 4

                def trd(e, q=q, bank=bank):
                    ins = None
                    for j in range(4):
                        blk = q * 4 + j
                        ins = e.transpose(out=psum[bank][:, j * 16:(j + 1) * 16],
                                          in_=DT[:, blk * 128:(blk + 1) * 128], identity=identf[:16, :16])
                    return ins
                sc.op("pe", trd, reads=[b_DT, b_ident], writes=[b_ps[bank]])
                sc.op("dve", lambda e, q=q, bank=bank: e.tensor_scalar(
                    out=negD[:, q * 4:q * 4 + 4, :],
                    in0=psum[bank][:, 0:64].rearrange("p (a b) -> p a b", b=16),
                    scalar1=-1.0, scalar2=None, op0=ALU.mult), reads=[b_ps[bank]], pwrites=[b_negD])
            sc.op("pe", lambda e: e.transpose(out=psum[4][:16, 0:16], in_=DT[:, L:TK], identity=identf[:16, :16]),
                  reads=[b_DT, b_ident], writes=[b_ps[4]])
            sc.op("dve", lambda e: e.tensor_scalar(out=negD[:16, 32, :], in0=psum[4][:16, 0:16],
                                                   scalar1=-1.0, scalar2=None, op0=ALU.mult),
                  reads=[b_ps[4]], pwrites=[b_negD])
            for i in range(16):
                t_, bt_ = tmpo[i % 2], b_tmpo[i % 2]
                sc.op("dve", lambda e, i=i, t_=t_: e.tensor_scalar(
                    out=t_[:], in0=DT[:, (2 * i) * 128:(2 * i + 1) * 128], scalar1=selt[:, 0:1], scalar2=None,
                    op0=ALU.mult), reads=[b_DT, b_sel], writes=[bt_])
                sc.op("dve", lambda e, i=i, t_=t_: e.scalar_tensor_tensor(
                    out=dto[:, i * 128:(i + 1) * 128], in0=DT[:, (2 * i + 1) * 128:(2 * i + 2) * 128],
                    scalar=selt[:, 1:2], in1=t_[:], op0=ALU.mult, op1=ALU.add),
                    reads=[b_DT, b_sel, bt_], pwrites=[b_dto])
            sc.op("sp", lambda e: e.dma_start(out=DTo, in_=dto[:]), reads=[b_dto], dma_on=b_dto)
            sc.barrier()

    def phase_C0():
        with ExitStack() as ps_:
            def sbl(name, shape, dt):
                return ps_.enter_context(nc.sbuf_tensor(name, list(shape), dt))
            ncf = sbl("ncf0", [128, H], F32)
            utl = [sbl("utl%d" % i, [128, 3, 128], F32) for i in range(2)]
            uml = [sbl("uml%d" % i, [16, 128], F32) for i in range(2)]
            bm = [sbl("bm0%d" % i, [128, 9, 512], BF16) for i in range(2)]
            bmm = [sbl("bmm0%d" % i, [16, 512], BF16) for i in range(2)]
            b_ncf = sc.buf("ncf")
            b_utl = sc.bufs(2, "utl")
            b_uml = sc.bufs(2, "uml")
            b_bm = sc.bufs(2, "bm")
            b_bmm = sc.bufs(2, "bmm")
            sc.op("sp", lambda e: e.dma_start(out=ncf[:], in_=ncfar), writes=[b_ncf], dma_on=b_ncf)
            for h in range(H):
                i2 = h % 2
                sc.op("sp", lambda e, h=h, i2=i2: e.dma_start(out=utl[i2][:], in_=ubias[h]),
                      writes=[b_utl[i2]], dma_on=b_utl[i2])
                sc.op("sp", lambda e, h=h, i2=i2: e.dma_start(out=uml[i2][:], in_=umeta[h]),
                      writes=[b_uml[i2]], dma_on=b_uml[i2])
                sc.op("dve", lambda e, i2=i2: e.memset(bm[i2][:], 1.0), writes=[b_bm[i2]])
                sc.op("dve", lambda e, i2=i2: e.memset(bmm[i2][:], 1.0), writes=[b_bmm[i2]])

                def fill(e, h=h, i2=i2):
                    ins = None
                    for k in range(4):
                        for jj in range(3):
                            slot = 2 * k + (jj - 1) + 1
                            if slot < 0 or slot > 8:
                                continue
                            ins = e.activation(out=bm[i2][:, slot, k * 128:(k + 1) * 128], in_=utl[i2][:, jj, :],
                                               func=AF.Exp, bias=ncf[:, h:h + 1], scale=1.0)
                    return ins
                sc.op("act", fill, reads=[b_utl[i2], b_ncf], pwrites=[b_bm[i2]])
                sc.op("act", lambda e, h=h, i2=i2: e.activation(
                    out=bmm[i2][:, 0:128], in_=uml[i2][:], func=AF.Exp, bias=ncf[:16, h:h + 1], scale=1.0),
                    reads=[b_uml[i2], b_ncf], pwrites=[b_bmm[i2]])
                sc.op("sp", lambda e, h=h, i2=i2: e.dma_start(out=BMs[h], in_=bm[i2][:]),
                      reads=[b_bm[i2]], dma_on=b_bm[i2])
                sc.op("sp", lambda e, h=h, i2=i2: e.dma_start(out=BMm[h], in_=bmm[i2][:]),
                      reads=[b_bmm[i2]], dma_on=b_bmm[i2])
            sc.barrier()

    def phase_C(do_fox=True, do_dsa=True):
        with ExitStack() as ps_:
            def sbl(name, shape, dt):
                return ps_.enter_context(nc.sbuf_tensor(name, list(shape), dt))
            kt = [sbl("kt%d" % i, [128, TK], BF16) for i in range(2)]
            vv = [sbl("vv%d" % i, [128, 33, 128], BF16) for i in range(2)]
            qt = [sbl("qt%d" % i, [128, 512], BF16) for i in range(2)]
            tmp = [sbl("tmp%d" % i, [128, 512], F32) for i in range(4)]
            pt = [sbl("pt%d" % i, [128, 512], BF16) for i in range(4)]
            rinv = [sbl("rinv%d" % i, [128, 512], F32) for i in range(2)]
            ost = [sbl("ost%d" % i, [128, 512], BF16) for i in range(2)]
            b_kt = sc.bufs(2, "kt")
            b_vv = sc.bufs(2, "vv")
            b_qt = sc.bufs(2, "qt")
            b_tmp = sc.bufs(4, "tmp")
            b_pt = sc.bufs(4, "pt")
            b_rinv = sc.bufs(2, "rinv")
            b_ost = sc.bufs(2, "ost")
            cnt = {"hd": 0, "s": 0, "t": 0, "p": 0, "e": 0, "e2": 0, "ip": 0, "rr": 0}

            def load_head(KT, V, QT, h, g, hb):
                nfb = 8 * g + 8
                sc.op("sp", lambda e: e.dma_start(out=kt[hb][:, 0:nfb * 128], in_=KT[h, :, 0:nfb * 128]),
                      writes=[b_kt[hb]], dma_on=b_kt[hb])
                sc.op("sp", lambda e: e.dma_start(out=kt[hb][:, L:TK], in_=KT[h, :, L:TK]),
                      pwrites=[b_kt[hb]], dma_on=b_kt[hb])
                sc.op("sp", lambda e: e.dma_start(
                    out=vv[hb][:, 0:nfb, :], in_=V[h, 0:nfb * 128, :].rearrange("(b s) d -> s b d", s=128)),
                    writes=[b_vv[hb]], dma_on=b_vv[hb])
                sc.op("sp", lambda e: e.dma_start(out=vv[hb][:16, 32, :], in_=V[h, L:TK, :]),
                      pwrites=[b_vv[hb]], dma_on=b_vv[hb])
                sc.op("sp", lambda e: e.dma_start(out=qt[hb][:], in_=QT[h, :, g * 512:(g + 1) * 512]),
                      writes=[b_qt[hb]], dma_on=b_qt[hb])

            def attn_head(g, hb, ew, OT, h, extra=None):
                nfb = 8 * g + 8
                blocks = [("m", 32)] + [("f", i) for i in range(nfb)]
                N = len(blocks)
                ob = 3 + 2 * (cnt["hd"] % 2)
                cnt["hd"] += 1
                sbanks = []

                def mm1(n):
                    kind, blk = blocks[n]
                    np_ = 16 if kind == "m" else 128
                    c0 = L if kind == "m" else blk * 128
                    sbank = (0, 1, 2, 7)[cnt["s"] % 4]
                    cnt["s"] += 1
                    sbanks.append(sbank)
                    ex = extra(kind, blk) if extra is not None else None
                    if ex is None:
                        sc.op("pe", lambda e: e.matmul(psum[sbank][:np_, :], kt[hb][:, c0:c0 + np_], qt[hb][:],
                                                       start=True, stop=True),
                              reads=[b_kt[hb], b_qt[hb]], writes=[b_ps[sbank]])
                    else:
                        def mm1x(e):
                            e.matmul(psum[sbank][:np_, :], kt[hb][:, c0:c0 + np_], qt[hb][:], start=True, stop=False)
                            return e.matmul(psum[sbank][:np_, :], identb[:], ex[0], start=False, stop=True)
                        sc.op("pe", mm1x, reads=[b_kt[hb], b_qt[hb], b_ident, ex[1]], writes=[b_ps[sbank]])
                LOOK = 3
                for n in range(min(LOOK, N)):
                    mm1(n)
                for n in range(N):
                    if n + LOOK < N:
                        mm1(n + LOOK)
                    kind, blk = blocks[n]
                    np_ = 16 if kind == "m" else 128
                    pi = ew(n, kind, blk, np_, sbanks[n])

                    def mm23(e, np_=np_, blk=blk, pi=pi, n=n):
                        e.matmul(psum[ob][:, :], vv[hb][:np_, blk, :], pt[pi][:np_, :],
                                 start=(n == 0), stop=(n == N - 1))
                        return e.matmul(psum[ob + 1][:, :], onesb[:np_, :], pt[pi][:np_, :],
                                        start=(n == 0), stop=(n == N - 1))
                    sc.op("pe", mm23, reads=[b_vv[hb], b_pt[pi], b_ident],
                          writes=([b_ps[ob], b_ps[ob + 1]] if n == 0 else []),
                          pwrites=([] if n == 0 else [b_ps[ob], b_ps[ob + 1]]))
                ri = cnt["hd"] % 2
                sc.op("dve", lambda e: e.reciprocal(out=rinv[ri][:], in_=psum[ob + 1][:]),
                      reads=[b_ps[ob + 1]], writes=[b_rinv[ri]])
                sc.op("dve", lambda e: e.tensor_tensor(out=ost[ri][:], in0=psum[ob][:], in1=rinv[ri][:], op=ALU.mult),
                      reads=[b_ps[ob], b_rinv[ri]], writes=[b_ost[ri]])
                sc.op("sp", lambda e: e.dma_start(out=OT[h, :, g * 512:(g + 1) * 512], in_=ost[ri][:]),
                      reads=[b_ost[ri]], dma_on=b_ost[ri])

            if do_fox:
                with ExitStack() as pf_:
                    def sbf(name, shape, dt):
                        return pf_.enter_context(nc.sbuf_tensor(name, list(shape), dt))
                    fmb = sbf("fmb", [128, 8, 512], BF16)
                    dtb = [sbf("dtb%d" % i, [128, 512], F32) for i in range(2)]
                    b_fmt = sc.buf("fmb")
                    b_dtb = sc.bufs(2, "dtb")
                    fmt = sbf("fmt", [128, 8, 512], F32)
                    b_fmt32 = sc.buf("fmt32")
                    sc.op("sp", lambda e: e.dma_start(out=fmt[:], in_=fmask.rearrange("r s t -> s r t")),
                          writes=[b_fmt32], dma_on=b_fmt32)
                    sc.op("dve", lambda e: e.tensor_copy(out=fmb[:], in_=fmt[:]), reads=[b_fmt32], writes=[b_fmt])

                    def load_fox(g, h, hb):
                        load_head(KTf, Vf, QTf, h, g, hb)
                        sc.op("sp", lambda e: e.dma_start(
                            out=dtb[hb][:], in_=DTo[h:h + 1, g * 512:(g + 1) * 512].partition_broadcast(128)),
                            writes=[b_dtb[hb]], dma_on=b_dtb[hb])
                    seq = [(g, h) for g in range(4) for h in range(H)]
                    load_fox(seq[0][0], seq[0][1], 0)
                    for qi_, (g, h) in enumerate(seq):
                        hb = qi_ % 2
                        if qi_ + 1 < len(seq):
                            load_fox(seq[qi_ + 1][0], seq[qi_ + 1][1], 1 - hb)

                        def ew(n, kind, blk, np_, sbank, g=g, h=h, hb=hb):
                            ti = cnt["t"] % 4
                            cnt["t"] += 1
                            pi = cnt["p"] % 4
                            cnt["p"] += 1
                            src, bsrc = dtb[hb][:np_, :], b_dtb[hb]
                            sc.op("dve", lambda e: e.scalar_tensor_tensor(
                                out=tmp[ti][:np_, :], in0=psum[sbank][:np_, :], scalar=SCALE, in1=src,
                                op0=ALU.mult, op1=ALU.add), reads=[b_ps[sbank], bsrc], writes=[b_tmp[ti]])
                            sc.op("act", lambda e: e.activation(
                                out=pt[pi][:np_, :], in_=tmp[ti][:np_, :], func=AF.Exp,
                                bias=negD[:np_, blk, h:h + 1], scale=1.0),
                                reads=[b_tmp[ti], b_negD], writes=[b_pt[pi]])
                            return pi
                        def extra(kind, blk, g=g):
                            if kind == "f" and blk >= 8 * g:
                                return (fmb[:, blk - 8 * g, :], b_fmt)
                            return None
                        attn_head(g, hb, ew, OTf, h, extra)
                    sc.barrier()

            if do_dsa:
                with ExitStack() as pd_:
                    def sbd(name, shape, dt):
                        return pd_.enter_context(nc.sbuf_tensor(name, list(shape), dt))
                    ki2 = sbd("ki2", [128, TK], BF16)
                    qi = [sbd("qi%d" % i, [128, 16, 128], BF16) for i in range(2)]
                    wa = [sbd("wa%d" % i, [128, 32], F32) for i in range(2)]
                    ws = [sbd("ws%d" % i, [128, 32], F32) for i in range(2)]
                    sct = sbd("sct", [128, TK], F32)
                    NIT = 24
                    p2 = sbd("p2", [128, NIT], F32)
                    Wt = sbd("Wt", [128, NIT], F32)
                    lo_t = sbd("lo_t", [128, 1], F32)
                    mid = sbd("mid", [128, 1], F32)
                    cntt = sbd("cntt", [128, 1], F32)
                    gt = sbd("gt", [128, 1], F32)
                    b_p2 = sc.buf("p2")
                    b_W = sc.buf("W")
                    b_lo = sc.buf("lo")
                    b_mid = sc.buf("mid")
                    b_cnt = sc.buf("cnt")
                    b_g = sc.buf("g")

                    def mkp2(e):
                        ins = None
                        for it in range(NIT):
                            ins = e.memset(p2[:, it:it + 1], 2.0 ** -(it + 1))
                        return ins
                    sc.op("dve", mkp2, writes=[b_p2])
                    rr = [sbd("rr%d" % i, [128, 512], F32) for i in range(4)]
                    m8 = sbd("m8", [128, 8], F32)
                    thr = sbd("thr", [128, 1], F32)
                    m01 = sbd("m01", [128, TK], BF16)
                    maskT = sbd("maskT", [128, 33, 512], BF16)
                    admt = sbd("admt", [128, 256], F32)
                    cf = sbd("cf", [128, H], F32)
                    et = [sbd("et%d" % i, [128, 512], BF16) for i in range(4)]
                    et2 = [sbd("et2%d" % i, [128, 512], BF16) for i in range(2)]
                    bm = [sbd("bm%d" % i, [128, 9, 512], BF16) for i in range(2)]
                    bmm = [sbd("bmm%d" % i, [16, 512], BF16) for i in range(2)]
                    b_ki2 = sc.buf("ki2")
                    b_qi = sc.bufs(2, "qi")
                    b_wa = sc.bufs(2, "wa")
                    b_ws = sc.bufs(2, "ws")
                    b_scA = sc.bufs(10, "scA")
                    b_scB = sc.bufs(10, "scB")
                    b_wk = sc.buf("wkk")
                    b_rr = sc.bufs(4, "rr")
                    b_m8 = sc.buf("m8")
                    b_thr = sc.buf("thr")
                    b_m01 = sc.buf("m01")
                    b_maskT = sc.buf("maskT")
                    b_adm = sc.buf("adm")
                    b_cf = sc.buf("cf")
                    b_et = sc.bufs(4, "et")
                    b_et2 = sc.bufs(2, "et2")
                    b_bm = sc.bufs(2, "bm")
                    b_bmm = sc.bufs(2, "bmm")
                    sc.op("sp", lambda e: e.dma_start(out=ki2[:], in_=KI2), writes=[b_ki2], dma_on=b_ki2)
                    sc.op("sp", lambda e: e.dma_start(out=admt[:], in_=adm), writes=[b_adm], dma_on=b_adm)
                    sc.op("sp", lambda e: e.dma_start(out=cf[:], in_=cfar), writes=[b_cf], dma_on=b_cf)

                    def load_dsa(g, h, hb):
                        load_head(KTd, Vd, QTd, h, g, hb)
                        sc.op("sp", lambda e: e.dma_start(out=bm[hb][:], in_=BMs[h]), writes=[b_bm[hb]], dma_on=b_bm[hb])
                        if g == 0:
                            sc.op("sp", lambda e: e.dma_start(out=bmm[hb][:], in_=BMm[h]),
                                  writes=[b_bmm[hb]], dma_on=b_bmm[hb])

                    def indexer_block(g, k):
                        i = 4 * g + k
                        nkb = 2 * i + 2
                        ncols = nkb * 128
                        ntot = ncols + 16
                        q2 = i % 2
                        sc.op("sp", lambda e: e.dma_start(
                            out=qi[q2][:], in_=QI[:, :, i * 128:(i + 1) * 128].rearrange("c p t -> p c t")),
                            writes=[b_qi[q2]], dma_on=b_qi[q2])
                        wb = 6 + (i % 2)
                        sc.op("pe", lambda e: e.transpose(out=psum[wb][:, 0:32], in_=WIT[:, i * 128:(i + 1) * 128],
                                                          identity=identf[:32, :32]),
                              reads=[b_WIT, b_ident], writes=[b_ps[wb]])
                        sc.op("act", lambda e: e.activation(out=wa[q2][:], in_=psum[wb][:, 0:32], func=AF.Abs, scale=CW),
                              reads=[b_ps[wb]], writes=[b_wa[q2]])
                        sc.op("act", lambda e: e.sign(out=ws[q2][:], in_=psum[wb][:, 0:32]),
                              reads=[b_ps[wb]], writes=[b_ws[q2]])
                        tiles = []
                        c0 = 0
                        while c0 < ncols:
                            n = min(512, ncols - c0)
                            tiles.append((c0, n, c0))
                            c0 += n
                        tiles.append((L, 16, ncols))
                        for ti_, (c0, n, d0) in enumerate(tiles):
                            bA, bB = b_scA[ti_], b_scB[ti_]
                            for hi in range(NIDX):
                                ch, half = hi // 2, hi % 2
                                bank = cnt["ip"] % 6
                                cnt["ip"] += 1
                                ri = cnt["rr"] % 4
                                cnt["rr"] += 1
                                sc.op("pe", lambda e, ch=ch, half=half, bank=bank, c0=c0, n=n: e.matmul(
                                    psum[bank][:, 0:n], qi[q2][64 * half:64 * half + 64, ch, :],
                                    ki2[64 * half:64 * half + 64, c0:c0 + n], start=True, stop=True),
                                    reads=[b_qi[q2], b_ki2], writes=[b_ps[bank]])
                                sc.op("act", lambda e, hi=hi, bank=bank, ri=ri, n=n: e.activation(
                                    out=rr[ri][:, 0:n], in_=psum[bank][:, 0:n], func=AF.Relu,
                                    scale=wa[q2][:, hi:hi + 1]), reads=[b_ps[bank], b_wa[q2]], writes=[b_rr[ri]])
                                if hi == 0:
                                    sc.op("dve", lambda e, hi=hi, ri=ri, n=n, d0=d0: e.tensor_scalar(
                                        out=sct[:, d0:d0 + n], in0=rr[ri][:, 0:n], scalar1=ws[q2][:, hi:hi + 1],
                                        scalar2=None, op0=ALU.mult), reads=[b_rr[ri], b_ws[q2]], writes=[bA])
                                else:
                                    sc.op("dve", lambda e, hi=hi, ri=ri, n=n, d0=d0: e.scalar_tensor_tensor(
                                        out=sct[:, d0:d0 + n], in0=rr[ri][:, 0:n], scalar=ws[q2][:, hi:hi + 1],
                                        in1=sct[:, d0:d0 + n], op0=ALU.mult, op1=ALU.add),
                                        reads=[b_rr[ri], b_ws[q2]], writes=[bA])
                        lastt = [b_scA[t] for t in range(len(tiles))]
                        sc.op("dve", lambda e: e.tensor_reduce(out=lo_t[:], in_=sct[:, 0:ntot], axis=mybir.AxisListType.X,
                                                               op=ALU.min), reads=lastt, writes=[b_lo])
                        sc.op("dve", lambda e: e.tensor_tensor(
                            out=sct[:, ncols - 256:ncols], in0=sct[:, ncols - 256:ncols], in1=admt[:], op=ALU.add),
                            reads=[b_adm], writes=lastt)
                        allsc = [b_scA[t] for t in range(len(tiles))]
                        sc.op("dve", lambda e: e.max(out=m8[:], in_=sct[:, 0:ntot]), reads=allsc, writes=[b_m8])
                        sc.op("dve", lambda e: e.tensor_scalar(out=thr[:], in0=m8[:, 0:1], scalar1=lo_t[:, 0:1], scalar2=1.0001,
                                                               op0=ALU.subtract, op1=ALU.mult),
                              reads=[b_m8, b_lo], writes=[b_thr])
                        sc.op("dve", lambda e: e.tensor_scalar(out=Wt[:], in0=p2[:], scalar1=thr[:, 0:1], scalar2=None,
                                                               op0=ALU.mult), reads=[b_thr, b_p2], writes=[b_W])
                        sc.op("dve", lambda e: e.tensor_tensor(out=mid[:], in0=lo_t[:], in1=Wt[:, 0:1], op=ALU.add),
                              reads=[b_lo, b_W], writes=[b_mid])
                        yield
                        for it in range(NIT):
                            sc.op("dve", lambda e: e.tensor_scalar(
                                out=m01[:, 0:ntot], in0=sct[:, 0:ntot], scalar1=mid[:, 0:1], scalar2=0.0,
                                op0=ALU.is_ge, op1=ALU.add, accum_out=cntt[:, 0:1]),
                                reads=allsc + [b_mid], writes=[b_m01, b_cnt])
                            sc.op("dve", lambda e, it=it: e.tensor_scalar(
                                out=gt[:], in0=cntt[:], scalar1=KTOP - 0.5, scalar2=Wt[:, it:it + 1],
                                op0=ALU.is_ge, op1=ALU.mult), reads=[b_cnt, b_W], writes=[b_g])
                            sc.op("dve", lambda e: e.tensor_tensor(out=lo_t[:], in0=lo_t[:], in1=gt[:], op=ALU.add),
                                  reads=[b_g], writes=[b_lo])
                            if it + 1 < NIT:
                                sc.op("dve", lambda e, it=it: e.tensor_tensor(out=mid[:], in0=lo_t[:], in1=Wt[:, it + 1:it + 2],
                                                                              op=ALU.add),
                                      reads=[b_lo, b_W], writes=[b_mid])
                            yield
                        sc.op("dve", lambda e: e.tensor_scalar(out=m01[:, 0:ntot], in0=sct[:, 0:ntot],
                                                               scalar1=lo_t[:, 0:1], scalar2=None, op0=ALU.is_ge),
                              reads=[b_lo] + allsc, writes=[b_m01])
                        j0 = 0
                        while j0 < nkb:
                            nb = min(4, nkb - j0)
                            bank = cnt["ip"] % 6
                            cnt["ip"] += 1

                            def trm(e, j0=j0, nb=nb, bank=bank):
                                ins = None
                                for j in range(nb):
                                    ins = e.transpose(out=psb(bank)[:, j * 128:(j + 1) * 128],
                                                      in_=m01[:, (j0 + j) * 128:(j0 + j + 1) * 128], identity=identb[:])
                                return ins
                            sc.op("pe", trm, reads=[b_m01, b_ident], writes=[b_ps[bank]])
                            ceng = "act" if (j0 // 4) % 2 == 0 else "pool"
                            if ceng == "act":
                                sc.op("act", lambda e, j0=j0, nb=nb, bank=bank: e.copy(
                                    out=maskT[:, j0:j0 + nb, k * 128:(k + 1) * 128],
                                    in_=psb(bank)[:, 0:nb * 128].rearrange("p (a b) -> p a b", b=128)),
                                    reads=[b_ps[bank]], pwrites=[b_maskT])
                            else:
                                sc.op("dve", lambda e, j0=j0, nb=nb, bank=bank: e.tensor_copy(
                                    out=maskT[:, j0:j0 + nb, k * 128:(k + 1) * 128],
                                    in_=psb(bank)[:, 0:nb * 128].rearrange("p (a b) -> p a b", b=128)),
                                    reads=[b_ps[bank]], pwrites=[b_maskT])
                            j0 += nb
                        bank = cnt["ip"] % 6
                        cnt["ip"] += 1
                        sc.op("pe", lambda e, bank=bank: e.transpose(out=psb(bank)[:16, 0:128], in_=m01[:, ncols:ncols + 16],
                                                                     identity=identb[:]),
                              reads=[b_m01, b_ident], writes=[b_ps[bank]])
                        sc.op("act", lambda e, bank=bank: e.copy(out=maskT[:16, 32, k * 128:(k + 1) * 128],
                                                                 in_=psb(bank)[:16, 0:128]),
                              reads=[b_ps[bank]], pwrites=[b_maskT])

                    for g in range(4):
                        sc.op("dve", lambda e: e.memset(maskT[:], 0.0), writes=[b_maskT])
                        for k in range(4):
                            for _ in indexer_block(g, k):
                                pass
                        load_dsa(g, 0, 0)
                        for h in range(H):
                            hb = h % 2
                            if h + 1 < H:
                                load_dsa(g, h + 1, 1 - hb)

                            def ew(n, kind, blk, np_, sbank, g=g, h=h, hb=hb):
                                ei = cnt["e"] % 4
                                cnt["e"] += 1
                                pi = cnt["p"] % 4
                                cnt["p"] += 1
                                sc.op("act", lambda e: e.activation(
                                    out=et[ei][:np_, :], in_=psum[sbank][:np_, :], func=AF.Exp,
                                    bias=cf[:np_, h:h + 1], scale=SCALE),
                                    reads=[b_ps[sbank], b_cf], writes=[b_et[ei]])
                                srcE, bsrcE = et[ei], b_et[ei]
                                near = None
                                if kind == "f" and blk >= 8 * g - 1:
                                    near = (bm[hb][:np_, blk - (8 * g - 1), :], b_bm[hb])
                                elif kind == "m" and g == 0:
                                    near = (bmm[hb][:np_, :], b_bmm[hb])
                                if near is not None:
                                    e2 = cnt["e2"] % 2
                                    cnt["e2"] += 1
                                    sc.op("dve", lambda e: e.tensor_tensor(
                                        out=et2[e2][:np_, :], in0=et[ei][:np_, :], in1=near[0], op=ALU.mult),
                                        reads=[b_et[ei], near[1]], writes=[b_et2[e2]])
                                    srcE, bsrcE = et2[e2], b_et2[e2]
                                meng = "dve"
                                sc.op(meng, lambda e: e.tensor_tensor(
                                    out=pt[pi][:np_, :], in0=srcE[:np_, :], in1=maskT[:np_, blk, :], op=ALU.mult),
                                    reads=[bsrcE, b_maskT], writes=[b_pt[pi]])
                                return pi
                            attn_head(g, hb, ew, OTd, h)
                    sc.barrier()


    def phase_D():
        with ExitStack() as ps_:
            def sbl(name, shape, dt):
                return ps_.enter_context(nc.sbuf_tensor(name, list(shape), dt))
            arena = sbl("arena", [128, 32768], F32)
            hidv = arena[:].bitcast(BF16).rearrange("p (c t) -> p c t", t=512)
            OTf_v = hidv[:, 0:16, :]
            OTd_v = hidv[:, 16:32, :]
            mixedT = hidv[:, 32:64, :]
            H2T = arena[:, 16384:32768].rearrange("p (c t) -> p c t", t=512)
            orow = arena[:, 0:16384].rearrange("p (b m) -> p b m", m=D)
            U2T = sbl("U2T", [128, 32, 512], BF16)
            wsl = [sbl("wD%d" % i, [128, 32, 128], BF16) for i in range(3)]
            gA = [sbl("gA%d" % i, [128, 512], BF16) for i in range(2)]
            gB = [sbl("gB%d" % i, [128, 512], BF16) for i in range(2)]
            t1 = [sbl("t1%d" % i, [128, 512], F32) for i in range(2)]
            t2 = [sbl("t2%d" % i, [128, 512], F32) for i in range(2)]
            xst = [sbl("xst%d" % i, [128, 512], F32) for i in range(2)]
            h3s = t2
            sq = [sbl("sq%d" % i, [128, 512], BF16) for i in range(2)]
            rstd = sbl("rstd", [128, 512], F32)
            g2t = sbl("g2t", [128, 32], F32)
            gft = sbl("gft", [128, 32], F32)
            b_OT = sc.buf("OTv")
            b_mix = sc.buf("mixedT")
            b_H2T = sc.buf("H2T")
            b_hid = sc.buf("hid")
            b_orow = sc.buf("orow")
            b_U2T = sc.buf("U2T")
            b_w = sc.bufs(3, "wD")
            b_gA = sc.bufs(2, "gA")
            b_gB = sc.bufs(2, "gB")
            b_t1 = sc.bufs(2, "t1")
            b_t2 = sc.bufs(2, "t2")
            b_xst = sc.bufs(2, "xst")
            b_h3s = b_t2
            b_sq = sc.bufs(2, "sq")
            b_rstd = sc.buf("rstd")
            b_g = sc.buf("g2f")
            b_h2d = sc.bufs(32, "h2d")
            sc.op("sp", lambda e: e.dma_start(out=g2t[:], in_=g_mlp), writes=[b_g], dma_on=b_g)
            sc.op("sp", lambda e: e.dma_start(out=gft[:], in_=g_fin), pwrites=[b_g], dma_on=b_g)
            cnt = {"w": 0, "ps": 0, "x": 0, "q": 0, "h": 0}

            def wload(src, kcn):
                wi_ = cnt["w"] % 3
                cnt["w"] += 1
                sc.op("pool", lambda e: e.dma_start(out=wsl[wi_][:, 0:kcn, :], in_=src),
                      writes=[b_w[wi_]], dma_on=b_w[wi_])
                return wi_

            def nbank(nb=6):
                b = cnt["ps"] % nb
                cnt["ps"] += 1
                return b

            def sumsq_rstd(tag):
                sc.op("act", lambda e: e.activation(out=rstd[:], in_=psum[7][:], func=AF.Sqrt,
                                                    bias=epsc[:, 0:1], scale=1.0 / D),
                      reads=[b_ps[7], b_ident], writes=[b_rstd])
                sc.op("dve", lambda e: e.reciprocal(out=rstd[:], in_=rstd[:]), reads=[b_rstd], writes=[b_rstd])

            def row_tile(rt):
                t0 = rt * 512
                sc.op("sp", lambda e: e.dma_start(out=OTf_v, in_=OTf[:, :, t0:t0 + 512].rearrange("h d t -> d h t")),
                      writes=[b_OT], dma_on=b_OT)
                sc.op("sp", lambda e: e.dma_start(out=OTd_v, in_=OTd[:, :, t0:t0 + 512].rearrange("h d t -> d h t")),
                      pwrites=[b_OT], dma_on=b_OT)
                for j in range(32):
                    wa_ = wload(wbf[j], 16)
                    wb_ = wload(wbd[j], 16)
                    gi = j % 2
                    sc.op("sp", lambda e, j=j, gi=gi: e.dma_start(out=gA[gi][:], in_=Gs[j, :, t0:t0 + 512]),
                          writes=[b_gA[gi]], dma_on=b_gA[gi])
                    sc.op("sp", lambda e, j=j, gi=gi: e.dma_start(out=gB[gi][:], in_=Gs[32 + j, :, t0:t0 + 512]),
                          writes=[b_gB[gi]], dma_on=b_gB[gi])
                    ba, bb = nbank(), nbank()

                    def mmA(e, w=wa_, bank=ba, src=OTf_v):
                        ins = None
                        for kc in range(16):
                            ins = e.matmul(psum[bank][:, :], wsl[w][:, kc, :], src[:, kc, :], start=(kc == 0), stop=(kc == 15))
                        return ins

                    def mmB(e, w=wb_, bank=bb, src=OTd_v):
                        ins = None
                        for kc in range(16):
                            ins = e.matmul(psum[bank][:, :], wsl[w][:, kc, :], src[:, kc, :], start=(kc == 0), stop=(kc == 15))
                        return ins
                    sc.op("pe", mmA, reads=[b_w[wa_], b_OT], writes=[b_ps[ba]])
                    sc.op("pe", mmB, reads=[b_w[wb_], b_OT], writes=[b_ps[bb]])
                    sc.op("dve", lambda e, gi=gi, ba=ba: e.tensor_tensor(out=t1[gi][:], in0=psum[ba][:], in1=gA[gi][:], op=ALU.mult),
                          reads=[b_ps[ba], b_gA[gi]], writes=[b_t1[gi]])
                    sc.op("dve", lambda e, gi=gi, bb=bb: e.tensor_tensor(out=t2[gi][:], in0=psum[bb][:], in1=gB[gi][:], op=ALU.mult),
                          reads=[b_ps[bb], b_gB[gi]], writes=[b_t2[gi]])
                    sc.op("dve", lambda e, gi=gi, j=j: e.tensor_tensor(out=mixedT[:, j, :], in0=t1[gi][:], in1=t2[gi][:], op=ALU.add),
                          reads=[b_t1[gi], b_t2[gi]], pwrites=[b_mix])
                for j in range(32):
                    w_ = wload(wo[j], 32)
                    bank = nbank()
                    xi = cnt["x"] % 2
                    cnt["x"] += 1
                    qi_ = cnt["q"] % 2
                    cnt["q"] += 1
                    sc.op("sp", lambda e, j=j, xi=xi: e.dma_start(out=xst[xi][:], in_=XTo[j, :, t0:t0 + 512]),
                          writes=[b_xst[xi]], dma_on=b_xst[xi])

                    def mmO(e, w=w_, bank=bank):
                        ins = None
                        for kc in range(32):
                            ins = e.matmul(psum[bank][:, :], wsl[w][:, kc, :], mixedT[:, kc, :], start=(kc == 0), stop=(kc == 31))
                        return ins
                    sc.op("pe", mmO, reads=[b_w[w_], b_mix], writes=[b_ps[bank]])
                    sc.op("dve", lambda e, j=j, xi=xi, bank=bank: e.tensor_tensor(
                        out=H2T[:, j, :], in0=psum[bank][:], in1=xst[xi][:], op=ALU.add),
                        reads=[b_ps[bank], b_xst[xi]], pwrites=[b_H2T])
                    sc.op("act", lambda e, j=j, qi_=qi_: e.activation(out=sq[qi_][:], in_=H2T[:, j, :], func=AF.Square),
                          reads=[b_H2T], writes=[b_sq[qi_]])
                    sc.op("pe", lambda e, j=j, qi_=qi_: e.matmul(psum[7][:, :], onesb[:], sq[qi_][:], start=(j == 0), stop=(j == 31)),
                          reads=[b_sq[qi_], b_ident], writes=([b_ps[7]] if j == 0 else []), pwrites=([] if j == 0 else [b_ps[7]]))
                    sc.op("sp", lambda e, j=j: e.dma_start(out=H2s[j, :, t0:t0 + 512], in_=H2T[:, j, :]),
                          reads=[b_H2T], writes=[b_h2d[j]], dma_on=b_H2T)
                sumsq_rstd("a")
                for j in range(32):
                    sc.op("dve", lambda e, j=j: e.scalar_tensor_tensor(
                        out=U2T[:, j, :], in0=H2T[:, j, :], scalar=g2t[:, j:j + 1], in1=rstd[:], op0=ALU.mult, op1=ALU.mult),
                        reads=[b_H2T, b_g, b_rstd], pwrites=[b_U2T])
                sc.barrier()
                for f in range(128):
                    w_ = wload(wu[f], 32)
                    bank = nbank()
                    gi = f % 2

                    def mmU(e, w=w_, bank=bank):
                        ins = None
                        for kc in range(32):
                            ins = e.matmul(psum[bank][:, :], wsl[w][:, kc, :], U2T[:, kc, :], start=(kc == 0), stop=(kc == 31))
                        return ins
                    sc.op("pe", mmU, reads=[b_w[w_], b_U2T], writes=[b_ps[bank]])
                    sc.op("act", lambda e, gi=gi, bank=bank: e.activation(out=t1[gi][:], in_=psum[bank][:], func=AF.Relu),
                          reads=[b_ps[bank]], writes=[b_t1[gi]])
                    sc.op("dve", lambda e, gi=gi, f=f: e.tensor_tensor(out=hidv[:, f, :], in0=t1[gi][:], in1=t1[gi][:], op=ALU.mult),
                          reads=[b_t1[gi]], pwrites=[b_hid])
                for j in range(32):
                    bank = nbank(4)
                    xi = cnt["x"] % 2
                    cnt["x"] += 1
                    qi_ = cnt["q"] % 2
                    cnt["q"] += 1
                    hi_ = cnt["h"] % 2
                    cnt["h"] += 1
                    sc.op("sp", lambda e, j=j, xi=xi: e.dma_start(out=xst[xi][:], in_=H2s[j, :, t0:t0 + 512]),
                          reads=[b_h2d[j]], writes=[b_xst[xi]], dma_on=b_xst[xi])
                    for q in range(4):
                        w_ = wload(wd[j, q], 32)

                        def mmD(e, w=w_, bank=bank, q=q):
                            ins = None
                            for kc in range(32):
                                ins = e.matmul(psum[bank][:, :], wsl[w][:, kc, :], hidv[:, q * 32 + kc, :],
                                               start=(q == 0 and kc == 0), stop=(q == 3 and kc == 31))
                            return ins
                        sc.op("pe", mmD, reads=[b_w[w_], b_hid], writes=([b_ps[bank]] if q == 0 else []),
                              pwrites=([] if q == 0 else [b_ps[bank]]))
                    sc.op("dve", lambda e, xi=xi, bank=bank, hi_=hi_: e.tensor_tensor(
                        out=h3s[hi_][:], in0=psum[bank][:], in1=xst[xi][:], op=ALU.add),
                        reads=[b_ps[bank], b_xst[xi]], writes=[b_h3s[hi_]])
                    sc.op("act", lambda e, hi_=hi_, qi_=qi_: e.activation(out=sq[qi_][:], in_=h3s[hi_][:], func=AF.Square),
                          reads=[b_h3s[hi_]], writes=[b_sq[qi_]])
                    sc.op("pe", lambda e, j=j, qi_=qi_: e.matmul(psum[7][:, :], onesb[:], sq[qi_][:], start=(j == 0), stop=(j == 31)),
                          reads=[b_sq[qi_], b_ident], writes=([b_ps[7]] if j == 0 else []), pwrites=([] if j == 0 else [b_ps[7]]))
                    sc.op("sp", lambda e, j=j, hi_=hi_: e.dma_start(out=H2s[j, :, t0:t0 + 512], in_=h3s[hi_][:]),
                          reads=[b_h3s[hi_]], writes=[b_h2d[j]], dma_on=b_h3s[hi_])
                sumsq_rstd("b")
                sc.barrier()
                for j in range(32):
                    bank = nbank(4)
                    xi = cnt["x"] % 2
                    cnt["x"] += 1
                    gi = j % 2
                    sc.op("sp", lambda e, j=j, xi=xi: e.dma_start(out=xst[xi][:], in_=H2s[j, :, t0:t0 + 512]),
                          reads=[b_h2d[j]], writes=[b_xst[xi]], dma_on=b_xst[xi])
                    sc.op("dve", lambda e, j=j, xi=xi, gi=gi: e.scalar_tensor_tensor(
                        out=t1[gi][:], in0=xst[xi][:], scalar=gft[:, j:j + 1], in1=rstd[:], op0=ALU.mult, op1=ALU.mult),
                        reads=[b_xst[xi], b_g, b_rstd], writes=[b_t1[gi]])

                    def trO(e, gi=gi, bank=bank):
                        ins = None
                        for tb in range(4):
                            ins = e.transpose(out=psum[bank][:, tb * 128:(tb + 1) * 128],
                                              in_=t1[gi][:, tb * 128:(tb + 1) * 128], identity=identf[:])
                        return ins
                    sc.op("pe", trO, reads=[b_t1[gi], b_ident], writes=[b_ps[bank]])
                    ceng = "act" if j % 2 == 0 else "dve"
                    if ceng == "act":
                        sc.op("act", lambda e, j=j, bank=bank: e.copy(
                            out=orow[:, :, j * 128:(j + 1) * 128],
                            in_=psum[bank][:].rearrange("p (a b) -> p a b", b=128)),
                            reads=[b_ps[bank]], pwrites=[b_orow])
                    else:
                        sc.op("dve", lambda e, j=j, bank=bank: e.tensor_copy(
                            out=orow[:, :, j * 128:(j + 1) * 128],
                            in_=psum[bank][:].rearrange("p (a b) -> p a b", b=128)),
                            reads=[b_ps[bank]], pwrites=[b_orow])
                for tb in range(4):
                    sc.op("sp", lambda e, tb=tb: e.dma_start(out=out[t0 + tb * 128:t0 + (tb + 1) * 128, :], in_=orow[:, tb, :]),
                          reads=[b_orow], dma_on=b_orow)
                sc.barrier()
            for rt in range(4):
                row_tile(rt)

    es_mid = ExitStack()

    def sbm(name, shape, dt):
        return es_mid.enter_context(nc.sbuf_tensor(name, list(shape), dt))
    DT = sbm("DT", [16, TK], F32)
    WIT = sbm("WIT", [32, NOWN], F32)
    negD = sbm("negD", [128, 33, 16], F32)
    b_DT = sc.buf("DT")
    es_lf = ExitStack()
    LF = es_lf.enter_context(nc.sbuf_tensor("LF", [16, TK], F32))
    phase_A()
    if stop_after != "A":
        phase_B()
    if stop_after not in ("A", "B"):
        phase_B2()
        es_lf.close()
        phase_C0()
        phase_C(do_fox=True, do_dsa=(stop_after != "C1"))
    else:
        es_lf.close()
    es_mid.close()
    if stop_after == "all":
        phase_D()

    sc.barrier()
    sc.run()
    es.close()
    return nc


def _tile_w(W):
    K, N = W.shape
    return np.ascontiguousarray(W.reshape(K // 128, 128, N // 128, 128).transpose(2, 1, 0, 3))


def _t5_bucket_table():
    rel = np.arange(-4300, 4301, dtype=np.int64)
    half, max_exact = 16, 8
    ret = np.where(rel > 0, half, 0)
    n = np.abs(rel)
    nf = np.maximum(n, 1).astype(np.float32)
    large = max_exact + (np.log(nf / np.float32(max_exact)) / np.float32(math.log(128 / max_exact))
                         * np.float32(half - max_exact)).astype(np.int32)
    large = np.minimum(large, half - 1)
    return (ret + np.where(n < max_exact, n, large)).astype(np.int64)


def _host_prepare(inp):
    f32 = np.float32
    x = np.asarray(inp["x"], f32)
    w_in = np.asarray(inp["w_in"], f32)[0]
    maps = []
    z = np.zeros
    ki = w_in[:, O_KI:O_KI + 64]
    fa_chunk = np.concatenate([w_in[:, O_FA:O_FA + 16], z((D, 112), f32)], axis=1)
    wi_chunk = np.concatenate([w_in[:, O_WI:O_WI + 32], z((D, 96), f32)], axis=1)
    wk_cols = np.concatenate([w_in[:, O_KA:O_KA + 2048], w_in[:, O_KB:O_KB + 2048],
                              w_in[:, O_VA:O_VA + 2048], w_in[:, O_VB:O_VB + 2048],
                              ki, ki, fa_chunk], axis=1)
    wq_cols = np.concatenate([w_in[:, O_QA:O_QA + 2048], w_in[:, O_QB:O_QB + 2048],
                              w_in[:, O_QI:O_QI + 2048], wi_chunk,
                              w_in[:, O_GL:O_GL + 8192]], axis=1)
    wk = _tile_w(wk_cols)
    wq = _tile_w(wq_cols)
    wbf = _tile_w(np.asarray(inp["w_branch_fox"], f32)[0])
    wbd = _tile_w(np.asarray(inp["w_branch_dsa"], f32)[0])
    wo = _tile_w(np.asarray(inp["w_out"], f32)[0])
    wu = _tile_w(np.asarray(inp["w_up"], f32)[0])
    wdn = _tile_w(np.asarray(inp["w_down"], f32)[0])
    wd = np.ascontiguousarray(wdn.reshape(32, 128, 4, 32, 128).transpose(0, 2, 1, 3, 4))
    g_attn = np.asarray(inp["attn_norm_g"], f32).reshape(1, D)
    g_mlp = np.ascontiguousarray(np.asarray(inp["mlp_norm_g"], f32).reshape(32, 128).T)
    g_fin = np.ascontiguousarray(np.asarray(inp["final_norm_g"], f32).reshape(32, 128).T)
    negfb = np.ascontiguousarray(-np.asarray(inp["forget_bias"], f32).reshape(16, 1))
    rel_bias = np.asarray(inp["rel_bias"], f32)
    meta = np.asarray(inp["meta_tokens"], f32)
    bt = _t5_bucket_table()

    def bkt(rel):
        return bt[rel + 4300]
    cfar = np.ascontiguousarray(np.broadcast_to(rel_bias[15][None, :], (128, H))).astype(f32)
    identf = np.eye(128, dtype=f32)
    si = np.arange(128)
    per_par = {}
    for p in (0, 1):
        u = np.zeros((H, 128, 3, 128), f32)
        for jj, j in enumerate((-1, 0, 1)):
            dblk = j - p
            rel = dblk * 128 + si[:, None] - si[None, :]
            u[:, :, jj, :] = rel_bias[bkt(rel)].transpose(2, 0, 1)
        relm = si[:16, None] - (16 + 128 * p + si[None, :])
        um = np.ascontiguousarray(rel_bias[bkt(relm)].transpose(2, 0, 1)).astype(f32)
        fm = np.zeros((8, 128, 512), f32)
        for r in range(8):
            for k in range(4):
                d = r - (2 * k + p)
                blk = fm[r, :, k * 128:(k + 1) * 128]
                if d > 0:
                    blk[:] = NEG
                elif d == 0:
                    blk[:] = np.where(si[:, None] <= si[None, :], 0.0, NEG)
        ad = np.zeros((128, 256), f32)
        cm = np.where((si[None, :] // 64) <= (si[:, None] // 64), 0.0, NEG)
        if p == 0:
            ad[:, 0:128] = cm
            ad[:, 128:256] = NEG
        else:
            ad[:, 128:256] = cm
        selv = np.zeros((16, 2), f32)
        selv[:, p] = 1.0
        per_par[p] = dict(ubias=u, umeta=um, fmask=fm, adm=ad, sel=selv)
    for c in range(N_CORES):
        b, p = c // 2, c % 2
        xb = x[b]
        x_own = np.ascontiguousarray(xb.reshape(32, 128, D)[p::2].reshape(NOWN, D))
        m = dict(x_all=xb, x_own=x_own, meta=meta, g_attn=g_attn, g_mlp=g_mlp, g_fin=g_fin,
                 wk=wk, wq=wq, wbf=wbf, wbd=wbd, wo=wo, wu=wu, wd=wd, negfb=negfb,
                 identf=identf, cfar=cfar, ncfar=-cfar)
        m.update(per_par[p])
        maps.append(m)
    return maps


def _assemble(results):
    outp = np.empty((4, L, D), np.float32)
    for c in range(N_CORES):
        b, p = c // 2, c % 2
        o = np.asarray(results[c]["out"], np.float32).reshape(16, 128, D)
        outp[b].reshape(32, 128, D)[p::2] = o
    return outp


def kernel(**inputs):
    maps = _host_prepare(inputs)
    nc = build_program()
    res = run_bass_kernel_spmd(nc, maps, core_ids=list(range(N_CORES)))
    return _assemble(res.results)
```

```python
import math
from contextlib import ExitStack

import numpy as np
import concourse.bass as bass
import concourse.mybir as mybir
from concourse.bass_utils import run_bass_kernel_spmd

F32 = mybir.dt.float32
BF16 = mybir.dt.bfloat16
AF = mybir.ActivationFunctionType
ALU = mybir.AluOpType

D = 4096
L = 4096
NMETA = 16
TK = L + NMETA
NOWN = 2048
H = 16
HD = 128
NIDX = 32
DIDX = 64
KTOP = 256
DFF = 16384
EPS = 1e-6
NEG = -1.0e30
N_CORES = 8

O_QA, O_KA, O_VA, O_FA, O_QB, O_KB, O_VB, O_QI, O_KI, O_WI, O_GL = (
    0, 2048, 4096, 6144, 6160, 8208, 10256, 12304, 14352, 14416, 14448)


class _S:
    __slots__ = ("h", "count")

    def __init__(self, h):
        self.h = h
        self.count = 0


class Buf:
    __slots__ = ("name", "writers", "readers", "dsem")

    def __init__(self, name=""):
        self.name = name
        self.writers = {}
        self.readers = {}
        self.dsem = None


class _Eng:
    def __init__(self, name):
        self.name = name
        self.ops = []
        self.sem = None
        self.waited = {}


class Sched:
    def __init__(self, nc, es):
        self.nc = nc
        self.es = es
        self.engs = {n: _Eng(n) for n in ("pe", "act", "dve", "pool", "sp")}
        for n, e in self.engs.items():
            e.sem = _S(es.enter_context(nc.semaphore("e_" + n)))
        self.dsems = []
        self.nbuf = 0

    def buf(self, name=""):
        self.nbuf += 1
        return Buf(name)

    def bufs(self, n, name=""):
        return [self.buf(name + str(i)) for i in range(n)]

    def _dsem(self, b):
        if b.dsem is None:
            b.dsem = _S(self.es.enter_context(self.nc.semaphore("d%d_%s" % (len(self.dsems), b.name[:10]))))
            self.dsems.append(b.dsem)
        return b.dsem

    def _emit(self, eng, toks, fn, sem, inc):
        waits = []
        for S, val in toks.items():
            if eng.name == "pe" and S is eng.sem:
                continue
            if eng.waited.get(S, 0) >= val:
                continue
            eng.waited[S] = val
            waits.append((S, val))
        sem.count += inc
        val = sem.count

        def thunk(e, waits=waits, fn=fn, sem=sem, inc=inc):
            for S, v in waits:
                e.wait_ge(S.h, v)
            ins = fn(e)
            ins.then_inc(sem.h, inc)

        eng.ops.append(thunk)
        return (sem, val)

    @staticmethod
    def _merge(d, tok):
        S, v = tok
        if d.get(S, 0) < v:
            d[S] = v

    def op(self, engname, fn, reads=(), writes=(), pwrites=(), dma_on=None):
        eng = self.engs[engname]
        toks = {}
        for b in reads:
            for t in b.writers.items():
                self._merge(toks, t)
        for b in writes:
            for t in b.writers.items():
                self._merge(toks, t)
            for t in b.readers.items():
                self._merge(toks, t)
        for b in pwrites:
            for t in b.readers.items():
                self._merge(toks, t)
        if dma_on is not None:
            sem, inc = self._dsem(dma_on), 16
        else:
            sem, inc = eng.sem, 1
        tok = self._emit(eng, toks, fn, sem, inc)
        for b in writes:
            b.writers = {tok[0]: tok[1]}
            b.readers = {}
        for b in pwrites:
            if b.readers:
                b.writers = {tok[0]: tok[1]}
                b.readers = {}
            else:
                self._merge(b.writers, tok)
        for b in reads:
            self._merge(b.readers, tok)
        return tok

    def barrier(self):
        allt = {}
        for e in self.engs.values():
            if e.sem.count:
                allt[e.sem] = e.sem.count
        for S in self.dsems:
            if S.count:
                allt[S] = S.count
        for e in self.engs.values():
            waits = []
            for S, val in allt.items():
                if e.waited.get(S, 0) >= val:
                    continue
                e.waited[S] = val
                waits.append((S, val))
            if waits:
                def thunk(en, waits=waits):
                    for S, v in waits:
                        en.wait_ge(S.h, v)
                e.ops.append(thunk)

    def run(self):
        nc = self.nc
        with nc.Block() as block:
            @block.tensor
            def _(e):
                for t in self.engs["pe"].ops:
                    t(e)

            @block.scalar
            def _(e):
                for t in self.engs["act"].ops:
                    t(e)

            @block.vector
            def _(e):
                for t in self.engs["dve"].ops:
                    t(e)

            @block.gpsimd
            def _(e):
                for t in self.engs["pool"].ops:
                    t(e)

            @block.sync
            def _(e):
                for t in self.engs["sp"].ops:
                    t(e)


def build_program(stop_after="all", debug_outs=()):
    nc = bass.Bass("TRN2", target_bir_lowering=False)
    es = ExitStack()
    sc = Sched(nc, es)

    def din(name, shape, dt=F32):
        return nc.dram_tensor(name, list(shape), dt, kind="ExternalInput").ap()

    def dscr(name, shape, dt):
        kind = "ExternalOutput" if name in debug_outs else "Internal"
        return nc.dram_tensor(name, list(shape), dt, kind=kind).ap()

    x_all = din("x_all", [L, D])
    x_own = din("x_own", [NOWN, D])
    meta = din("meta", [NMETA, D])
    g_attn = din("g_attn", [1, D])
    g_mlp = din("g_mlp", [128, 32])
    g_fin = din("g_fin", [128, 32])
    NK = 66
    NQ = 113
    wk = din("wk", [NK, 128, 32, 128])
    wq = din("wq", [NQ, 128, 32, 128])
    wbf = din("wbf", [32, 128, 16, 128])
    wbd = din("wbd", [32, 128, 16, 128])
    wo = din("wo", [32, 128, 32, 128])
    wu = din("wu", [128, 128, 32, 128])
    wd = din("wd", [32, 4, 128, 32, 128])
    negfb = din("negfb", [16, 1])
    sel = din("sel", [16, 2])
    identf_d = din("identf", [128, 128])
    ubias = din("ubias", [H, 128, 3, 128])
    umeta = din("umeta", [H, 16, 128])
    cfar = din("cfar", [128, H])
    ncfar = din("ncfar", [128, H])
    fmask = din("fmask", [8, 128, 512])
    adm = din("adm", [128, 256])
    out = nc.dram_tensor("out", [NOWN, D], F32, kind="ExternalOutput").ap()

    UTa = dscr("UTa", [32, 128, TK], BF16)
    UTo = dscr("UTo", [32, 128, NOWN], BF16)
    XTo = dscr("XTo", [32, 128, NOWN], F32)
    KTf = dscr("KTf", [H, 128, TK], BF16)
    KTd = dscr("KTd", [H, 128, TK], BF16)
    Vf = dscr("Vf", [H, TK, 128], BF16)
    Vd = dscr("Vd", [H, TK, 128], BF16)
    KI2 = dscr("KI2", [128, TK], BF16)
    QTf = dscr("QTf", [H, 128, NOWN], BF16)
    QTd = dscr("QTd", [H, 128, NOWN], BF16)
    QI = dscr("QI", [16, 128, NOWN], BF16)
    Gs = dscr("Gs", [64, 128, NOWN], BF16)
    DTo = dscr("DTo", [H, NOWN], F32)
    BMs = dscr("BMs", [H, 128, 9, 512], BF16)
    BMm = dscr("BMm", [H, 16, 512], BF16)
    OTf = dscr("OTf", [H, 128, NOWN], BF16)
    OTd = dscr("OTd", [H, 128, NOWN], BF16)
    H2s = dscr("H2s", [32, 128, NOWN], F32)
    dbg = {}

    def sb(name, shape, dt):
        return es.enter_context(nc.sbuf_tensor(name, list(shape), dt))

    identf = sb("identf_s", [128, 128], F32)
    identb = sb("identb_s", [128, 128], BF16)
    onesb = sb("onesb", [128, 128], BF16)
    b_ident = sc.buf("ident")
    b_LF = sc.buf("LF")
    b_WIT = sc.buf("WIT")
    b_negD = sc.buf("negD")
    psum = [es.enter_context(nc.psum_tensor("ps%d" % i, [128, 512], F32)) for i in range(8)]
    b_ps = sc.bufs(8, "ps")

    def psb(i):
        return psum[i][:].bitcast(BF16)

    sc.op("sp", lambda e: e.dma_start(out=identf[:], in_=identf_d), writes=[b_ident], dma_on=b_ident)
    sc.op("dve", lambda e: e.tensor_copy(out=identb[:], in_=identf[:]), reads=[b_ident], pwrites=[b_ident])
    sc.op("dve", lambda e: e.memset(onesb[:], 1.0), pwrites=[b_ident])
    epsc = sb("epsc", [128, 1], F32)
    sc.op("dve", lambda e: e.memset(epsc[:], EPS), pwrites=[b_ident])

    def phase_A():
        with ExitStack() as ps_:
            def sbl(name, shape, dt):
                return ps_.enter_context(nc.sbuf_tensor(name, list(shape), dt))
            gb = sbl("gb", [128, D], F32)
            xt = [sbl("xt%d" % i, [128, D], F32) for i in range(3)]
            ub = [sbl("ub%d" % i, [128, D], BF16) for i in range(2)]
            junk = sbl("junk", [128, D], BF16)
            ss = [sbl("ss%d" % i, [128, 1], F32) for i in range(2)]
            rs = [sbl("rs%d" % i, [128, 1], F32) for i in range(2)]
            ut = [sbl("ut%d" % i, [128, 32, 128], BF16) for i in range(2)]
            xtt = [sbl("xtt%d" % i, [128, 32, 128], F32) for i in range(2)]
            b_gb = sc.buf("gb")
            b_xt = sc.bufs(3, "xt")
            b_ub = sc.bufs(2, "ub")
            b_junk = sc.buf("junk")
            b_ss = sc.bufs(2, "ss")
            b_rs = sc.bufs(2, "rs")
            b_ut = sc.bufs(2, "ut")
            b_xtt = sc.bufs(2, "xtt")
            sc.op("sp", lambda e: e.dma_start(out=gb[:], in_=g_attn.partition_broadcast(128)),
                  writes=[b_gb], dma_on=b_gb)
            blocks = [("all", i) for i in range(32)] + [("meta", 0)] + [("own", i) for i in range(16)]
            pcnt = [0]
            for bi, (kind, i) in enumerate(blocks):
                np_ = 16 if kind == "meta" else 128
                if kind == "all":
                    src = x_all[i * 128:(i + 1) * 128, :]
                    dstu = UTa[:, :, i * 128:(i + 1) * 128]
                elif kind == "meta":
                    src = meta
                    dstu = UTa[:, :, L:L + 16]
                else:
                    src = x_own[i * 128:(i + 1) * 128, :]
                    dstu = UTo[:, :, i * 128:(i + 1) * 128]
                X, bX = xt[bi % 3], b_xt[bi % 3]
                U, bU = ub[bi % 2], b_ub[bi % 2]
                SS, bSS = ss[bi % 2], b_ss[bi % 2]
                RS, bRS = rs[bi % 2], b_rs[bi % 2]
                UT, bUT = ut[bi % 2], b_ut[bi % 2]
                sc.op("sp", lambda e, X=X, src=src, np_=np_: e.dma_start(out=X[:np_, :], in_=src),
                      writes=[bX], dma_on=bX)
                sc.op("act", lambda e, X=X, SS=SS, np_=np_: e.activation(
                    out=junk[:np_, :], in_=X[:np_, :], func=AF.Square, accum_out=SS[:np_, :]),
                    reads=[bX], writes=[b_junk, bSS])
                sc.op("act", lambda e, SS=SS, RS=RS, np_=np_: e.activation(
                    out=RS[:np_, :], in_=SS[:np_, :], func=AF.Sqrt, bias=epsc[:np_, 0:1], scale=1.0 / D),
                    reads=[bSS, b_ident], writes=[bRS])
                sc.op("dve", lambda e, RS=RS, np_=np_: e.reciprocal(out=RS[:np_, :], in_=RS[:np_, :]),
                    reads=[bRS], writes=[bRS])
                sc.op("dve", lambda e, X=X, U=U, RS=RS, np_=np_: e.scalar_tensor_tensor(
                    out=U[:np_, :], in0=X[:np_, :], scalar=RS[:np_, 0:1], in1=gb[:np_, :],
                    op0=ALU.mult, op1=ALU.mult), reads=[bX, bRS, b_gb], writes=[bU])
                for q in range(8):
                    bank = pcnt[0] % 4
                    pcnt[0] += 1

                    def tr(e, U=U, q=q, bank=bank, np_=np_):
                        ins = None
                        for j in range(4):
                            kc = q * 4 + j
                            ins = e.transpose(out=psb(bank)[:, j * 128:j * 128 + np_],
                                              in_=U[:np_, kc * 128:(kc + 1) * 128],
                                              identity=identb[:np_, :np_])
                        return ins
                    sc.op("pe", tr, reads=[bU, b_ident], writes=[b_ps[bank]])
                    ceng = "act" if q % 2 == 0 else "dve"

                    def cp(e, UT=UT, q=q, bank=bank, np_=np_, ceng=ceng):
                        src_ = psb(bank)[:, 0:512].rearrange("p (a b) -> p a b", b=128)[:, :, :np_]
                        if ceng == "act":
                            return e.copy(out=UT[:, q * 4:q * 4 + 4, :np_], in_=src_)
                        return e.tensor_copy(out=UT[:, q * 4:q * 4 + 4, :np_], in_=src_)
                    sc.op(ceng, cp, reads=[b_ps[bank]], pwrites=[bUT])
                sc.op("sp", lambda e, UT=UT, dstu=dstu, np_=np_: e.dma_start(
                    out=dstu.rearrange("kc kp c -> kp kc c"), in_=UT[:, :, :np_]),
                    reads=[bUT], dma_on=bUT)
                if kind == "own":
                    XT, bXT = xtt[bi % 2], b_xtt[bi % 2]
                    for q in range(8):
                        bank = 4 + pcnt[0] % 4
                        pcnt[0] += 1

                        def trf(e, X=X, q=q, bank=bank):
                            ins = None
                            for j in range(4):
                                m = q * 4 + j
                                ins = e.transpose(out=psum[bank][:, j * 128:(j + 1) * 128],
                                                  in_=X[:, m * 128:(m + 1) * 128], identity=identf[:])
                            return ins
                        sc.op("pe", trf, reads=[bX, b_ident], writes=[b_ps[bank]])
                        ceng = "dve" if q % 2 == 0 else "act"

                        def cpf(e, XT=XT, q=q, bank=bank, ceng=ceng):
                            src_ = psum[bank][:].rearrange("p (a b) -> p a b", b=128)
                            if ceng == "act":
                                return e.copy(out=XT[:, q * 4:q * 4 + 4, :], in_=src_)
                            return e.tensor_copy(out=XT[:, q * 4:q * 4 + 4, :], in_=src_)
                        sc.op(ceng, cpf, reads=[b_ps[bank]], pwrites=[bXT])
                    sc.op("sp", lambda e, XT=XT, i=i: e.dma_start(
                        out=XTo[:, :, i * 128:(i + 1) * 128].rearrange("kc kp c -> kp kc c"), in_=XT[:]),
                        reads=[bXT], dma_on=bXT)
            sc.barrier()

    def phase_B():
        with ExitStack() as ps_:
            def sbl(name, shape, dt):
                return ps_.enter_context(nc.sbuf_tensor(name, list(shape), dt))
            uT = [sbl("uT%d" % i, [128, 32, 1040], BF16) for i in range(2)]
            wsl = [sbl("wsl%d" % i, [128, 32, 128], BF16) for i in range(3)]
            stg = [sbl("stg%d" % i, [128, 512], BF16) for i in range(4)]
            vstg = [sbl("vstg%d" % i, [128, 4, 128], BF16) for i in range(2)]
            ftmp = [sbl("ftmp%d" % i, [16, 512], F32) for i in range(2)]
            nfb = sbl("nfb", [16, 1], F32)
            b_uT = sc.bufs(2, "uT")
            b_w = sc.bufs(3, "wsl")
            b_stg = sc.bufs(4, "stg")
            b_vstg = sc.bufs(2, "vstg")
            b_ftmp = sc.bufs(2, "ftmp")
            b_nfb = sc.buf("nfb")
            sc.op("sp", lambda e: e.dma_start(out=nfb[:], in_=negfb), writes=[b_nfb], dma_on=b_nfb)

            kchunks = ([("kt", KTf, h) for h in range(H)] + [("kt", KTd, h) for h in range(H)] +
                       [("vt", Vf, h) for h in range(H)] + [("vt", Vd, h) for h in range(H)] +
                       [("k2", KI2, 0), ("fa", None, 0)])
            qchunks = ([("kt", QTf, h) for h in range(H)] + [("kt", QTd, h) for h in range(H)] +
                       [("kt", QI, h) for h in range(16)] + [("wi", None, 0)] +
                       [("g", Gs, j) for j in range(64)])
            ktiles = [[(r * 1024, 512), (r * 1024 + 512, 512)] for r in range(3)] + \
                     [[(3072, 512), (3584, 512), (4096, 16)]]
            qtiles = [[(r * 1024, 512), (r * 1024 + 512, 512)] for r in range(2)]
            cnt = {"w": 0, "ps": 0, "stg": 0, "v": 0, "f": 0, "u": 0}

            def run_side(UTsrc, tiles, chunks, wdram):
                for segs in tiles:
                    c_lo = segs[0][0]
                    ncol = segs[-1][0] + segs[-1][1] - c_lo
                    ui = cnt["u"] % 2
                    cnt["u"] += 1
                    sc.op("sp", lambda e, ui=ui, c_lo=c_lo, ncol=ncol: e.dma_start(
                        out=uT[ui][:, :, 0:ncol],
                        in_=UTsrc[:, :, c_lo:c_lo + ncol].rearrange("kc kp c -> kp kc c")),
                        writes=[b_uT[ui]], dma_on=b_uT[ui])
                    for ci, (kind, dst, idx) in enumerate(chunks):
                        wi_ = cnt["w"] % 3
                        cnt["w"] += 1
                        sc.op("pool", lambda e, wi_=wi_, ci=ci: e.dma_start(out=wsl[wi_][:], in_=wdram[ci]),
                              writes=[b_w[wi_]], dma_on=b_w[wi_])
                        for (c0, n) in segs:
                            bank = cnt["ps"] % 6
                            cnt["ps"] += 1
                            o0 = c0 - c_lo

                            def mm(e, wi_=wi_, ui=ui, bank=bank, o0=o0, n=n):
                                ins = None
                                for kc in range(32):
                                    ins = e.matmul(psum[bank][:, 0:n], wsl[wi_][:, kc, :],
                                                   uT[ui][:, kc, o0:o0 + n], start=(kc == 0), stop=(kc == 31))
                                return ins
                            sc.op("pe", mm, reads=[b_w[wi_], b_uT[ui]], writes=[b_ps[bank]])
                            if kind in ("kt", "k2", "g", "vt"):
                                si = cnt["stg"] % 4
                                cnt["stg"] += 1
                                fn = AF.Sigmoid if kind == "g" else AF.Copy
                                sc.op("act", lambda e, si=si, bank=bank, n=n, fn=fn: e.activation(
                                    out=stg[si][:, 0:n], in_=psum[bank][:, 0:n], func=fn),
                                    reads=[b_ps[bank]], writes=[b_stg[si]])
                                if kind == "vt":
                                    vi = cnt["v"] % 2
                                    cnt["v"] += 1
                                    tb = 6 + vi
                                    nb = (n + 127) // 128
                                    w_ = min(n, 128)

                                    def trv(e, si=si, tb=tb, nb=nb, w_=w_):
                                        ins = None
                                        for j in range(nb):
                                            ins = e.transpose(out=psb(tb)[:w_, j * 128:(j + 1) * 128],
                                                              in_=stg[si][:, j * 128:j * 128 + w_],
                                                              identity=identb[:])
                                        return ins
                                    sc.op("pe", trv, reads=[b_stg[si], b_ident], writes=[b_ps[tb]])
                                    sc.op("dve", lambda e, vi=vi, tb=tb, nb=nb, w_=w_: e.tensor_copy(
                                        out=vstg[vi][:w_, 0:nb, :],
                                        in_=psb(tb)[:w_, 0:nb * 128].rearrange("p (a b) -> p a b", b=128)),
                                        reads=[b_ps[tb]], writes=[b_vstg[vi]])
                                    if n >= 128:
                                        dv = dst[idx, c0:c0 + n, :].rearrange("(b s) d -> s b d", s=128)
                                        sc.op("sp", lambda e, vi=vi, dv=dv, nb=nb: e.dma_start(
                                            out=dv, in_=vstg[vi][:, 0:nb, :]), reads=[b_vstg[vi]], dma_on=b_vstg[vi])
                                    else:
                                        dv = dst[idx, c0:c0 + n, :]
                                        sc.op("sp", lambda e, vi=vi, dv=dv, n=n: e.dma_start(
                                            out=dv, in_=vstg[vi][:n, 0, :]), reads=[b_vstg[vi]], dma_on=b_vstg[vi])
                                else:
                                    dd = dst[c0:c0 + n] if False else (
                                        dst[:, c0:c0 + n] if kind == "k2" else dst[idx, :, c0:c0 + n])
                                    sc.op("sp", lambda e, si=si, dd=dd, n=n: e.dma_start(
                                        out=dd, in_=stg[si][:, 0:n]), reads=[b_stg[si]], dma_on=b_stg[si])
                            elif kind == "fa":
                                fi = cnt["f"] % 2
                                cnt["f"] += 1
                                sc.op("act", lambda e, fi=fi, bank=bank, n=n: e.activation(
                                    out=ftmp[fi][:, 0:n], in_=psum[bank][0:16, 0:n], func=AF.Exp,
                                    bias=nfb[:, 0:1], scale=-1.0),
                                    reads=[b_ps[bank], b_nfb], writes=[b_ftmp[fi]])
                                sc.op("act", lambda e, fi=fi, n=n: e.activation(
                                    out=ftmp[fi][:, 0:n], in_=ftmp[fi][:, 0:n], func=AF.Ln, bias=1.0),
                                    reads=[b_ftmp[fi]], writes=[b_ftmp[fi]])
                                sc.op("dve", lambda e, fi=fi, c0=c0, n=n: e.tensor_scalar(
                                    out=LF[:, c0:c0 + n], in0=ftmp[fi][:, 0:n], scalar1=-1.0, scalar2=None,
                                    op0=ALU.mult), reads=[b_ftmp[fi]], pwrites=[b_LF])
                            elif kind == "wi":
                                sc.op("act", lambda e, bank=bank, c0=c0, n=n: e.copy(
                                    out=WIT[:, c0:c0 + n], in_=psum[bank][0:32, 0:n]),
                                    reads=[b_ps[bank]], pwrites=[b_WIT])
            run_side(UTa, ktiles, kchunks, wk)
            run_side(UTo, qtiles, qchunks, wq)
            sc.barrier()


    SCALE = float(HD) ** -0.5
    CW = float(DIDX) ** -0.5 * float(NIDX) ** -0.5

    def phase_B2():
        with ExitStack() as ps_:
            def sbl(name, shape, dt):
                return ps_.enter_context(nc.sbuf_tensor(name, list(shape), dt))
            ones16 = sbl("ones16", [16, L], F32)
            dto = sbl("dto", [16, NOWN], F32)
            tmpo = [sbl("tmpo%d" % i, [16, 128], F32) for i in range(2)]
            selt = sbl("selt", [16, 2], F32)
            b_ones = sc.buf("ones16")
            b_dto = sc.buf("dto")
            b_tmpo = sc.bufs(2, "tmpo")
            b_sel = sc.buf("sel")
            sc.op("pool", lambda e: e.memset(ones16[:], 1.0), writes=[b_ones])
            sc.op("sp", lambda e: e.dma_start(out=selt[:], in_=sel), writes=[b_sel], dma_on=b_sel)
            sc.op("dve", lambda e: e.tensor_tensor_scan(
                out=DT[:, L:TK], data0=ones16[:, 0:16], data1=LF[:, L:TK], initial=0.0,
                op0=ALU.mult, op1=ALU.add), reads=[b_LF, b_ones], writes=[b_DT])
            sc.op("dve", lambda e: e.tensor_tensor_scan(
                out=DT[:, 0:L], data0=ones16[:, 0:L], data1=LF[:, 0:L], initial=DT[:, TK - 1:TK],
                op0=ALU.mult, op1=ALU.add), reads=[b_LF, b_ones, b_DT], pwrites=[b_DT])
            for q in range(8):
                bank = q % 4

                def trd(e, q=q, bank=bank):
                    ins = None
                    for j in range(4):
                        blk = q * 4 + j
                        ins = e.transpose(out=psum[bank][:, j * 16:(j + 1) * 16],
                                          in_=DT[:, blk * 128:(blk + 1) * 128], identity=identf[:16, :16])
                    return ins
                sc.op("pe", trd, reads=[b_DT, b_ident], writes=[b_ps[bank]])
                sc.op("dve", lambda e, q=q, bank=bank: e.tensor_scalar(
                    out=negD[:, q * 4:q * 4 + 4, :],
                    in0=psum[bank][:, 0:64].rearrange("p (a b) -> p a b", b=16),
                    scalar1=-1.0, scalar2=None, op0=ALU.mult), reads=[b_ps[bank]], pwrites=[b_negD])
            sc.op("pe", lambda e: e.transpose(out=psum[4][:16, 0:16], in_=DT[:, L:TK], identity=identf[:16, :16]),
                  reads=[b_DT, b_ident], writes=[b_ps[4]])
            sc.op("dve", lambda e: e.tensor_scalar(out=negD[:16, 32, :], in0=psum[4][:16, 0:16],
                                                   scalar1=-1.0, scalar2=None, op0=ALU.mult),
                  reads=[b_ps[4]], pwrites=[b_negD])
            for i in range(16):
                t_, bt_ = tmpo[i % 2], b_tmpo[i % 2]
                sc.op("dve", lambda e, i=i, t_=t_: e.tensor_scalar(
                    out=t_[:], in0=DT[:, (2 * i) * 128:(2 * i + 1) * 128], scalar1=selt[:, 0:1], scalar2=None,
                    op0=ALU.mult), reads=[b_DT, b_sel], writes=[bt_])
                sc.op("dve", lambda e, i=i, t_=t_: e.scalar_tensor_tensor(
                    out=dto[:, i * 128:(i + 1) * 128], in0=DT[:, (2 * i + 1) * 128:(2 * i + 2) * 128],
                    scalar=selt[:, 1:2], in1=t_[:], op0=ALU.mult, op1=ALU.add),
                    reads=[b_DT, b_sel, bt_], pwrites=[b_dto])
            sc.op("sp", lambda e: e.dma_start(out=DTo, in_=dto[:]), reads=[b_dto], dma_on=b_dto)
            sc.barrier()

    def phase_C0():
        with ExitStack() as ps_:
            def sbl(name, shape, dt):
                return ps_.enter_context(nc.sbuf_tensor(name, list(shape), dt))
            ncf = sbl("ncf0", [128, H], F32)
            utl = [sbl("utl%d" % i, [128, 3, 128], F32) for i in range(2)]
            uml = [sbl("uml%d" % i, [16, 128], F32) for i in range(2)]
            bm = [sbl("bm0%d" % i, [128, 9, 512], BF16) for i in range(2)]
            bmm = [sbl("bmm0%d" % i, [16, 512], BF16) for i in range(2)]
            b_ncf = sc.buf("ncf")
            b_utl = sc.bufs(2, "utl")
            b_uml = sc.bufs(2, "uml")
            b_bm = sc.bufs(2, "bm")
            b_bmm = sc.bufs(2, "bmm")
            sc.op("sp", lambda e: e.dma_start(out=ncf[:], in_=ncfar), writes=[b_ncf], dma_on=b_ncf)
            for h in range(H):
                i2 = h % 2
                sc.op("sp", lambda e, h=h, i2=i2: e.dma_start(out=utl[i2][:], in_=ubias[h]),
                      writes=[b_utl[i2]], dma_on=b_utl[i2])
                sc.op("sp", lambda e, h=h, i2=i2: e.dma_start(out=uml[i2][:], in_=umeta[h]),
                      writes=[b_uml[i2]], dma_on=b_uml[i2])
                sc.op("dve", lambda e, i2=i2: e.memset(bm[i2][:], 1.0), writes=[b_bm[i2]])
                sc.op("dve", lambda e, i2=i2: e.memset(bmm[i2][:], 1.0), writes=[b_bmm[i2]])

                def fill(e, h=h, i2=i2):
                    ins = None
                    for k in range(4):
                        for jj in range(3):
                            slot = 2 * k + (jj - 1) + 1
                            if slot < 0 or slot > 8:
                                continue
                            ins = e.activation(out=bm[i2][:, slot, k * 128:(k + 1) * 128], in_=utl[i2][:, jj, :],
                                               func=AF.Exp, bias=ncf[:, h:h + 1], scale=1.0)
                    return ins
                sc.op("act", fill, reads=[b_utl[i2], b_ncf], pwrites=[b_bm[i2]])
                sc.op("act", lambda e, h=h, i2=i2: e.activation(
                    out=bmm[i2][:, 0:128], in_=uml[i2][:], func=AF.Exp, bias=ncf[:16, h:h + 1], scale=1.0),
                    reads=[b_uml[i2], b_ncf], pwrites=[b_bmm[i2]])
                sc.op("sp", lambda e, h=h, i2=i2: e.dma_start(out=BMs[h], in_=bm[i2][:]),
                      reads=[b_bm[i2]], dma_on=b_bm[i2])
                sc.op("sp", lambda e, h=h, i2=i2: e.dma_start(out=BMm[h], in_=bmm[i2][:]),
                      reads=[b_bmm[i2]], dma_on=b_bmm[i2])
            sc.barrier()

    def phase_C(do_fox=True, do_dsa=True):
        with ExitStack() as ps_:
            def sbl(name, shape, dt):
                return ps_.enter_context(nc.sbuf_tensor(name, list(shape), dt))
            kt = [sbl("kt%d" % i, [128, TK], BF16) for i in range(2)]
            vv = [sbl("vv%d" % i, [128, 33, 128], BF16) for i in range(2)]
            qt = [sbl("qt%d" % i, [128, 512], BF16) for i in range(2)]
            tmp = [sbl("tmp%d" % i, [128, 512], F32) for i in range(4)]
            pt = [sbl("pt%d" % i, [128, 512], BF16) for i in range(4)]
            rinv = [sbl("rinv%d" % i, [128, 512], F32) for i in range(2)]
            ost = [sbl("ost%d" % i, [128, 512], BF16) for i in range(2)]
            b_kt = sc.bufs(2, "kt")
            b_vv = sc.bufs(2, "vv")
            b_qt = sc.bufs(2, "qt")
            b_tmp = sc.bufs(4, "tmp")
            b_pt = sc.bufs(4, "pt")
            b_rinv = sc.bufs(2, "rinv")
            b_ost = sc.bufs(2, "ost")
            cnt = {"hd": 0, "s": 0, "t": 0, "p": 0, "e": 0, "e2": 0, "ip": 0, "rr": 0, "sb": 0}

            def load_head(KT, V, QT, h, g, hb):
                nfb = 8 * g + 8
                sc.op("sp", lambda e: e.dma_start(out=kt[hb][:, 0:nfb * 128], in_=KT[h, :, 0:nfb * 128]),
                      writes=[b_kt[hb]], dma_on=b_kt[hb])
                sc.op("sp", lambda e: e.dma_start(out=kt[hb][:, L:TK], in_=KT[h, :, L:TK]),
                      pwrites=[b_kt[hb]], dma_on=b_kt[hb])
                sc.op("sp", lambda e: e.dma_start(
                    out=vv[hb][:, 0:nfb, :], in_=V[h, 0:nfb * 128, :].rearrange("(b s) d -> s b d", s=128)),
                    writes=[b_vv[hb]], dma_on=b_vv[hb])
                sc.op("sp", lambda e: e.dma_start(out=vv[hb][:16, 32, :], in_=V[h, L:TK, :]),
                      pwrites=[b_vv[hb]], dma_on=b_vv[hb])
                sc.op("sp", lambda e: e.dma_start(out=qt[hb][:], in_=QT[h, :, g * 512:(g + 1) * 512]),
                      writes=[b_qt[hb]], dma_on=b_qt[hb])

            def attn_head(g, hb, ew, OT, h, extra=None):
                nfb = 8 * g + 8
                blocks = [("m", 32)] + [("f", i) for i in range(nfb)]
                N = len(blocks)
                ob = 3 + 2 * (cnt["hd"] % 2)
                cnt["hd"] += 1
                sbanks = []

                def mm1(n):
                    kind, blk = blocks[n]
                    np_ = 16 if kind == "m" else 128
                    c0 = L if kind == "m" else blk * 128
                    sbank = (0, 1, 2, 7)[cnt["s"] % 4]
                    cnt["s"] += 1
                    sbanks.append(sbank)
                    ex = extra(kind, blk) if extra is not None else None
                    if ex is None:
                        sc.op("pe", lambda e: e.matmul(psum[sbank][:np_, :], kt[hb][:, c0:c0 + np_], qt[hb][:],
                                                       start=True, stop=True),
                              reads=[b_kt[hb], b_qt[hb]], writes=[b_ps[sbank]])
                    else:
                        def mm1x(e):
                            e.matmul(psum[sbank][:np_, :], kt[hb][:, c0:c0 + np_], qt[hb][:], start=True, stop=False)
                            return e.matmul(psum[sbank][:np_, :], identb[:], ex[0], start=False, stop=True)
                        sc.op("pe", mm1x, reads=[b_kt[hb], b_qt[hb], b_ident, ex[1]], writes=[b_ps[sbank]])
                LOOK = 3
                for n in range(min(LOOK, N)):
                    mm1(n)
                for n in range(N):
                    if n + LOOK < N:
                        mm1(n + LOOK)
                    kind, blk = blocks[n]
                    np_ = 16 if kind == "m" else 128
                    pi = ew(n, kind, blk, np_, sbanks[n])

                    def mm23(e, np_=np_, blk=blk, pi=pi, n=n):
                        e.matmul(psum[ob][:, :], vv[hb][:np_, blk, :], pt[pi][:np_, :],
                                 start=(n == 0), stop=(n == N - 1))
                        return e.matmul(psum[ob + 1][:, :], onesb[:np_, :], pt[pi][:np_, :],
                                        start=(n == 0), stop=(n == N - 1))
                    sc.op("pe", mm23, reads=[b_vv[hb], b_pt[pi], b_ident],
                          writes=([b_ps[ob], b_ps[ob + 1]] if n == 0 else []),
                          pwrites=([] if n == 0 else [b_ps[ob], b_ps[ob + 1]]))
                ri = cnt["hd"] % 2
                sc.op("dve", lambda e: e.reciprocal(out=rinv[ri][:], in_=psum[ob + 1][:]),
                      reads=[b_ps[ob + 1]], writes=[b_rinv[ri]])
                sc.op("dve", lambda e: e.tensor_tensor(out=ost[ri][:], in0=psum[ob][:], in1=rinv[ri][:], op=ALU.mult),
                      reads=[b_ps[ob], b_rinv[ri]], writes=[b_ost[ri]])
                sc.op("sp", lambda e: e.dma_start(out=OT[h, :, g * 512:(g + 1) * 512], in_=ost[ri][:]),
                      reads=[b_ost[ri]], dma_on=b_ost[ri])

            if do_fox:
                with ExitStack() as pf_:
                    def sbf(name, shape, dt):
                        return pf_.enter_context(nc.sbuf_tensor(name, list(shape), dt))
                    fmb = sbf("fmb", [128, 8, 512], BF16)
                    dtb = [sbf("dtb%d" % i, [128, 512], F32) for i in range(2)]
                    b_fmt = sc.buf("fmb")
                    b_dtb = sc.bufs(2, "dtb")
                    fmt = sbf("fmt", [128, 8, 512], F32)
                    b_fmt32 = sc.buf("fmt32")
                    sc.op("sp", lambda e: e.dma_start(out=fmt[:], in_=fmask.rearrange("r s t -> s r t")),
                          writes=[b_fmt32], dma_on=b_fmt32)
                    sc.op("dve", lambda e: e.tensor_copy(out=fmb[:], in_=fmt[:]), reads=[b_fmt32], writes=[b_fmt])

                    def load_fox(g, h, hb):
                        load_head(KTf, Vf, QTf, h, g, hb)
                        sc.op("sp", lambda e: e.dma_start(
                            out=dtb[hb][:], in_=DTo[h:h + 1, g * 512:(g + 1) * 512].partition_broadcast(128)),
                            writes=[b_dtb[hb]], dma_on=b_dtb[hb])
                    seq = [(g, h) for g in range(4) for h in range(H)]
                    load_fox(seq[0][0], seq[0][1], 0)
                    for qi_, (g, h) in enumerate(seq):
                        hb = qi_ % 2
                        if qi_ + 1 < len(seq):
                            load_fox(seq[qi_ + 1][0], seq[qi_ + 1][1], 1 - hb)

                        def ew(n, kind, blk, np_, sbank, g=g, h=h, hb=hb):
                            ti = cnt["t"] % 4
                            cnt["t"] += 1
                            pi = cnt["p"] % 4
                            cnt["p"] += 1
                            src, bsrc = dtb[hb][:np_, :], b_dtb[hb]
                            sc.op("dve", lambda e: e.scalar_tensor_tensor(
                                out=tmp[ti][:np_, :], in0=psum[sbank][:np_, :], scalar=SCALE, in1=src,
                                op0=ALU.mult, op1=ALU.add), reads=[b_ps[sbank], bsrc], writes=[b_tmp[ti]])
                            sc.op("act", lambda e: e.activation(
                                out=pt[pi][:np_, :], in_=tmp[ti][:np_, :], func=AF.Exp,
                                bias=negD[:np_, blk, h:h + 1], scale=1.0),
                                reads=[b_tmp[ti], b_negD], writes=[b_pt[pi]])
                            return pi
                        def extra(kind, blk, g=g):
                            if kind == "f" and blk >= 8 * g:
                                return (fmb[:, blk - 8 * g, :], b_fmt)
                            return None
                        attn_head(g, hb, ew, OTf, h, extra)
                    sc.barrier()

            if do_dsa:
                with ExitStack() as pd_:
                    def sbd(name, shape, dt):
                        return pd_.enter_context(nc.sbuf_tensor(name, list(shape), dt))
                    ki2 = sbd("ki2", [128, TK], BF16)
                    qi = [sbd("qi%d" % i, [128, 16, 128], BF16) for i in range(2)]
                    wa = [sbd("wa%d" % i, [128, 32], F32) for i in range(2)]
                    ws = [sbd("ws%d" % i, [128, 32], F32) for i in range(2)]
                    sct2 = [sbd("sct%d" % i, [128, TK], F32) for i in range(2)]
                    dg = [sbd("dg%d" % i, [128, NIDX, 128], BF16) for i in range(2)]
                    b_dg = sc.bufs(2, "dg")
                    NIT = 24
                    p2 = sbd("p2", [128, NIT], F32)
                    Wt = sbd("Wt", [128, NIT], F32)
                    lo_t = sbd("lo_t", [128, 1], F32)
                    mid = sbd("mid", [128, 1], F32)
                    cntt = sbd("cntt", [128, 1], F32)
                    gt = sbd("gt", [128, 1], F32)
                    b_p2 = sc.buf("p2")
                    b_W = sc.buf("W")
                    b_lo = sc.buf("lo")
                    b_mid = sc.buf("mid")
                    b_cnt = sc.buf("cnt")
                    b_g = sc.buf("g")

                    def mkp2(e):
                        ins = None
                        for it in range(NIT):
                            ins = e.memset(p2[:, it:it + 1], 2.0 ** -(it + 1))
                        return ins
                    sc.op("dve", mkp2, writes=[b_p2])
                    rr = [sbd("rr%d" % i, [128, 512], BF16) for i in range(4)]
                    m8 = sbd("m8", [128, 8], F32)
                    thr = sbd("thr", [128, 1], F32)
                    m01 = sbd("m01", [128, TK], BF16)
                    maskT = sbd("maskT", [128, 33, 512], BF16)
                    admt = sbd("admt", [128, 256], F32)
                    cf = sbd("cf", [128, H], F32)
                    et = [sbd("et%d" % i, [128, 512], BF16) for i in range(4)]
                    et2 = [sbd("et2%d" % i, [128, 512], BF16) for i in range(2)]
                    bm = [sbd("bm%d" % i, [128, 9, 512], BF16) for i in range(2)]
                    bmm = [sbd("bmm%d" % i, [16, 512], BF16) for i in range(2)]
                    b_ki2 = sc.buf("ki2")
                    b_qi = sc.bufs(2, "qi")
                    b_wa = sc.bufs(2, "wa")
                    b_ws = sc.bufs(2, "ws")
                    b_scA = [sc.bufs(10, "scA"), sc.bufs(10, "scB")]
                    b_scB = sc.bufs(10, "scB")
                    b_wk = sc.buf("wkk")
                    b_rr = sc.bufs(4, "rr")
                    b_m8 = sc.buf("m8")
                    b_thr = sc.buf("thr")
                    b_m01 = sc.buf("m01")
                    b_maskT = sc.buf("maskT")
                    b_adm = sc.buf("adm")
                    b_cf = sc.buf("cf")
                    b_et = sc.bufs(4, "et")
                    b_et2 = sc.bufs(2, "et2")
                    b_bm = sc.bufs(2, "bm")
                    b_bmm = sc.bufs(2, "bmm")
                    sc.op("sp", lambda e: e.dma_start(out=ki2[:], in_=KI2), writes=[b_ki2], dma_on=b_ki2)
                    sc.op("sp", lambda e: e.dma_start(out=admt[:], in_=adm), writes=[b_adm], dma_on=b_adm)
                    sc.op("sp", lambda e: e.dma_start(out=cf[:], in_=cfar), writes=[b_cf], dma_on=b_cf)

                    def load_dsa(g, h, hb):
                        load_head(KTd, Vd, QTd, h, g, hb)
                        sc.op("sp", lambda e: e.dma_start(out=bm[hb][:], in_=BMs[h]), writes=[b_bm[hb]], dma_on=b_bm[hb])
                        if g == 0:
                            sc.op("sp", lambda e: e.dma_start(out=bmm[hb][:], in_=BMm[h]),
                                  writes=[b_bmm[hb]], dma_on=b_bmm[hb])

                    def idx_scores(g, k):
                        i = 4 * g + k
                        nkb = 2 * i + 2
                        ncols = nkb * 128
                        q2 = i % 2
                        SC = sct2[q2]
                        sc.op("sp", lambda e: e.dma_start(
                            out=qi[q2][:], in_=QI[:, :, i * 128:(i + 1) * 128].rearrange("c p t -> p c t")),
                            writes=[b_qi[q2]], dma_on=b_qi[q2])
                        wb = 6 + (i % 2)
                        sc.op("pe", lambda e: e.transpose(out=psum[wb][:, 0:32], in_=WIT[:, i * 128:(i + 1) * 128],
                                                          identity=identf[:32, :32]),
                              reads=[b_WIT, b_ident], writes=[b_ps[wb]])
                        sc.op("act", lambda e: e.activation(out=wa[q2][:], in_=psum[wb][:, 0:32], func=AF.Copy, scale=CW),
                              reads=[b_ps[wb]], writes=[b_wa[q2]])

                        def mkdiag(e):
                            ins = None
                            for hi in range(NIDX):
                                ins = e.tensor_scalar(out=dg[q2][:, hi, :], in0=identb[:], scalar1=wa[q2][:, hi:hi + 1],
                                                      scalar2=None, op0=ALU.mult)
                            return ins
                        sc.op("dve", mkdiag, reads=[b_wa[q2], b_ident], writes=[b_dg[q2]])
                        tiles = []
                        c0 = 0
                        while c0 < ncols:
                            n = min(512, ncols - c0)
                            tiles.append((c0, n, c0))
                            c0 += n
                        tiles.append((L, 16, ncols))
                        for ti_, (c0, n, d0) in enumerate(tiles):
                            sb_ = 4 + (cnt["sb"] % 2)
                            cnt["sb"] += 1
                            pend = []

                            def dots(hi, c0=c0, n=n):
                                ch, half = hi // 2, hi % 2
                                bank = cnt["ip"] % 4
                                cnt["ip"] += 1
                                ri = cnt["rr"] % 4
                                cnt["rr"] += 1
                                sc.op("pe", lambda e: e.matmul(
                                    psum[bank][:, 0:n], qi[q2][64 * half:64 * half + 64, ch, :],
                                    ki2[64 * half:64 * half + 64, c0:c0 + n], start=True, stop=True),
                                    reads=[b_qi[q2], b_ki2], writes=[b_ps[bank]])
                                sc.op("act", lambda e: e.activation(out=rr[ri][:, 0:n], in_=psum[bank][:, 0:n], func=AF.Relu),
                                      reads=[b_ps[bank]], writes=[b_rr[ri]])
                                pend.append(ri)
                            LOOKI = 3
                            for hi in range(min(LOOKI, NIDX)):
                                dots(hi)
                            for hi in range(NIDX):
                                if hi + LOOKI < NIDX:
                                    dots(hi + LOOKI)
                                ri = pend[hi]
                                sc.op("pe", lambda e, hi=hi, ri=ri, n=n, sb_=sb_: e.matmul(
                                    psum[sb_][:, 0:n], dg[q2][:, hi, :], rr[ri][:, 0:n], start=(hi == 0), stop=(hi == NIDX - 1)),
                                    reads=[b_dg[q2], b_rr[ri]], writes=([b_ps[sb_]] if hi == 0 else []),
                                    pwrites=([] if hi == 0 else [b_ps[sb_]]))
                            sc.op("act", lambda e, n=n, d0=d0, sb_=sb_: e.copy(out=SC[:, d0:d0 + n], in_=psum[sb_][:, 0:n]),
                                  reads=[b_ps[sb_]], writes=[b_scA[q2][ti_]])
                        return len(tiles)

                    def idx_select(g, k, ntiles):
                        i = 4 * g + k
                        nkb = 2 * i + 2
                        ncols = nkb * 128
                        ntot = ncols + 16
                        q2 = i % 2
                        SC = sct2[q2]
                        lastt = [b_scA[q2][t] for t in range(ntiles)]
                        allsc = lastt
                        sc.op("dve", lambda e: e.tensor_reduce(out=lo_t[:], in_=SC[:, 0:ntot], axis=mybir.AxisListType.X,
                                                               op=ALU.min), reads=lastt, writes=[b_lo])
                        sc.op("dve", lambda e: e.tensor_tensor(
                            out=SC[:, ncols - 256:ncols], in0=SC[:, ncols - 256:ncols], in1=admt[:], op=ALU.add),
                            reads=[b_adm], writes=lastt)
                        sc.op("dve", lambda e: e.max(out=m8[:], in_=SC[:, 0:ntot]), reads=allsc, writes=[b_m8])
                        sc.op("dve", lambda e: e.tensor_scalar(out=thr[:], in0=m8[:, 0:1], scalar1=lo_t[:, 0:1], scalar2=1.0001,
                                                               op0=ALU.subtract, op1=ALU.mult),
                              reads=[b_m8, b_lo], writes=[b_thr])
                        sc.op("dve", lambda e: e.tensor_scalar(out=Wt[:], in0=p2[:], scalar1=thr[:, 0:1], scalar2=None,
                                                               op0=ALU.mult), reads=[b_thr, b_p2], writes=[b_W])
                        sc.op("dve", lambda e: e.tensor_tensor(out=mid[:], in0=lo_t[:], in1=Wt[:, 0:1], op=ALU.add),
                              reads=[b_lo, b_W], writes=[b_mid])
                        for it in range(NIT):
                            sc.op("dve", lambda e: e.tensor_scalar(
                                out=m01[:, 0:ntot], in0=SC[:, 0:ntot], scalar1=mid[:, 0:1], scalar2=0.0,
                                op0=ALU.is_ge, op1=ALU.add, accum_out=cntt[:, 0:1]),
                                reads=allsc + [b_mid], writes=[b_m01, b_cnt])
                            sc.op("dve", lambda e, it=it: e.tensor_scalar(
                                out=gt[:], in0=cntt[:], scalar1=KTOP - 0.5, scalar2=Wt[:, it:it + 1],
                                op0=ALU.is_ge, op1=ALU.mult), reads=[b_cnt, b_W], writes=[b_g])
                            sc.op("dve", lambda e: e.tensor_tensor(out=lo_t[:], in0=lo_t[:], in1=gt[:], op=ALU.add),
                                  reads=[b_g], writes=[b_lo])
                            if it + 1 < NIT:
                                sc.op("dve", lambda e, it=it: e.tensor_tensor(out=mid[:], in0=lo_t[:], in1=Wt[:, it + 1:it + 2],
                                                                              op=ALU.add),
                                      reads=[b_lo, b_W], writes=[b_mid])
                        sc.op("dve", lambda e: e.tensor_scalar(out=m01[:, 0:ntot], in0=SC[:, 0:ntot],
                                                               scalar1=lo_t[:, 0:1], scalar2=None, op0=ALU.is_ge),
                              reads=[b_lo] + allsc, writes=[b_m01])
                        j0 = 0
                        while j0 < nkb:
                            nb = min(4, nkb - j0)
                            bank = cnt["ip"] % 4
                            cnt["ip"] += 1

                            def trm(e, j0=j0, nb=nb, bank=bank):
                                ins = None
                                for j in range(nb):
                                    ins = e.transpose(out=psb(bank)[:, j * 128:(j + 1) * 128],
                                                      in_=m01[:, (j0 + j) * 128:(j0 + j + 1) * 128], identity=identb[:])
                                return ins
                            sc.op("pe", trm, reads=[b_m01, b_ident], writes=[b_ps[bank]])
                            sc.op("dve", lambda e, j0=j0, nb=nb, bank=bank: e.tensor_copy(
                                out=maskT[:, j0:j0 + nb, k * 128:(k + 1) * 128],
                                in_=psb(bank)[:, 0:nb * 128].rearrange("p (a b) -> p a b", b=128)),
                                reads=[b_ps[bank]], pwrites=[b_maskT])
                            j0 += nb
                        bank = cnt["ip"] % 4
                        cnt["ip"] += 1
                        sc.op("pe", lambda e, bank=bank: e.transpose(out=psb(bank)[:16, 0:128], in_=m01[:, ncols:ncols + 16],
                                                                     identity=identb[:]),
                              reads=[b_m01, b_ident], writes=[b_ps[bank]])
                        sc.op("dve", lambda e, bank=bank: e.tensor_copy(out=maskT[:16, 32, k * 128:(k + 1) * 128],
                                                                        in_=psb(bank)[:16, 0:128]),
                              reads=[b_ps[bank]], pwrites=[b_maskT])

                    for g in range(4):
                        sc.op("dve", lambda e: e.memset(maskT[:], 0.0), writes=[b_maskT])
                        nt_ = idx_scores(g, 0)
                        for k in range(4):
                            nt_next = idx_scores(g, k + 1) if k < 3 else None
                            idx_select(g, k, nt_)
                            nt_ = nt_next
                        load_dsa(g, 0, 0)
                        for h in range(H):
                            hb = h % 2
                            if h + 1 < H:
                                load_dsa(g, h + 1, 1 - hb)

                            def ew(n, kind, blk, np_, sbank, g=g, h=h, hb=hb):
                                ei = cnt["e"] % 4
                                cnt["e"] += 1
                                pi = cnt["p"] % 4
                                cnt["p"] += 1
                                sc.op("act", lambda e: e.activation(
                                    out=et[ei][:np_, :], in_=psum[sbank][:np_, :], func=AF.Exp,
                                    bias=cf[:np_, h:h + 1], scale=SCALE),
                                    reads=[b_ps[sbank], b_cf], writes=[b_et[ei]])
                                srcE, bsrcE = et[ei], b_et[ei]
                                near = None
                                if kind == "f" and blk >= 8 * g - 1:
                                    near = (bm[hb][:np_, blk - (8 * g - 1), :], b_bm[hb])
                                elif kind == "m" and g == 0:
                                    near = (bmm[hb][:np_, :], b_bmm[hb])
                                if near is not None:
                                    e2 = cnt["e2"] % 2
                                    cnt["e2"] += 1
                                    sc.op("dve", lambda e: e.tensor_tensor(
                                        out=et2[e2][:np_, :], in0=et[ei][:np_, :], in1=near[0], op=ALU.mult),
                                        reads=[b_et[ei], near[1]], writes=[b_et2[e2]])
                                    srcE, bsrcE = et2[e2], b_et2[e2]
                                meng = "dve"
                                sc.op(meng, lambda e: e.tensor_tensor(
                                    out=pt[pi][:np_, :], in0=srcE[:np_, :], in1=maskT[:np_, blk, :], op=ALU.mult),
                                    reads=[bsrcE, b_maskT], writes=[b_pt[pi]])
                                return pi
                            attn_head(g, hb, ew, OTd, h)
                    sc.barrier()


    def phase_D():
        with ExitStack() as ps_:
            def sbl(name, shape, dt):
                return ps_.enter_context(nc.sbuf_tensor(name, list(shape), dt))
            arena = sbl("arena", [128, 32768], F32)
            hidv = arena[:].bitcast(BF16).rearrange("p (c t) -> p c t", t=512)
            OTf_v = hidv[:, 0:16, :]
            OTd_v = hidv[:, 16:32, :]
            mixedT = hidv[:, 32:64, :]
            H2T = arena[:, 16384:32768].rearrange("p (c t) -> p c t", t=512)
            orow = arena[:, 0:16384].rearrange("p (b m) -> p b m", m=D)
            U2T = sbl("U2T", [128, 32, 512], BF16)
            wsl = [sbl("wD%d" % i, [128, 32, 128], BF16) for i in range(3)]
            gA = [sbl("gA%d" % i, [128, 512], BF16) for i in range(2)]
            gB = [sbl("gB%d" % i, [128, 512], BF16) for i in range(2)]
            t1 = [sbl("t1%d" % i, [128, 512], F32) for i in range(2)]
            t2 = [sbl("t2%d" % i, [128, 512], F32) for i in range(2)]
            xst = [sbl("xst%d" % i, [128, 512], F32) for i in range(2)]
            h3s = t2
            sq = [sbl("sq%d" % i, [128, 512], BF16) for i in range(2)]
            rstd = sbl("rstd", [128, 512], F32)
            g2t = sbl("g2t", [128, 32], F32)
            gft = sbl("gft", [128, 32], F32)
            b_OT = sc.buf("OTv")
            b_mix = sc.buf("mixedT")
            b_H2T = sc.buf("H2T")
            b_hid = sc.buf("hid")
            b_orow = sc.buf("orow")
            b_U2T = sc.buf("U2T")
            b_w = sc.bufs(3, "wD")
            b_gA = sc.bufs(2, "gA")
            b_gB = sc.bufs(2, "gB")
            b_t1 = sc.bufs(2, "t1")
            b_t2 = sc.bufs(2, "t2")
            b_xst = sc.bufs(2, "xst")
            b_h3s = b_t2
            b_sq = sc.bufs(2, "sq")
            b_rstd = sc.buf("rstd")
            b_g = sc.buf("g2f")
            b_h2d = sc.bufs(32, "h2d")
            sc.op("sp", lambda e: e.dma_start(out=g2t[:], in_=g_mlp), writes=[b_g], dma_on=b_g)
            sc.op("sp", lambda e: e.dma_start(out=gft[:], in_=g_fin), pwrites=[b_g], dma_on=b_g)
            cnt = {"w": 0, "ps": 0, "x": 0, "q": 0, "h": 0}

            def wload(src, kcn):
                wi_ = cnt["w"] % 3
                cnt["w"] += 1
                sc.op("pool", lambda e: e.dma_start(out=wsl[wi_][:, 0:kcn, :], in_=src),
                      writes=[b_w[wi_]], dma_on=b_w[wi_])
                return wi_

            def nbank(nb=6):
                b = cnt["ps"] % nb
                cnt["ps"] += 1
                return b

            def sumsq_rstd(tag):
                sc.op("act", lambda e: e.activation(out=rstd[:], in_=psum[7][:], func=AF.Sqrt,
                                                    bias=epsc[:, 0:1], scale=1.0 / D),
                      reads=[b_ps[7], b_ident], writes=[b_rstd])
                sc.op("dve", lambda e: e.reciprocal(out=rstd[:], in_=rstd[:]), reads=[b_rstd], writes=[b_rstd])

            def row_tile(rt):
                t0 = rt * 512
                sc.op("sp", lambda e: e.dma_start(out=OTf_v, in_=OTf[:, :, t0:t0 + 512].rearrange("h d t -> d h t")),
                      writes=[b_OT], dma_on=b_OT)
                sc.op("sp", lambda e: e.dma_start(out=OTd_v, in_=OTd[:, :, t0:t0 + 512].rearrange("h d t -> d h t")),
                      pwrites=[b_OT], dma_on=b_OT)
                for j in range(32):
                    wa_ = wload(wbf[j], 16)
                    wb_ = wload(wbd[j], 16)
                    gi = j % 2
                    sc.op("sp", lambda e, j=j, gi=gi: e.dma_start(out=gA[gi][:], in_=Gs[j, :, t0:t0 + 512]),
                          writes=[b_gA[gi]], dma_on=b_gA[gi])
                    sc.op("sp", lambda e, j=j, gi=gi: e.dma_start(out=gB[gi][:], in_=Gs[32 + j, :, t0:t0 + 512]),
                          writes=[b_gB[gi]], dma_on=b_gB[gi])
                    ba, bb = nbank(), nbank()

                    def mmA(e, w=wa_, bank=ba, src=OTf_v):
                        ins = None
                        for kc in range(16):
                            ins = e.matmul(psum[bank][:, :], wsl[w][:, kc, :], src[:, kc, :], start=(kc == 0), stop=(kc == 15))
                        return ins

                    def mmB(e, w=wb_, bank=bb, src=OTd_v):
                        ins = None
                        for kc in range(16):
                            ins = e.matmul(psum[bank][:, :], wsl[w][:, kc, :], src[:, kc, :], start=(kc == 0), stop=(kc == 15))
                        return ins
                    sc.op("pe", mmA, reads=[b_w[wa_], b_OT], writes=[b_ps[ba]])
                    sc.op("pe", mmB, reads=[b_w[wb_], b_OT], writes=[b_ps[bb]])
                    sc.op("dve", lambda e, gi=gi, ba=ba: e.tensor_tensor(out=t1[gi][:], in0=psum[ba][:], in1=gA[gi][:], op=ALU.mult),
                          reads=[b_ps[ba], b_gA[gi]], writes=[b_t1[gi]])
                    sc.op("dve", lambda e, gi=gi, bb=bb: e.tensor_tensor(out=t2[gi][:], in0=psum[bb][:], in1=gB[gi][:], op=ALU.mult),
                          reads=[b_ps[bb], b_gB[gi]], writes=[b_t2[gi]])
                    sc.op("dve", lambda e, gi=gi, j=j: e.tensor_tensor(out=mixedT[:, j, :], in0=t1[gi][:], in1=t2[gi][:], op=ALU.add),
                          reads=[b_t1[gi], b_t2[gi]], pwrites=[b_mix])
                for j in range(32):
                    w_ = wload(wo[j], 32)
                    bank = nbank()
                    xi = cnt["x"] % 2
                    cnt["x"] += 1
                    qi_ = cnt["q"] % 2
                    cnt["q"] += 1
                    sc.op("sp", lambda e, j=j, xi=xi: e.dma_start(out=xst[xi][:], in_=XTo[j, :, t0:t0 + 512]),
                          writes=[b_xst[xi]], dma_on=b_xst[xi])

                    def mmO(e, w=w_, bank=bank):
                        ins = None
                        for kc in range(32):
                            ins = e.matmul(psum[bank][:, :], wsl[w][:, kc, :], mixedT[:, kc, :], start=(kc == 0), stop=(kc == 31))
                        return ins
                    sc.op("pe", mmO, reads=[b_w[w_], b_mix], writes=[b_ps[bank]])
                    sc.op("dve", lambda e, j=j, xi=xi, bank=bank: e.tensor_tensor(
                        out=H2T[:, j, :], in0=psum[bank][:], in1=xst[xi][:], op=ALU.add),
                        reads=[b_ps[bank], b_xst[xi]], pwrites=[b_H2T])
                    sc.op("act", lambda e, j=j, qi_=qi_: e.activation(out=sq[qi_][:], in_=H2T[:, j, :], func=AF.Square),
                          reads=[b_H2T], writes=[b_sq[qi_]])
                    sc.op("pe", lambda e, j=j, qi_=qi_: e.matmul(psum[7][:, :], onesb[:], sq[qi_][:], start=(j == 0), stop=(j == 31)),
                          reads=[b_sq[qi_], b_ident], writes=([b_ps[7]] if j == 0 else []), pwrites=([] if j == 0 else [b_ps[7]]))
                    sc.op("sp", lambda e, j=j: e.dma_start(out=H2s[j, :, t0:t0 + 512], in_=H2T[:, j, :]),
                          reads=[b_H2T], writes=[b_h2d[j]], dma_on=b_H2T)
                sumsq_rstd("a")
                for j in range(32):
                    sc.op("dve", lambda e, j=j: e.scalar_tensor_tensor(
                        out=U2T[:, j, :], in0=H2T[:, j, :], scalar=g2t[:, j:j + 1], in1=rstd[:], op0=ALU.mult, op1=ALU.mult),
                        reads=[b_H2T, b_g, b_rstd], pwrites=[b_U2T])
                sc.barrier()
                for f in range(128):
                    w_ = wload(wu[f], 32)
                    bank = nbank()
                    gi = f % 2

                    def mmU(e, w=w_, bank=bank):
                        ins = None
                        for kc in range(32):
                            ins = e.matmul(psum[bank][:, :], wsl[w][:, kc, :], U2T[:, kc, :], start=(kc == 0), stop=(kc == 31))
                        return ins
                    sc.op("pe", mmU, reads=[b_w[w_], b_U2T], writes=[b_ps[bank]])
                    sc.op("act", lambda e, gi=gi, bank=bank: e.activation(out=t1[gi][:], in_=psum[bank][:], func=AF.Relu),
                          reads=[b_ps[bank]], writes=[b_t1[gi]])
                    sc.op("dve", lambda e, gi=gi, f=f: e.tensor_tensor(out=hidv[:, f, :], in0=t1[gi][:], in1=t1[gi][:], op=ALU.mult),
                          reads=[b_t1[gi]], pwrites=[b_hid])
                for j in range(32):
                    bank = nbank(4)
                    xi = cnt["x"] % 2
                    cnt["x"] += 1
                    qi_ = cnt["q"] % 2
                    cnt["q"] += 1
                    hi_ = cnt["h"] % 2
                    cnt["h"] += 1
                    sc.op("sp", lambda e, j=j, xi=xi: e.dma_start(out=xst[xi][:], in_=H2s[j, :, t0:t0 + 512]),
                          reads=[b_h2d[j]], writes=[b_xst[xi]], dma_on=b_xst[xi])
                    for q in range(4):
                        w_ = wload(wd[j, q], 32)

                        def mmD(e, w=w_, bank=bank, q=q):
                            ins = None
                            for kc in range(32):
                                ins = e.matmul(psum[bank][:, :], wsl[w][:, kc, :], hidv[:, q * 32 + kc, :],
                                               start=(q == 0 and kc == 0), stop=(q == 3 and kc == 31))
                            return ins
                        sc.op("pe", mmD, reads=[b_w[w_], b_hid], writes=([b_ps[bank]] if q == 0 else []),
                              pwrites=([] if q == 0 else [b_ps[bank]]))
                    sc.op("dve", lambda e, xi=xi, bank=bank, hi_=hi_: e.tensor_tensor(
                        out=h3s[hi_][:], in0=psum[bank][:], in1=xst[xi][:], op=ALU.add),
                        reads=[b_ps[bank], b_xst[xi]], writes=[b_h3s[hi_]])
                    sc.op("act", lambda e, hi_=hi_, qi_=qi_: e.activation(out=sq[qi_][:], in_=h3s[hi_][:], func=AF.Square),
                          reads=[b_h3s[hi_]], writes=[b_sq[qi_]])
                    sc.op("pe", lambda e, j=j, qi_=qi_: e.matmul(psum[7][:, :], onesb[:], sq[qi_][:], start=(j == 0), stop=(j == 31)),
                          reads=[b_sq[qi_], b_ident], writes=([b_ps[7]] if j == 0 else []), pwrites=([] if j == 0 else [b_ps[7]]))
                    sc.op("sp", lambda e, j=j, hi_=hi_: e.dma_start(out=H2s[j, :, t0:t0 + 512], in_=h3s[hi_][:]),
                          reads=[b_h3s[hi_]], writes=[b_h2d[j]], dma_on=b_h3s[hi_])
                sumsq_rstd("b")
                sc.barrier()
                for j in range(32):
                    bank = nbank(4)
                    xi = cnt["x"] % 2
                    cnt["x"] += 1
                    gi = j % 2
                    sc.op("sp", lambda e, j=j, xi=xi: e.dma_start(out=xst[xi][:], in_=H2s[j, :, t0:t0 + 512]),
                          reads=[b_h2d[j]], writes=[b_xst[xi]], dma_on=b_xst[xi])
                    sc.op("dve", lambda e, j=j, xi=xi, gi=gi: e.scalar_tensor_tensor(
                        out=t1[gi][:], in0=xst[xi][:], scalar=gft[:, j:j + 1], in1=rstd[:], op0=ALU.mult, op1=ALU.mult),
                        reads=[b_xst[xi], b_g, b_rstd], writes=[b_t1[gi]])

                    def trO(e, gi=gi, bank=bank):
                        ins = None
                        for tb in range(4):
                            ins = e.transpose(out=psum[bank][:, tb * 128:(tb + 1) * 128],
                                              in_=t1[gi][:, tb * 128:(tb + 1) * 128], identity=identf[:])
                        return ins
                    sc.op("pe", trO, reads=[b_t1[gi], b_ident], writes=[b_ps[bank]])
                    ceng = "act" if j % 2 == 0 else "dve"
                    if ceng == "act":
                        sc.op("act", lambda e, j=j, bank=bank: e.copy(
                            out=orow[:, :, j * 128:(j + 1) * 128],
                            in_=psum[bank][:].rearrange("p (a b) -> p a b", b=128)),
                            reads=[b_ps[bank]], pwrites=[b_orow])
                    else:
                        sc.op("dve", lambda e, j=j, bank=bank: e.tensor_copy(
                            out=orow[:, :, j * 128:(j + 1) * 128],
                            in_=psum[bank][:].rearrange("p (a b) -> p a b", b=128)),
                            reads=[b_ps[bank]], pwrites=[b_orow])
                for tb in range(4):
                    sc.op("sp", lambda e, tb=tb: e.dma_start(out=out[t0 + tb * 128:t0 + (tb + 1) * 128, :], in_=orow[:, tb, :]),
                          reads=[b_orow], dma_on=b_orow)
                sc.barrier()
            for rt in range(4):
                row_tile(rt)

    es_mid = ExitStack()

    def sbm(name, shape, dt):
        return es_mid.enter_context(nc.sbuf_tensor(name, list(shape), dt))
    WIT = sbm("WIT", [32, NOWN], F32)
    negD = sbm("negD", [128, 33, 16], F32)
    b_DT = sc.buf("DT")
    es_lf = ExitStack()
    LF = es_lf.enter_context(nc.sbuf_tensor("LF", [16, TK], F32))
    DT = es_lf.enter_context(nc.sbuf_tensor("DT", [16, TK], F32))
    phase_A()
    if stop_after != "A":
        phase_B()
    if stop_after not in ("A", "B"):
        phase_B2()
        es_lf.close()
        phase_C0()
        phase_C(do_fox=True, do_dsa=(stop_after != "C1"))
    else:
        es_lf.close()
    es_mid.close()
    if stop_after == "all":
        phase_D()

    sc.barrier()
    sc.run()
    es.close()
    return nc


def _tile_w(W):
    K, N = W.shape
    return np.ascontiguousarray(W.reshape(K // 128, 128, N // 128, 128).transpose(2, 1, 0, 3))


def _t5_bucket_table():
    rel = np.arange(-4300, 4301, dtype=np.int64)
    half, max_exact = 16, 8
    ret = np.where(rel > 0, half, 0)
    n = np.abs(rel)
    nf = np.maximum(n, 1).astype(np.float32)
    large = max_exact + (np.log(nf / np.float32(max_exact)) / np.float32(math.log(128 / max_exact))
                         * np.float32(half - max_exact)).astype(np.int32)
    large = np.minimum(large, half - 1)
    return (ret + np.where(n < max_exact, n, large)).astype(np.int64)


def _host_prepare(inp):
    f32 = np.float32
    x = np.asarray(inp["x"], f32)
    w_in = np.asarray(inp["w_in"], f32)[0]
    maps = []
    z = np.zeros
    ki = w_in[:, O_KI:O_KI + 64]
    fa_chunk = np.concatenate([w_in[:, O_FA:O_FA + 16], z((D, 112), f32)], axis=1)
    wi_chunk = np.concatenate([w_in[:, O_WI:O_WI + 32], z((D, 96), f32)], axis=1)
    wk_cols = np.concatenate([w_in[:, O_KA:O_KA + 2048], w_in[:, O_KB:O_KB + 2048],
                              w_in[:, O_VA:O_VA + 2048], w_in[:, O_VB:O_VB + 2048],
                              ki, ki, fa_chunk], axis=1)
    wq_cols = np.concatenate([w_in[:, O_QA:O_QA + 2048], w_in[:, O_QB:O_QB + 2048],
                              w_in[:, O_QI:O_QI + 2048], wi_chunk,
                              w_in[:, O_GL:O_GL + 8192]], axis=1)
    wk = _tile_w(wk_cols)
    wq = _tile_w(wq_cols)
    wbf = _tile_w(np.asarray(inp["w_branch_fox"], f32)[0])
    wbd = _tile_w(np.asarray(inp["w_branch_dsa"], f32)[0])
    wo = _tile_w(np.asarray(inp["w_out"], f32)[0])
    wu = _tile_w(np.asarray(inp["w_up"], f32)[0])
    wdn = _tile_w(np.asarray(inp["w_down"], f32)[0])
    wd = np.ascontiguousarray(wdn.reshape(32, 128, 4, 32, 128).transpose(0, 2, 1, 3, 4))
    g_attn = np.asarray(inp["attn_norm_g"], f32).reshape(1, D)
    g_mlp = np.ascontiguousarray(np.asarray(inp["mlp_norm_g"], f32).reshape(32, 128).T)
    g_fin = np.ascontiguousarray(np.asarray(inp["final_norm_g"], f32).reshape(32, 128).T)
    negfb = np.ascontiguousarray(-np.asarray(inp["forget_bias"], f32).reshape(16, 1))
    rel_bias = np.asarray(inp["rel_bias"], f32)
    meta = np.asarray(inp["meta_tokens"], f32)
    bt = _t5_bucket_table()

    def bkt(rel):
        return bt[rel + 4300]
    cfar = np.ascontiguousarray(np.broadcast_to(rel_bias[15][None, :], (128, H))).astype(f32)
    identf = np.eye(128, dtype=f32)
    si = np.arange(128)
    per_par = {}
    for p in (0, 1):
        u = np.zeros((H, 128, 3, 128), f32)
        for jj, j in enumerate((-1, 0, 1)):
            dblk = j - p
            rel = dblk * 128 + si[:, None] - si[None, :]
            u[:, :, jj, :] = rel_bias[bkt(rel)].transpose(2, 0, 1)
        relm = si[:16, None] - (16 + 128 * p + si[None, :])
        um = np.ascontiguousarray(rel_bias[bkt(relm)].transpose(2, 0, 1)).astype(f32)
        fm = np.zeros((8, 128, 512), f32)
        for r in range(8):
            for k in range(4):
                d = r - (2 * k + p)
                blk = fm[r, :, k * 128:(k + 1) * 128]
                if d > 0:
                    blk[:] = NEG
                elif d == 0:
                    blk[:] = np.where(si[:, None] <= si[None, :], 0.0, NEG)
        ad = np.zeros((128, 256), f32)
        cm = np.where((si[None, :] // 64) <= (si[:, None] // 64), 0.0, NEG)
        if p == 0:
            ad[:, 0:128] = cm
            ad[:, 128:256] = NEG
        else:
            ad[:, 128:256] = cm
        selv = np.zeros((16, 2), f32)
        selv[:, p] = 1.0
        per_par[p] = dict(ubias=u, umeta=um, fmask=fm, adm=ad, sel=selv)
    for c in range(N_CORES):
        b, p = c // 2, c % 2
        xb = x[b]
        x_own = np.ascontiguousarray(xb.reshape(32, 128, D)[p::2].reshape(NOWN, D))
        m = dict(x_all=xb, x_own=x_own, meta=meta, g_attn=g_attn, g_mlp=g_mlp, g_fin=g_fin,
                 wk=wk, wq=wq, wbf=wbf, wbd=wbd, wo=wo, wu=wu, wd=wd, negfb=negfb,
                 identf=identf, cfar=cfar, ncfar=-cfar)
        m.update(per_par[p])
        maps.append(m)
    return maps


def _assemble(results):
    outp = np.empty((4, L, D), np.float32)
    for c in range(N_CORES):
        b, p = c // 2, c % 2
        o = np.asarray(results[c]["out"], np.float32).reshape(16, 128, D)
        outp[b].reshape(32, 128, D)[p::2] = o
    return outp


def kernel(**inputs):
    maps = _host_prepare(inputs)
    nc = build_program()
    res = run_bass_kernel_spmd(nc, maps, core_ids=list(range(N_CORES)))
    return _assemble(res.results)
```
